# Optimizing a Trainium2 kernel written in Bass

```python
import jax
import jax.numpy as jnp
from jax import lax
import numpy as np

D_MODEL = 1024
BATCH = 4
SEQ = 4096
DEPTH = 4
DEC_BATCH = 128
DEC_SEQ = 4
PAST_LEN = 2048
PAGE_SIZE = 128

HEAD_DIM = 64
N_HEADS = D_MODEL // 128
N_KV_HEADS = max(2, N_HEADS // 4)
Q_PER_KV = N_HEADS // N_KV_HEADS
ATT_WIDTH = N_HEADS * HEAD_DIM
KV_WIDTH = N_KV_HEADS * HEAD_DIM
N_KV_STREAMS = 6
N_CACHED_STREAMS = 4
CMP_LEN = 32
CMP_STRIDE = 16
CMP_HIDDEN = 2 * HEAD_DIM
SEL_BLOCK = 64
SEL_TOPK = 16
N_LOCAL_BLOCKS = 2
WINDOW = 512
Q_BLOCK = 128
FORCED_SCORE = 1e4
POOL_WINDOWS = (2, 4, 8, 16)
N_POOL_GROUPS = 4
POOL_WIDTH = D_MODEL // 4
POOL_GROUP = POOL_WIDTH // N_POOL_GROUPS
POOL_MAX = 16
CONV_WIDTH = 3
CONV_DIM = D_MODEL // 4
D_FF = ((8 * D_MODEL // 3 + 127) // 128) * 128
ROPE_THETA = 10000.0
EPS = 1e-6
IN_SPLITS = (POOL_WIDTH, ATT_WIDTH, N_KV_STREAMS * KV_WIDTH, 3 * N_HEADS, CONV_DIM, CONV_DIM, CONV_DIM, D_MODEL, D_MODEL, D_MODEL)
IN_WIDTH = sum(IN_SPLITS)

kernel_name = 'nsa_pool_conv_hybrid_step'


def rmsnorm(x, g):
    xf = x.astype(jnp.float32)
    y = xf * lax.rsqrt(jnp.mean(xf * xf, axis=-1, keepdims=True) + EPS)
    return (y * g.astype(jnp.float32)).astype(x.dtype)


def rope(x, pos):
    half = HEAD_DIM // 2
    inv = ROPE_THETA ** (-jnp.arange(half, dtype=jnp.float32) / half)
    ang = pos.astype(jnp.float32)[:, None] * inv[None, :]
    cos = jnp.cos(ang)[:, None, :]
    sin = jnp.sin(ang)[:, None, :]
    xf = x.astype(jnp.float32)
    x1, x2 = xf[..., :half], xf[..., half:]
    return jnp.concatenate([x1 * cos - x2 * sin, x2 * cos + x1 * sin], axis=-1).astype(x.dtype)


def split_columns(z):
    parts, off = [], 0
    for w in IN_SPLITS:
        parts.append(z[..., off:off + w])
        off += w
    return parts


def query_blocks(S):
    c = Q_BLOCK if S % Q_BLOCK == 0 else S
    return c, S // c


def causal_dwconv(u, prefix, w):
    S = u.shape[1]
    ext = jnp.concatenate([prefix, u], axis=1)
    out = w[0] * ext[:, 0:S]
    for k in range(1, CONV_WIDTH):
        out = out + w[k] * ext[:, k:k + S]
    return out, ext[:, ext.shape[1] - (CONV_WIDTH - 1):]


def pool_mixer(u, prefix, pos0, pool_w, pool_scale):
    B, S, C = u.shape
    P = prefix.shape[1]
    ext = jnp.concatenate([prefix, u], axis=1).astype(jnp.float32)
    cs = jnp.concatenate([jnp.zeros((B, 1, C), jnp.float32), jnp.cumsum(ext, axis=1)], axis=1)
    end = cs[:, P + 1:]
    t_abs = (pos0 + jnp.arange(S)).astype(jnp.float32)
    means = []
    for g, w in enumerate(POOL_WINDOWS):
        cols = slice(g * POOL_GROUP, (g + 1) * POOL_GROUP)
        start = cs[:, P + 1 - w:P + 1 - w + S, cols]
        count = jnp.minimum(float(w), t_abs + 1.0)[:, None]
        means.append((end[..., cols] - start) / count)
    pooled = jnp.concatenate(means, axis=-1) - ext[:, P:]
    y = jnp.einsum('bsgc,gcd->bsgd', pooled.reshape(B, S, N_POOL_GROUPS, POOL_GROUP), pool_w.astype(jnp.float32))
    y = y.reshape(B, S, C) * pool_scale.astype(jnp.float32)
    return y.astype(u.dtype), ext[:, ext.shape[1] - P:].astype(u.dtype)


def compress_blocks(rows, pe, w1, w2):
    B, T = rows.shape[0], rows.shape[1]
    n_cmp = (T - CMP_LEN) // CMP_STRIDE + 1
    idx = jnp.arange(n_cmp)[:, None] * CMP_STRIDE + jnp.arange(CMP_LEN)[None, :]
    blk = rows[:, idx] + pe[:, None, :]
    blk = jnp.moveaxis(blk, 3, 2).reshape(B, n_cmp, N_KV_HEADS, CMP_LEN * HEAD_DIM)
    return jax.nn.gelu(blk @ w1) @ w2


def nsa_cmp_sel(q, pos_q, kc, vc, pos_c, ks_blocks, vs_blocks):
    B, c = q.shape[0], q.shape[1]
    n_cmp = kc.shape[1]
    n_sel = ks_blocks.shape[2]
    scale = HEAD_DIM ** -0.5
    qg = q.reshape(B, c, N_KV_HEADS, Q_PER_KV, HEAD_DIM)
    s = jnp.einsum('bcgrd,bngd->bgrcn', qg, kc).astype(jnp.float32) * scale
    ok_c = pos_c[None, :] <= pos_q[:, None]
    s = jnp.where(ok_c, s, -jnp.inf)
    m = jnp.max(s, axis=-1, keepdims=True)
    m = jnp.where(jnp.isfinite(m), m, 0.0)
    e = jnp.exp(s - m)
    p_cmp = e / jnp.maximum(jnp.sum(e, axis=-1, keepdims=True), 1.0)
    o_cmp = jnp.einsum('bgrcn,bngd->bcgrd', p_cmp.astype(vc.dtype), vc)
    c_start = jnp.arange(n_cmp) * CMP_STRIDE
    j_start = jnp.arange(n_sel) * SEL_BLOCK
    cover = ((c_start[:, None] < j_start[None, :] + SEL_BLOCK) & (c_start[:, None] + CMP_LEN > j_start[None, :])).astype(jnp.float32)
    imp = jnp.einsum('bgrcn,nj->bgcj', p_cmp, cover)
    cur = pos_q // SEL_BLOCK
    jj = jnp.arange(n_sel)[None, :]
    causal = j_start[None, :] <= pos_q[:, None]
    forced = (jj == 0) | ((jj <= cur[:, None]) & (jj > cur[:, None] - N_LOCAL_BLOCKS))
    score = jnp.where(forced, FORCED_SCORE, jnp.where(causal, imp, -1.0))
    vals, idx = lax.top_k(score, min(SEL_TOPK, n_sel))
    blk_ok = vals >= 0.0
    pick = jax.vmap(jax.vmap(lambda blocks, ix: blocks[ix]))
    kg = pick(ks_blocks, idx)
    vg = pick(vs_blocks, idx)
    s2 = jnp.einsum('bcgrd,bgcksd->bgrcks', qg, kg).astype(jnp.float32) * scale
    kpos = idx[..., None] * SEL_BLOCK + jnp.arange(SEL_BLOCK)
    ok_s = blk_ok[..., None] & (kpos <= pos_q[None, None, :, None, None])
    s2 = jnp.where(ok_s[:, :, None], s2, -jnp.inf)
    n_keys = idx.shape[-1] * SEL_BLOCK
    p_sel = jax.nn.softmax(s2.reshape(B, N_KV_HEADS, Q_PER_KV, c, n_keys), axis=-1)
    p_sel = p_sel.reshape(s2.shape).astype(vg.dtype)
    o_sel = jnp.einsum('bgrcks,bgcksd->bcgrd', p_sel, vg)
    return (o_cmp.reshape(B, c, N_HEADS, HEAD_DIM), o_sel.reshape(B, c, N_HEADS, HEAD_DIM))


def nsa_window(q, pos0, k_ext, v_ext):
    B, S = q.shape[0], q.shape[1]
    P = k_ext.shape[1] - S
    c, nb = query_blocks(S)
    span = P + c
    idx = jnp.arange(nb)[:, None] * c + jnp.arange(span)[None, :]
    kb = k_ext[:, idx]
    vb = v_ext[:, idx]
    qb = q.reshape(B, nb, c, N_KV_HEADS, Q_PER_KV, HEAD_DIM)
    kp = (pos0 - P + idx)[:, None, :]
    qp = (pos0 + jnp.arange(S)).reshape(nb, c)[:, :, None]
    ok = (kp >= 0) & (kp <= qp) & (kp >= qp - WINDOW)
    s = jnp.einsum('bncgrd,bnsgd->bngrcs', qb, kb).astype(jnp.float32) * (HEAD_DIM ** -0.5)
    s = jnp.where(ok[None, :, None, None], s, -jnp.inf)
    p = jax.nn.softmax(s, axis=-1).astype(vb.dtype)
    o = jnp.einsum('bngrcs,bnsgd->bncgrd', p, vb)
    return o.reshape(B, S, N_HEADS, HEAD_DIM)


def nsa_mixer(q_raw, kv_raw, gate_raw, kv_past, win_past, pos0, cmp_pe, cmp_w1, cmp_w2):
    B, S = q_raw.shape[0], q_raw.shape[1]
    pos_q = pos0 + jnp.arange(S)
    q = rope(q_raw, pos_q)
    new_rows = jnp.stack([kv_raw[:, :, 0], kv_raw[:, :, 1], rope(kv_raw[:, :, 2], pos_q), kv_raw[:, :, 3]], axis=2)
    rows = jnp.concatenate([kv_past, new_rows], axis=1)
    T = rows.shape[1]
    kc = compress_blocks(rows[:, :, 0], cmp_pe[0], cmp_w1[0], cmp_w2[0])
    vc = compress_blocks(rows[:, :, 1], cmp_pe[1], cmp_w1[1], cmp_w2[1])
    pos_c = jnp.arange(kc.shape[1]) * CMP_STRIDE + (CMP_LEN - 1)
    kc = rope(kc, pos_c)
    n_sel = -(-T // SEL_BLOCK)
    pad = ((0, 0), (0, n_sel * SEL_BLOCK - T), (0, 0), (0, 0))

    def to_blocks(r):
        r = jnp.pad(r, pad).reshape(B, n_sel, SEL_BLOCK, N_KV_HEADS, HEAD_DIM)
        return jnp.transpose(r, (0, 3, 1, 2, 4))

    ks_blocks = to_blocks(rows[:, :, 2])
    vs_blocks = to_blocks(rows[:, :, 3])
    c, nb = query_blocks(S)
    q_blk = jnp.moveaxis(q.reshape(B, nb, c, N_HEADS, HEAD_DIM), 1, 0)
    pos_blk = pos_q.reshape(nb, c)
    o_cmp, o_sel = lax.map(lambda a: nsa_cmp_sel(a[0], a[1], kc, vc, pos_c, ks_blocks, vs_blocks), (q_blk, pos_blk))
    o_cmp = jnp.moveaxis(o_cmp, 0, 1).reshape(B, S, N_HEADS, HEAD_DIM)
    o_sel = jnp.moveaxis(o_sel, 0, 1).reshape(B, S, N_HEADS, HEAD_DIM)
    win_rows = jnp.stack([rope(kv_raw[:, :, 4], pos_q), kv_raw[:, :, 5]], axis=2)
    win_ext = jnp.concatenate([win_past, win_rows], axis=1)
    o_win = nsa_window(q, pos0, win_ext[:, :, 0], win_ext[:, :, 1])
    g = jax.nn.sigmoid(gate_raw.astype(jnp.float32)).reshape(B, S, 3, N_HEADS, 1).astype(q.dtype)
    o = g[:, :, 0] * o_cmp + g[:, :, 1] * o_sel + g[:, :, 2] * o_win
    win_new = win_ext[:, win_ext.shape[1] - min(WINDOW, pos0 + S):]
    return o.reshape(B, S, ATT_WIDTH), new_rows, win_new


def short_conv_mixer(b_gate, c_gate, x_in, prefix, conv_w):
    y, new_prefix = causal_dwconv(c_gate * x_in, prefix, conv_w)
    return b_gate * y, new_prefix


def conv_ffn(h, prefix, ffn_up, ffn_conv, ffn_down):
    up = h @ ffn_up
    a, b = up[..., :D_FF], up[..., D_FF:]
    a_conv, new_prefix = causal_dwconv(a, prefix, ffn_conv)
    return (jax.nn.silu(a_conv) * b) @ ffn_down, new_prefix


def layer(x, pos0, kv_past, win_past, pool_past, conv_past, ffn_past,
          g_mix, w_in, pool_w, pool_scale, cmp_pe, cmp_w1, cmp_w2, conv_w,
          w_br_pool, w_br_nsa, w_br_conv, w_out, g_ffn, ffn_up, ffn_conv, ffn_down):
    B, S = x.shape[0], x.shape[1]
    h = rmsnorm(x, g_mix)
    z = h @ w_in
    u_pool, q_raw, kv_raw, nsa_gate, c_b, c_c, c_x, gate_pool, gate_nsa, gate_conv = split_columns(z)
    y_pool, pool_new = pool_mixer(u_pool, pool_past, pos0, pool_w, pool_scale)
    y_nsa, kv_rows, win_new = nsa_mixer(q_raw.reshape(B, S, N_HEADS, HEAD_DIM),
                                        kv_raw.reshape(B, S, N_KV_STREAMS, N_KV_HEADS, HEAD_DIM),
                                        nsa_gate, kv_past, win_past, pos0, cmp_pe, cmp_w1, cmp_w2)
    y_conv, conv_new = short_conv_mixer(c_b, c_c, c_x, conv_past, conv_w)
    merged = (jax.nn.sigmoid(gate_pool) * (y_pool @ w_br_pool)
              + jax.nn.sigmoid(gate_nsa) * (y_nsa @ w_br_nsa)
              + jax.nn.sigmoid(gate_conv) * (y_conv @ w_br_conv))
    x = x + merged @ w_out
    f, ffn_new = conv_ffn(rmsnorm(x, g_ffn), ffn_past, ffn_up, ffn_conv, ffn_down)
    return x + f, kv_rows, win_new, pool_new, conv_new, ffn_new


def setup_inputs(seed: int = 0) -> dict:
    key = jax.random.key(seed)
    ks = jax.random.split(key, 28)
    n_pages = PAST_LEN // PAGE_SIZE
    n_used = DEC_BATCH * n_pages
    n_pool = n_used + n_used // 4
    w_s = min(WINDOW, PAST_LEN)

    def nrm(k, shape, s):
        return jax.random.normal(k, shape, jnp.float32) * s

    page_table = jax.random.permutation(ks[8], n_pool)[:n_used].reshape(DEC_BATCH, n_pages).astype(jnp.int32)
    return {
        'x_prompt': nrm(ks[0], (BATCH, SEQ, D_MODEL), 1.0),
        'x_sample': nrm(ks[1], (DEC_BATCH, DEC_SEQ, D_MODEL), 1.0),
        'cache_kv': nrm(ks[2], (DEPTH, n_pool, PAGE_SIZE, N_CACHED_STREAMS, N_KV_HEADS, HEAD_DIM), 1.0),
        'cache_win': nrm(ks[3], (DEPTH, DEC_BATCH, w_s, 2, N_KV_HEADS, HEAD_DIM), 1.0),
        'state_pool': nrm(ks[4], (DEPTH, DEC_BATCH, POOL_MAX - 1, POOL_WIDTH), 1.0),
        'state_conv': nrm(ks[5], (DEPTH, DEC_BATCH, CONV_WIDTH - 1, CONV_DIM), 1.0),
        'state_ffn': nrm(ks[6], (DEPTH, DEC_BATCH, CONV_WIDTH - 1, D_FF), 1.0),
        'page_table': page_table,
        'norm_mix': 1.0 + nrm(ks[9], (DEPTH, D_MODEL), 0.02),
        'w_in': nrm(ks[10], (DEPTH, D_MODEL, IN_WIDTH), D_MODEL ** -0.5),
        'pool_w': nrm(ks[11], (DEPTH, N_POOL_GROUPS, POOL_GROUP, POOL_GROUP), POOL_GROUP ** -0.5),
        'pool_scale': 1.0 + nrm(ks[12], (DEPTH, POOL_WIDTH), 0.02),
        'cmp_pe': nrm(ks[13], (DEPTH, 2, CMP_LEN, HEAD_DIM), 0.02),
        'cmp_w1': nrm(ks[14], (DEPTH, 2, CMP_LEN * HEAD_DIM, CMP_HIDDEN), (CMP_LEN * HEAD_DIM) ** -0.5),
        'cmp_w2': nrm(ks[15], (DEPTH, 2, CMP_HIDDEN, HEAD_DIM), CMP_HIDDEN ** -0.5),
        'conv_w': nrm(ks[16], (DEPTH, CONV_WIDTH, CONV_DIM), CONV_WIDTH ** -0.5),
        'w_br_pool': nrm(ks[17], (DEPTH, POOL_WIDTH, D_MODEL), POOL_WIDTH ** -0.5),
        'w_br_nsa': nrm(ks[18], (DEPTH, ATT_WIDTH, D_MODEL), ATT_WIDTH ** -0.5),
        'w_br_conv': nrm(ks[19], (DEPTH, CONV_DIM, D_MODEL), CONV_DIM ** -0.5),
        'w_out': nrm(ks[20], (DEPTH, D_MODEL, D_MODEL), D_MODEL ** -0.5),
        'norm_ffn': 1.0 + nrm(ks[21], (DEPTH, D_MODEL), 0.02),
        'ffn_up': nrm(ks[22], (DEPTH, D_MODEL, 2 * D_FF), D_MODEL ** -0.5),
        'ffn_conv': nrm(ks[23], (DEPTH, CONV_WIDTH, D_FF), CONV_WIDTH ** -0.5),
        'ffn_down': nrm(ks[24], (DEPTH, D_FF, D_MODEL), D_FF ** -0.5),
        'norm_final': 1.0 + nrm(ks[25], (D_MODEL,), 0.02),
    }


def reference(x_prompt, x_sample, cache_kv, cache_win, state_pool, state_conv, state_ffn, page_table,
              norm_mix, w_in, pool_w, pool_scale, cmp_pe, cmp_w1, cmp_w2, conv_w,
              w_br_pool, w_br_nsa, w_br_conv, w_out, norm_ffn, ffn_up, ffn_conv, ffn_down, norm_final):
    dt = x_prompt.dtype
    bp = x_prompt.shape[0]
    bs = x_sample.shape[0]
    n_pages = page_table.shape[1]
    kv0 = jnp.zeros((bp, 0, N_CACHED_STREAMS, N_KV_HEADS, HEAD_DIM), dt)
    win0 = jnp.zeros((bp, WINDOW, 2, N_KV_HEADS, HEAD_DIM), dt)
    pool0 = jnp.zeros((bp, POOL_MAX - 1, POOL_WIDTH), dt)
    conv0 = jnp.zeros((bp, CONV_WIDTH - 1, CONV_DIM), dt)
    ffn0 = jnp.zeros((bp, CONV_WIDTH - 1, D_FF), dt)
    kv_p, win_p, pool_p, conv_p, ffn_p = [], [], [], [], []
    kv_s, win_s, pool_s, conv_s, ffn_s = [], [], [], [], []
    xp, xs = x_prompt, x_sample
    for l in range(DEPTH):
        weights = (norm_mix[l], w_in[l], pool_w[l], pool_scale[l], cmp_pe[l], cmp_w1[l], cmp_w2[l], conv_w[l],
                   w_br_pool[l], w_br_nsa[l], w_br_conv[l], w_out[l], norm_ffn[l], ffn_up[l], ffn_conv[l], ffn_down[l])
        xp, a, b, c, d, e = layer(xp, 0, kv0, win0, pool0, conv0, ffn0, *weights)
        kv_p.append(a); win_p.append(b); pool_p.append(c); conv_p.append(d); ffn_p.append(e)
        kv_past = cache_kv[l][page_table].reshape(bs, n_pages * PAGE_SIZE, N_CACHED_STREAMS, N_KV_HEADS, HEAD_DIM)
        xs, a, b, c, d, e = layer(xs, PAST_LEN, kv_past, cache_win[l], state_pool[l], state_conv[l], state_ffn[l], *weights)
        kv_s.append(a); win_s.append(b); pool_s.append(c); conv_s.append(d); ffn_s.append(e)
    y_prompt = rmsnorm(xp, norm_final)
    y_sample = rmsnorm(xs, norm_final)
    kv_prompt = jnp.stack(kv_p)
    kv_sample = jnp.stack(kv_s)
    win_prompt = jnp.stack(win_p)
    win_sample = jnp.stack(win_s)
    pool_prompt = jnp.stack(pool_p)
    pool_sample = jnp.stack(pool_s)
    conv_prompt = jnp.stack(conv_p)
    conv_sample = jnp.stack(conv_s)
    ffn_prompt = jnp.stack(ffn_p)
    ffn_sample = jnp.stack(ffn_s)
    return (y_prompt, y_sample, kv_prompt, kv_sample, win_prompt, win_sample, pool_prompt, pool_sample, conv_prompt, conv_sample, ffn_prompt, ffn_sample)
```

```python
import numpy as np
import ml_dtypes
from contextlib import ExitStack
import concourse.bass as bass
import concourse.mybir as mybir
from concourse.bass_utils import run_bass_kernel_spmd

F32 = mybir.dt.float32
BF16 = mybir.dt.bfloat16
I32 = mybir.dt.int32
AF = mybir.ActivationFunctionType
ALU = mybir.AluOpType

SEM_EPOCH = 30000
N_DSEM = 80
N_DSEM_SW = 48
NEG = -30000.0


class Buf:
    __slots__ = ("name", "last_w", "readers")

    def __init__(self, name):
        self.name = name
        self.last_w = None
        self.readers = []


class Op:
    __slots__ = ("eng", "fn", "reads", "writes", "dma", "deps", "sig", "needed", "idx")

    def __init__(self, eng, fn, reads, writes, dma):
        self.eng = eng
        self.fn = fn
        self.reads = reads
        self.writes = writes
        self.dma = dma
        self.deps = []
        self.sig = None
        self.needed = False


class Prog:
    ENGS = ("tensor", "vector", "scalar", "gpsimd", "sync")

    def __init__(self, nc):
        self.nc = nc
        self.ops = []
        self.stack = ExitStack()
        self.nbuf = 0

    def buf(self, name="b"):
        self.nbuf += 1
        return Buf(name)

    def sb(self, name, shape, dt):
        return self.stack.enter_context(self.nc.sbuf_tensor("s_" + name, list(shape), dt))

    def ps(self, name, shape, dt):
        return self.stack.enter_context(self.nc.psum_tensor("p_" + name, list(shape), dt))

    def op(self, eng, fn, reads=(), writes=()):
        o = Op(eng, fn, [b for b in reads if b is not None], [b for b in writes if b is not None], False)
        self.ops.append(o)
        return o

    def dma(self, q, out, in_, reads=(), writes=(), **kw):
        def fn(e):
            return e.dma_start(out=out, in_=in_, **kw)
        o = Op(q, fn, [b for b in reads if b is not None], [b for b in writes if b is not None], True)
        self.ops.append(o)
        return o

    def dma_fn(self, q, fn, reads=(), writes=()):
        o = Op(q, fn, [b for b in reads if b is not None], [b for b in writes if b is not None], True)
        self.ops.append(o)
        return o

    def finalize(self):
        nc = self.nc
        ops = self.ops
        for o in ops:
            deps = set()
            for b in o.reads:
                if b.last_w is not None:
                    deps.add(b.last_w)
            for b in o.writes:
                if b.last_w is not None:
                    deps.add(b.last_w)
                for r in b.readers:
                    deps.add(r)
            deps.discard(o)
            dl = []
            for d in deps:
                if (not o.dma) and (not d.dma) and o.eng == "tensor" and d.eng == "tensor":
                    continue
                d.needed = True
                dl.append(d)
            o.deps = dl
            for b in o.reads:
                b.readers.append(o)
            for b in o.writes:
                b.last_w = o
                b.readers = []
        cnt = {e: 0 for e in self.ENGS}
        epoch = {e: 0 for e in self.ENGS}
        dcount = [0] * N_DSEM
        qrange = {"gpsimd": (0, N_DSEM_SW), "sync": (N_DSEM_SW, N_DSEM)}
        qi = {q: 0 for q in qrange}
        for o in ops:
            if o.dma:
                lo, hi = qrange[o.eng]
                s = lo + qi[o.eng] % (hi - lo)
                qi[o.eng] += 1
                prev = dcount[s]
                dcount[s] += 16
                o.sig = ("d", s, dcount[s], prev)
            elif o.needed:
                e = o.eng
                if cnt[e] >= SEM_EPOCH:
                    epoch[e] += 1
                    cnt[e] = 0
                cnt[e] += 1
                o.sig = ("e", (e, epoch[e]), cnt[e])
        st = self.stack
        esem = {}
        for e in self.ENGS:
            for k in range(epoch[e] + 1):
                esem[(e, k)] = st.enter_context(nc.semaphore(f"s_{e}_{k}"))
        dsem = [st.enter_context(nc.semaphore(f"d_{k}")) for k in range(N_DSEM)]
        per_eng = {e: [] for e in self.ENGS}
        for o in ops:
            per_eng[o.eng].append(o)
        final_d = {}
        for o in ops:
            if o.dma:
                final_d[o.sig[1]] = o.sig[2]

        def emit(ename, e):
            seen = {}
            for o in per_eng[ename]:
                for d in o.deps:
                    if d.sig[0] == "d":
                        key, val, sem = ("d", d.sig[1]), d.sig[2], dsem[d.sig[1]]
                    else:
                        key, val, sem = d.sig[1], d.sig[2], esem[d.sig[1]]
                    if seen.get(key, 0) >= val:
                        continue
                    e.wait_ge(sem, val)
                    seen[key] = val
                if o.dma:
                    _, s, val, prev = o.sig
                    if prev > 0 and seen.get(("d", s), 0) < prev:
                        e.wait_ge(dsem[s], prev)
                        seen[("d", s)] = prev
                    o.fn(e).then_inc(dsem[s], 16)
                else:
                    ins = o.fn(e)
                    if o.sig is not None:
                        ins.then_inc(esem[o.sig[1]], 1)
            if ename == "sync":
                for s, val in final_d.items():
                    if seen.get(("d", s), 0) < val:
                        e.wait_ge(dsem[s], val)

        with nc.Block() as block:
            @block.tensor
            def _(e):
                emit("tensor", e)

            @block.vector
            def _(e):
                emit("vector", e)

            @block.scalar
            def _(e):
                emit("scalar", e)

            @block.gpsimd
            def _(e):
                emit("gpsimd", e)

            @block.sync
            def _(e):
                emit("sync", e)
        self.stack.close()
        return {e: len(per_eng[e]) for e in self.ENGS}


class Cfg:
    def __init__(self, SEQ=4096, L=4, NB=16, PAST=2048, NPOOL=2560, nsa=True, do_prompt=True, do_sample=True, ntile=None):
        self.D = 1024
        self.SEQ = SEQ
        self.L = L
        self.NB = NB
        self.PAST = PAST
        self.NPOOL = NPOOL
        self.NT = 256
        self.NTILE = SEQ // 256
        self.NPG = PAST // 128
        self.TS = PAST + 4
        self.NCMP_S = (self.TS - 32) // 16 + 1
        self.NSEL_S = -(-self.TS // 64)
        self.NSEL_P = SEQ // 64
        self.NCMP_P = (SEQ - 32) // 16 + 1
        self.NCH_P = -(-(self.NCMP_P + 1) // 128)
        self.DFF = 2816
        self.NF = 22
        self.INW = 5400
        self.nsa = nsa
        self.do_prompt = do_prompt
        self.do_sample = do_sample
        self.ntile_run = ntile if ntile is not None else self.NTILE
        import os
        self.nsa_stage = int(os.environ.get('NSA_STAGE', '99'))
        self.nsa_sub = int(os.environ.get('NSA_SUB', '99'))
        self.nsa_var = int(os.environ.get('NSA_VAR', '0'))


FULL = Cfg()

OFF_POOL = 0
OFF_Q = 256
OFF_KV = 768
OFF_NG = 1536
OFF_CB = 1560
OFF_CC = 1816
OFF_CX = 2072
OFF_GP = 2328
OFF_GN = 3352
OFF_GC = 4376


def slot_plan(cfg):
    slots = []
    names = {}

    def full_k(wname, cols, key, K=1024):
        pieces = []
        m0 = 0
        for (c0, n) in cols:
            for kc in range(K // 128):
                pieces.append((wname, kc * 128, 128, 0, kc, c0, n, m0))
            m0 += n
        names[key] = len(slots)
        slots.append(pieces)

    for g in range(4):
        full_k("w_in", [(OFF_POOL + 64 * g, 64)], ("pool", g))
    for j in range(2):
        full_k("w_in", [(OFF_CB + 128 * j, 128)], ("cb", j))
    for j in range(2):
        full_k("w_in", [(OFF_CC + 128 * j, 128)], ("cc", j))
    for j in range(2):
        full_k("w_in", [(OFF_CX + 128 * j, 128)], ("cx", j))
    for s in range(6):
        full_k("w_in", [(OFF_KV + 128 * s, 128)], ("kv", s))
    for r in range(4):
        full_k("w_in", [(OFF_Q + 64 * r, 64), (OFF_Q + 64 * (4 + r), 64)], ("q", r))
    full_k("w_in", [(OFF_NG, 24)], ("ng", 0))
    names[("pool_w", 0)] = len(slots)
    slots.append([("pool_w", g * 64, 64, 0, g, 0, 64, 0) for g in range(4)])
    for s in range(2):
        for q4 in range(4):
            pieces = []
            for l8 in range(8):
                l = q4 * 8 + l8
                for dup in range(2):
                    pieces.append((f"cmp_w1_{s}", l * 64, 64, dup * 64, l8, 0, 128, 0))
            names[("w1", s, q4)] = len(slots)
            slots.append(pieces)
    names[("w2", 0)] = len(slots)
    slots.append([(f"cmp_w2_{s}", 0, 128, 0, s, 0, 64, 0) for s in range(2)]
                 + [(f"cmp_w2_{s}", 0, 128, 0, s, 0, 64, 64) for s in range(2)])
    for m in range(8):
        full_k("w_in", [(OFF_GP + 128 * m, 128)], ("gp", m))
        names[("brp", m)] = len(slots)
        slots.append([("w_br_pool", g * 64, 64, 0, g, 128 * m, 128, 0) for g in range(4)])
        full_k("w_in", [(OFF_GN + 128 * m, 128)], ("gn", m))
        names[("brnc", m)] = len(slots)
        slots.append([("w_br_nsa", k * 128, 128, 0, k, 128 * m, 128, 0) for k in range(4)]
                     + [("w_br_conv", k * 128, 128, 0, 4 + k, 128 * m, 128, 0) for k in range(2)])
        full_k("w_in", [(OFF_GC + 128 * m, 128)], ("gc", m))
    for m in range(8):
        full_k("w_out", [(128 * m, 128)], ("wo", m))
    for f in range(22):
        full_k("ffn_up", [(128 * f, 128)], ("fa", f))
        full_k("ffn_up", [(2816 + 128 * f, 128)], ("fb", f))
    for m in range(8):
        for part in range(3):
            kcs = list(range(part * 8, min(22, part * 8 + 8)))
            names[("fd", m, part)] = len(slots)
            slots.append([("ffn_down", kc * 128, 128, 0, i, 128 * m, 128, 0) for i, kc in enumerate(kcs)])
    return slots, names


def make_consts(cfg):
    c = {}
    half = 32
    inv = (10000.0 ** (-np.arange(half, dtype=np.float32) / half)).astype(np.float32)

    def cs_tab(pos):
        ang = pos.astype(np.float32)[None, :] * inv[:, None]
        cos = np.cos(ang).astype(np.float32)
        sin = np.sin(ang).astype(np.float32)
        return np.tile(cos, (4, 1)), np.tile(sin, (4, 1))

    c["cos_p"], c["sin_p"] = cs_tab(np.arange(cfg.SEQ))
    ps = cfg.PAST + np.arange(4)
    cs, sn = cs_tab(ps)
    c["cos_s"] = np.tile(cs, (1, cfg.NB)).astype(np.float32)
    c["sin_s"] = np.tile(sn, (1, cfg.NB)).astype(np.float32)
    ncolp = cfg.NCH_P * 128
    c["cosc_p"], c["sinc_p"] = cs_tab(16 * (np.arange(ncolp) - 1) + 31)
    c["cosc_s"], c["sinc_s"] = cs_tab(16 * np.arange(128) + 31)
    R = np.zeros((128, 128), np.float32)
    for hh in range(2):
        for d in range(64):
            m = hh * 64 + d
            if d < 32:
                R[hh * 64 + d + 32, m] = -1.0
            else:
                R[hh * 64 + d - 32, m] = 1.0
    c["rot"] = R
    c["ident"] = np.eye(128, dtype=np.float32)
    c["iota_p"] = np.arange(128, dtype=np.float32).reshape(128, 1)
    nk = max(cfg.SEQ, cfg.NSEL_S * 64)
    e = np.zeros((64, nk), np.float32)
    for j in range(64):
        e[j, j * 64:(j + 1) * 64] = 1.0
    c["eexp"] = e
    k = np.arange(128)[:, None]
    q = np.arange(128)[None, :]
    c["tri"] = np.where(k <= q, 0.0, NEG).astype(np.float32)
    c["winlo"] = np.where(k >= q, 0.0, NEG).astype(np.float32)
    nq = cfg.SEQ // 128
    cmpb = np.zeros((nq, 128, cfg.NCH_P, 128), np.float32)
    scA = np.zeros((nq, 128, 64), np.float32)
    scB = np.zeros((nq, 128, 64), np.float32)
    for qi in range(nq):
        pos = qi * 128 + np.arange(128)
        for ch in range(cfg.NCH_P):
            col = ch * 128 + np.arange(128)
            ok = (16 * col[:, None] + 15 <= pos[None, :]) & (col[:, None] >= 1) & (col[:, None] <= cfg.NCMP_P)
            cmpb[qi, :, ch, :] = np.where(ok, 0.0, NEG)
        jj = np.arange(64)[None, :]
        cur = (pos // 64)[:, None]
        causal = (jj * 64 <= pos[:, None]) & (jj < cfg.NSEL_P)
        forced = ((jj == 0) | ((jj <= cur) & (jj > cur - 2))) & (jj < cfg.NSEL_P)
        scA[qi] = np.where(forced, 0.0, np.where(causal, 1.0, 0.0))
        scB[qi] = np.where(forced, 1e4, np.where(causal, 0.0, -1.0))
    c["cmpb"] = cmpb
    c["scA"] = scA
    c["scB"] = scB
    cov = np.zeros((128, cfg.NCH_P, 65), np.float32)
    for ch in range(cfg.NCH_P):
        for cc in range(128):
            n = ch * 128 + cc - 1
            if n < 0 or n >= cfg.NCMP_P:
                continue
            cst = 16 * n
            for j in range(cfg.NSEL_P):
                if cst < j * 64 + 64 and cst + 32 > j * 64:
                    cov[cc, ch, j] = 1.0
            cov[cc, ch, 64] = 1.0
    c["cover_p"] = cov
    covs = np.zeros((128, 65), np.float32)
    for n in range(cfg.NCMP_S):
        cst = 16 * n
        for j in range(cfg.NSEL_S):
            if cst < j * 64 + 64 and cst + 32 > j * 64:
                covs[n, j] = 1.0
        covs[n, 64] = 1.0
    c["cover_s"] = covs
    cur = cfg.PAST // 64
    jj = np.arange(64)
    forced = (jj == 0) | ((jj <= cur) & (jj > cur - 2))
    exist = jj < cfg.NSEL_S
    A = np.where(forced | ~exist, 0.0, 1.0)
    B = np.where(~exist, -1.0, np.where(forced, 1e4, 0.0))
    c["scA_s"] = np.tile(A[None, :], (4, 1)).astype(np.float32)
    c["scB_s"] = np.tile(B[None, :], (4, 1)).astype(np.float32)
    kk = np.arange(4)[:, None]
    qq = np.arange(4)[None, :]
    c["newtri"] = np.where(kk <= qq, 0.0, NEG).astype(np.float32)
    c["winlo_s"] = np.where(np.arange(128)[:, None] >= qq, 0.0, NEG).astype(np.float32)
    rc = np.zeros((64, 4, 16), np.float32)
    for g, w in enumerate((2, 4, 8, 16)):
        rc[:, g, :] = 1.0 / np.minimum(float(w), np.arange(16) + 1.0)[None, :]
    c["rcnt"] = rc
    return c


CONST_BF = ("ident", "eexp", "tri", "winlo", "cmpb", "cover_p", "cover_s", "newtri", "winlo_s")


def bc_mid(ap, r):
    a = [list(x) for x in ap.ap]
    return bass.AP(ap.tensor, ap.offset, [a[0], [0, r]] + a[1:])


class Builder:
    def __init__(self, cfg):
        self.cfg = cfg
        self.nc = bass.Bass("TRN2", target_bir_lowering=False)
        self.P = Prog(self.nc)
        self.din = {}
        self.dout = {}
        self.slots, self.sname = slot_plan(cfg)
        self.NS = len(self.slots)
        self.sdims = [(max(p[3] + p[2] for p in ps), max(p[4] for p in ps) + 1, max(p[7] + p[6] for p in ps)) for ps in self.slots]

    def inp(self, name, shape, dt=F32):
        t = self.nc.dram_tensor(name, list(shape), dt, kind="ExternalInput").ap()
        self.din[name] = (tuple(shape), dt)
        return t

    def outp(self, name, shape, dt=F32):
        t = self.nc.dram_tensor(name, list(shape), dt, kind="ExternalOutput").ap()
        self.dout[name] = (tuple(shape), dt)
        return t

    def mm(self, out, lhsT, rhs, start, stop, reads, writes, skip=False):
        kw = {}
        if skip:
            kw["skip_group_check"] = True
        self.P.op("tensor", lambda e: e.matmul(out, lhsT=lhsT, rhs=rhs, start=start, stop=stop, **kw), reads, writes)

    def tr(self, out, in_, ident, reads, writes):
        self.P.op("tensor", lambda e: e.transpose(out, in_, ident), reads, writes)

    def act(self, out, in_, func, reads, writes, **kw):
        self.P.op("scalar", lambda e: e.activation(out, in_, func, **kw), reads, writes)

    def v(self, eng, name, reads, writes, *a, **kw):
        self.P.op(eng, lambda e: getattr(e, name)(*a, **kw), reads, writes)

    def build(self):
        cfg = self.cfg
        P = self.P
        nc = self.nc
        L, NT, NB = cfg.L, cfg.NT, cfg.NB
        NS = self.NS
        SEQ = cfg.SEQ
        NKT = SEQ // 128
        NSUB = NT // 128
        W = {}
        W["w_in"] = self.inp("w_in", [L, 1024, 5400])
        W["pool_w"] = self.inp("pool_w", [L, 256, 64])
        for s in range(2):
            W[f"cmp_w1_{s}"] = self.inp(f"cmp_w1_{s}", [L, 2048, 128])
            W[f"cmp_w2_{s}"] = self.inp(f"cmp_w2_{s}", [L, 128, 64])
        W["w_br_pool"] = self.inp("w_br_pool", [L, 256, 1024])
        W["w_br_nsa"] = self.inp("w_br_nsa", [L, 512, 1024])
        W["w_br_conv"] = self.inp("w_br_conv", [L, 256, 1024])
        W["w_out"] = self.inp("w_out", [L, 1024, 1024])
        W["ffn_up"] = self.inp("ffn_up", [L, 1024, 5632])
        W["ffn_down"] = self.inp("ffn_down", [L, 2816, 1024])
        ws = nc.dram_tensor("ws", [L * NS, 128, 1024], BF16, kind="Internal").ap()
        hist_k = nc.dram_tensor("hist_k", [L, 128, SEQ], BF16, kind="Internal").ap()
        hist_v = nc.dram_tensor("hist_v", [L, 128, NKT * 130], BF16, kind="Internal").ap()
        d_gmix = self.inp("gmix", [128, L, 8])
        d_gffn = self.inp("gffn", [128, L, 8])
        d_gfin = self.inp("gfin", [128, 8])
        d_pscale = self.inp("pscale", [64, L, 4])
        d_convw = self.inp("convw", [128, L, 2, 3])
        d_ffnw = self.inp("ffnw", [128, L, 22, 3])
        d_peT = self.inp("peT", [64, L, 2, 32])
        consts = make_consts(cfg)
        dC = {}
        for k, a in consts.items():
            dC[k] = self.inp("c_" + k, a.shape, F32)
        d_xp = self.inp("xpT", [1024, SEQ])
        d_xs = self.inp("xsT", [1024, NB * 4])
        d_poolA = self.inp("poolA", [L * cfg.NPOOL * 128, 384])
        d_poolB = self.inp("poolB", [L * cfg.NPOOL * 128, 128])
        d_pt = self.inp("ptab", [1, NB * cfg.NPG], I32)
        d_cwin = self.inp("cwin", [L, NB, 512, 256])
        d_kwinT = self.inp("kwinT", [L, NB, 128, 512])
        d_spool = self.inp("spoolT", [L, 64, 4 * NB * 15])
        d_sconv = self.inp("sconvT", [L, 128, 2 * NB * 2])
        d_sffn = self.inp("sffnT", [L, 128, 22 * NB * 2])
        o_yp = self.outp("o_yp", [1024, SEQ])
        o_ys = self.outp("o_ys", [1024, NB * 4])
        o_kvp = self.outp("o_kvp", [L, 4, 128, SEQ])
        o_kvs = self.outp("o_kvs", [L, 4, 128, NB * 4])
        o_winp = self.outp("o_winp", [L, 2, 128, 512])
        o_wins_new = self.outp("o_wins_new", [L, 2, 128, NB * 4])
        o_wins_old = self.outp("o_wins_old", [L, NB, 508, 256])
        o_poolp = self.outp("o_poolp", [L, 64, 4 * 15])
        o_pools = self.outp("o_pools", [L, 64, 4 * NB * 15])
        o_convp = self.outp("o_convp", [L, 128, 2 * 2])
        o_convs = self.outp("o_convs", [L, 128, 2 * NB * 2])
        o_ffnp = self.outp("o_ffnp", [L, 128, 22 * 2])
        o_ffns = self.outp("o_ffns", [L, 128, 22 * NB * 2])

        OQ = "gpsimd"
        Bws = [[P.buf(f"ws{l}_{si}") for si in range(NS)] for l in range(L)]

        def emit_casts(l):
            for si, pieces in enumerate(self.slots):
                groups = {}
                for (wn, r0, nr, p0, kc, c0, ncol, m0) in pieces:
                    groups.setdefault((wn, nr, p0, c0, ncol, m0), []).append((r0, kc))
                for (wn, nr, p0, c0, ncol, m0), lst in groups.items():
                    lst.sort(key=lambda t: t[1])
                    r0s = [t[0] for t in lst]
                    kcs = [t[1] for t in lst]
                    n = len(lst)
                    step_r = (r0s[1] - r0s[0]) if n > 1 else 0
                    step_k = (kcs[1] - kcs[0]) if n > 1 else 1
                    assert all(r0s[i] - r0s[0] == i * step_r and kcs[i] - kcs[0] == i * step_k for i in range(n))
                    wt = W[wn]
                    rowlen = wt.shape[2]
                    src0 = wt[l, r0s[0]:r0s[0] + nr, c0:c0 + ncol]
                    src = bass.AP(src0.tensor, src0.offset, [[rowlen, nr], [step_r * rowlen, n], [1, ncol]])
                    ph, kh, mh = self.sdims[si]
                    dst0 = ws[l * NS + si, p0:p0 + nr, kcs[0] * mh + m0: kcs[0] * mh + m0 + ncol]
                    dst = bass.AP(dst0.tensor, dst0.offset, [[1024, nr], [step_k * mh, n], [1, ncol]])
                    P.dma("gpsimd", dst, src, writes=[Bws[l][si]])

        for l in range(L):
            emit_casts(l)

        def load_const(name, dram, shape, dt, buf=None):
            t = P.sb(name, shape, dt)
            b = buf if buf is not None else P.buf(name)
            if dt == F32:
                P.dma("sync", t[:], dram, writes=[b])
            else:
                P.dma("gpsimd", t[:], dram, writes=[b])
            return t, b

        gmix, Bgmix = load_const("gmix", d_gmix, [128, L, 8], F32)
        gffn, Bgffn = load_const("gffn", d_gffn, [128, L, 8], F32)
        gfin, Bgfin = load_const("gfin", d_gfin, [128, 8], F32)
        pscale, Bpscale = load_const("pscale", d_pscale, [64, L, 4], F32)
        convw, Bconvw = load_const("convw", d_convw, [128, L, 2, 3], F32)
        ffnw, Bffnw = load_const("ffnw", d_ffnw, [128, L, 22, 3], F32)
        peT, BpeT = load_const("peT", d_peT, [64, L, 2, 32], BF16)
        rot, Brot = load_const("rot", dC["rot"], [128, 128], F32)
        identf, Bidentf = load_const("identf", dC["ident"], [128, 128], F32)
        identb, Bidentb = load_const("identb", dC["ident"], [128, 128], BF16)
        eexp, Beexp = load_const("eexp", dC["eexp"], list(consts["eexp"].shape), BF16)
        tri, Btri = load_const("tri", dC["tri"], [128, 128], BF16)
        winlo, Bwinlo = load_const("winlo", dC["winlo"], [128, 128], BF16)
        cover_p, Bcover_p = load_const("cover_p", dC["cover_p"], [128, cfg.NCH_P, 65], BF16)
        cover_s, Bcover_s = load_const("cover_s", dC["cover_s"], [128, 65], BF16)
        cosc_p, Bcc1 = load_const("cosc_p", dC["cosc_p"], [128, cfg.NCH_P * 128], F32)
        sinc_p, _ = load_const("sinc_p", dC["sinc_p"], [128, cfg.NCH_P * 128], F32, Bcc1)
        cosc_s, Bcc3 = load_const("cosc_s", dC["cosc_s"], [128, 128], F32)
        sinc_s, _ = load_const("sinc_s", dC["sinc_s"], [128, 128], F32, Bcc3)
        cos_s, Bcs1 = load_const("cos_s", dC["cos_s"], [128, NB * 4], F32)
        sin_s, _ = load_const("sin_s", dC["sin_s"], [128, NB * 4], F32, Bcs1)
        scA_s, BscA_s = load_const("scA_s", dC["scA_s"], [4, 64], F32)
        scB_s, _ = load_const("scB_s", dC["scB_s"], [4, 64], F32, BscA_s)
        newtri, Bnewtri = load_const("newtri", dC["newtri"], [4, 4], BF16)
        winlo_s, Bwinlo_s = load_const("winlo_s", dC["winlo_s"], [128, 4], BF16)
        rcnt, Brcnt = load_const("rcnt", dC["rcnt"], [64, 4, 16], F32)
        iota_p, Biota = load_const("iota_p", dC["iota_p"], [128, 1], F32)
        ones_b = P.sb("ones_b", [128, 128], BF16)
        Bones = P.buf("ones")
        self.v("vector", "memset", [], [Bones], ones_b[:], 1.0)

        x = P.sb("x", [128, 8, NT], F32); Bx = P.buf("x")
        hT = P.sb("hT", [128, 8, NT], BF16); BhT = P.buf("hT")
        sq = P.sb("sq", [128, 8, NT], BF16); Bsq = P.buf("sq")
        rstd = P.sb("rstd", [128, NT], F32); Brstd = P.buf("rstd")
        NRING = 7
        ring = [P.sb(f"wr{i}", [128, 8, 128], BF16) for i in range(NRING)]
        Bring = [P.buf(f"wr{i}") for i in range(NRING)]
        self.ring_i = 0
        PSA = [P.ps(f"psA{i}", [128, 512], F32) for i in range(2)]
        BPSA = [P.buf(f"psA{i}") for i in range(2)]
        PSS = [P.ps(f"psS{i}", [128, 512], F32) for i in range(2)]
        BPSS = [P.buf(f"psS{i}") for i in range(2)]
        PSO = [P.ps(f"psO{i}", [128, 512], F32) for i in range(2)]
        BPSO = [P.buf(f"psO{i}") for i in range(2)]
        PST = P.ps("psT", [128, 512], F32); BPST = P.buf("psT")
        PSM = P.ps("psM", [128, 512], F32); BPSM = P.buf("psM")
        PSMb = PSM[:].bitcast(BF16)
        PSTb = PST[:].bitcast(BF16)
        self.psa_i = 0
        self.pss_i = 0
        self.pso_i = 0

        def next_psa():
            i = self.psa_i % 2
            self.psa_i += 1
            return PSA[i], BPSA[i]

        def next_pss():
            i = self.pss_i % 2
            self.pss_i += 1
            return PSS[i], BPSS[i]

        def next_pso():
            i = self.pso_i % 2
            self.pso_i += 1
            return PSO[i], BPSO[i]

        def wslot(l, key):
            si = self.sname[key]
            i = self.ring_i % NRING
            self.ring_i += 1
            ph, kh, mh = self.sdims[si]
            flat = ring[i][:].rearrange("p k m -> p (k m)")
            P.dma("sync", flat[0:ph, 0:kh * mh], ws[l * NS + si, 0:ph, 0:kh * mh], reads=[Bws[l][si]], writes=[Bring[i]])
            return flat[:, 0:kh * mh].rearrange("p (k m) -> p k m", m=mh), Bring[i]

        EWMAX = max(NB * 19, 15 + NT)
        extp = P.sb("extp", [64, 4, EWMAX], F32); Bextp = P.buf("extp")
        ptmp = [P.sb(f"ptmp{i}", [64, EWMAX], F32) for i in range(2)]
        Bptmp = [P.buf(f"ptmp{i}") for i in range(2)]
        pooled = P.sb("pooled", [64, 4, NT], BF16); Bpooled = P.buf("pooled")
        ypool = P.sb("ypool", [64, 4, NT], BF16); Bypool = P.buf("ypool")
        cbuf = P.sb("cbuf", [128, 2, NT], F32); Bcb = P.buf("cb")
        cctmp = P.sb("cctmp", [128, NT], F32); Bcct = P.buf("cct")
        extc = P.sb("extc", [128, 2, NT + 2 * NB], F32); Bextc = P.buf("extc")
        ctmp = P.sb("ctmp", [128, NT], F32); Bctmp = P.buf("ctmp")
        yconv = P.sb("yconv", [128, 2, NT], BF16); Byconv = P.buf("yconv")
        kvf = P.sb("kvf", [128, 6, NT], F32); Bkvf = [P.buf(f"kvf{s}") for s in range(6)]
        ropet = P.sb("ropet", [128, 2, NT], F32); Bropet = P.buf("ropet")
        cosT = P.sb("cosT", [128, NT], F32); sinT = P.sb("sinT", [128, NT], F32); Bcs = P.buf("cossin")
        QTz = [P.sb(f"QTz{g}", [128, 4, NT], BF16) for g in range(2)]; BQT = P.buf("QT")
        for g in range(2):
            self.v("gpsimd", "memset", [], [BQT], QTz[g][:], 0.0)
        NG = max(NSUB, NB)
        gsig = P.sb("gsig", [128, NG, 24], F32); Bgsig = P.buf("gsig")
        ynsa = P.sb("ynsa", [128, 4, NT], BF16); Bynsa = P.buf("ynsa")
        macc = P.sb("macc", [128, NT], F32); Bmacc = P.buf("macc")
        ostg = [P.sb(f"ostg{i}", [128, NT], F32) for i in range(2)]
        Bostg = [P.buf(f"ostg{i}") for i in range(2)]
        sgate = [P.sb(f"sgate{i}", [128, NT], F32) for i in range(2)]
        Bsgate = [P.buf(f"sgate{i}") for i in range(2)]
        mtmp = P.sb("mtmp", [128, NT], F32); Bmtmp = P.buf("mtmp")
        exta = [P.sb(f"exta{i}", [128, NT + 2 * NB], F32) for i in range(2)]
        Bexta = [P.buf(f"exta{i}") for i in range(2)]
        atmp = [P.sb(f"atmp{i}", [128, NT], F32) for i in range(2)]
        Batmp = [P.buf(f"atmp{i}") for i in range(2)]
        gT = P.sb("gT", [128, 22, NT], BF16); BgT = P.buf("gT")
        poolpref = P.sb("poolpref", [64, L, 4, 15], F32); Bpoolpref = P.buf("poolpref")
        convpref = P.sb("convpref", [128, L, 2, 2], F32); Bconvpref = P.buf("convpref")
        ffnpref = P.sb("ffnpref", [128, L, 22, 2], F32); Bffnpref = P.buf("ffnpref")
        sffn = P.sb("sffn", [128, 22, NB, 2], F32); Bsffn = P.buf("sffn")
        ffnnew = P.sb("ffnnew", [128, 22, NB, 2], F32); Bffnnew = P.buf("ffnnew")
        spool_st = P.sb("spool_st", [64, 4, NB, 15], F32); Bspool_st = P.buf("spool_st")
        sconv_st = P.sb("sconv_st", [128, 2, NB, 2], F32); Bsconv_st = P.buf("sconv_st")
        self.v("gpsimd", "memset", [], [Bpoolpref], poolpref[:], 0.0)
        self.v("gpsimd", "memset", [], [Bconvpref], convpref[:], 0.0)
        self.v("gpsimd", "memset", [], [Bffnpref], ffnpref[:], 0.0)

        Kwk = P.sb("Kwk", [128, SEQ], BF16); BKwk = P.buf("Kwk")
        Vwk = P.sb("Vwk", [128, NKT, 2, 65], BF16); BVwk = P.buf("Vwk")
        Bhk = [P.buf(f"hk{l}") for l in range(L)]
        Bhv = [P.buf(f"hv{l}") for l in range(L)]
        NWS = 512 // NT + 1
        KwT = [P.sb(f"KwT{l}", [128, NWS, NT], BF16) for l in range(L)]
        Vw = [P.sb(f"Vw{l}", [128, NWS * NSUB, 2, 65], BF16) for l in range(L)]
        NCC = cfg.NCH_P * 128
        kcT = [P.sb(f"kcT{l}", [128, NCC], BF16) for l in range(L)]
        vcT = [P.sb(f"vcT{l}", [128, NCC], BF16) for l in range(L)]
        vcp = [P.sb(f"vcp{l}", [128, cfg.NCH_P, 2, 65], BF16) for l in range(L)]
        ctail = [P.sb(f"ctail{l}", [128, 2, 16], BF16) for l in range(L)]
        BKw = [P.buf() for l in range(L)]; BVw = [P.buf() for l in range(L)]
        Bkc = [P.buf() for l in range(L)]; BvcT = [P.buf() for l in range(L)]; Bvc = [P.buf() for l in range(L)]
        Bctail = [P.buf() for l in range(L)]
        self.v("gpsimd", "memset", [], [BKwk], Kwk[:], 0.0)
        self.v("gpsimd", "memset", [], [BVwk], Vwk[:], 0.0)
        self.v("gpsimd", "memset", [], [BVwk], Vwk[:, :, :, 64:65], 1.0)
        for l in range(L):
            self.v("gpsimd", "memset", [], [BKw[l]], KwT[l][:], 0.0)
            self.v("gpsimd", "memset", [], [BVw[l]], Vw[l][:], 0.0)
            self.v("gpsimd", "memset", [], [BVw[l]], Vw[l][:, :, :, 64:65], 1.0)
            self.v("gpsimd", "memset", [], [Bkc[l]], kcT[l][:], 0.0)
            self.v("gpsimd", "memset", [], [BvcT[l]], vcT[l][:], 0.0)
            self.v("gpsimd", "memset", [], [Bvc[l]], vcp[l][:], 0.0)
            self.v("gpsimd", "memset", [], [Bvc[l]], vcp[l][:, :, :, 64:65], 1.0)
            self.v("gpsimd", "memset", [], [Bctail[l]], ctail[l][:], 0.0)
        cext = P.sb("cext", [128, 2, 2, 16 + NT], BF16); Bcext = P.buf("cext")
        self.v("gpsimd", "memset", [], [Bcext], cext[:], 0.0)
        cgel = P.sb("cgel", [128, 2, 128], F32); Bcgel = P.buf("cgel")
        cgel2 = P.sb("cgel2", [128, 2, 128], F32); Bcgel2 = P.buf("cgel2")
        cgb = P.sb("cgb", [128, 2, 128], BF16); Bcgb = P.buf("cgb")
        cvec = P.sb("cvec", [128, 2], F32); Bcvec = P.buf("cvec")
        NPT = 3
        PT = [P.sb(f"PT{i}", [128, 512], BF16) for i in range(NPT)]
        BPT = [P.buf(f"PT{i}") for i in range(NPT)]
        self.pt_i = 0

        def next_pt():
            i = self.pt_i % NPT
            self.pt_i += 1
            return PT[i], BPT[i]

        osb = [P.sb(f"osb{i}", [65, 512], F32) for i in range(2)]
        Bosb = [P.buf(f"osb{i}") for i in range(2)]
        self.osb_i = 0
        otok = P.sb("otok", [128, 512], F32); Botok = P.buf("otok")
        otokb = P.sb("otokb", [128, 512], BF16); Botokb = P.buf("otokb")
        smal = P.sb("smal", [128, 64], F32); Bsmal = P.buf("smal")
        imp = P.sb("imp", [128, 64], F32); Bimp = P.buf("imp")
        score = P.sb("score", [128, 64], F32); Bscore = P.buf("score")
        scw = P.sb("scw", [128, 64], F32); Bscw = P.buf("scw")
        mx = P.sb("mx", [128, 16], F32); Bmx = P.buf("mx")
        selb = P.sb("selb", [128, 64], F32); Bselb = P.buf("selb")
        selbT = P.sb("selbT", [64, 128], BF16); BselbT = P.buf("selbT")
        scAB = P.sb("scAB", [128, 2, 64], F32); BscAB = P.buf("scAB")
        cmpb = P.sb("cmpb", [128, cfg.NCH_P, 128], BF16); Bcmpb = P.buf("cmpb")

        def rms_stats(nt):
            for kc in range(8):
                self.act(sq[:, kc, 0:nt], x[:, kc, 0:nt], AF.Square, [Bx], [Bsq])
            for kc in range(8):
                self.mm(PSM[:, 0:nt], ones_b[:], sq[:, kc, 0:nt], kc == 0, kc == 7, [Bones, Bsq], [BPSM])
            self.v("vector", "tensor_scalar", [BPSM], [Brstd], out=rstd[:, 0:nt], in0=PSM[:, 0:nt],
                   scalar1=1.0 / 1024.0, scalar2=1e-6, op0=ALU.mult, op1=ALU.add)
            self.act(rstd[:, 0:nt], rstd[:, 0:nt], AF.Sqrt, [Brstd], [Brstd])
            self.v("vector", "reciprocal", [Brstd], [Brstd], out=rstd[:, 0:nt], in_=rstd[:, 0:nt])

        def rmsnorm(g_ap_fn, Bg, nt):
            rms_stats(nt)
            for kc in range(8):
                self.v("vector", "scalar_tensor_tensor", [Bx, Brstd, Bg], [BhT], out=hT[:, kc, 0:nt], in0=x[:, kc, 0:nt],
                       scalar=g_ap_fn(kc), in1=rstd[:, 0:nt], op0=ALU.mult, op1=ALU.mult)

        def final_out(nt, dst_fn):
            rms_stats(nt)
            for kc in range(8):
                st, Bst = ostg[kc % 2], Bostg[kc % 2]
                self.v("vector", "scalar_tensor_tensor", [Bx, Brstd, Bgfin], [Bst], out=st[:, 0:nt], in0=x[:, kc, 0:nt],
                       scalar=gfin[:, kc:kc + 1], in1=rstd[:, 0:nt], op0=ALU.mult, op1=ALU.mult)
                P.dma(OQ, dst_fn(kc), st[:, 0:nt], reads=[Bst])

        def proj(l, key, M, nt, K=8):
            wt, Bw = wslot(l, key)
            ps, Bp = next_psa()
            for kc in range(K):
                self.mm(ps[0:M, 0:nt], wt[:, kc, 0:M], hT[:, kc, 0:nt], kc == 0, kc == K - 1, [Bw, BhT], [Bp])
            return ps, Bp

        def rope_apply(src_ps, Bsrc, dst_f32, dst_bf, Bdst_list, nt, cos_ap, sin_ap, Bcos, rows=slice(0, 128)):
            r = rows
            self.act(ropet[r, 0, 0:nt], src_ps, AF.Copy, [Bsrc], [Bropet])
            self.mm(PSM[r, 0:nt], rot[r, r], ropet[r, 0, 0:nt], True, True, [Brot, Bropet], [BPSM])
            self.v("vector", "tensor_tensor", [BPSM, Bcos], [Bropet], out=ropet[r, 1, 0:nt], in0=PSM[r, 0:nt], in1=sin_ap, op=ALU.mult)
            self.v("gpsimd", "tensor_tensor", [Bropet, Bcos], [Bropet], out=ropet[r, 0, 0:nt], in0=ropet[r, 0, 0:nt], in1=cos_ap, op=ALU.mult)
            if isinstance(dst_bf, tuple):
                for hh, d_ in enumerate(dst_bf):
                    rr = slice(hh * 64, (hh + 1) * 64)
                    self.v("vector", "tensor_tensor", [Bropet], Bdst_list, out=d_, in0=ropet[rr, 0, 0:nt], in1=ropet[rr, 1, 0:nt], op=ALU.add)
                return
            dst = dst_f32 if dst_f32 is not None else dst_bf
            self.v("vector", "tensor_tensor", [Bropet], Bdst_list, out=dst, in0=ropet[r, 0, 0:nt], in1=ropet[r, 1, 0:nt], op=ALU.add)

        def layer_tile(l, nt, nseq, S, prompt, ti):
            PP, PC = 15, 2
            rmsnorm(lambda kc: gmix[:, l, kc:kc + 1], Bgmix, nt)
            EW = PP + S
            extv = extp[:, :, 0:nseq * EW].rearrange("p g (b e) -> p g b e", e=EW)
            if prompt:
                self.v("gpsimd", "tensor_copy", [Bpoolpref], [Bextp], out=extv[:, :, 0, 0:PP], in_=poolpref[:, l, :, :])
            else:
                P.dma("sync", spool_st[:].rearrange("p g b e -> p (g b e)"), d_spool[l], writes=[Bspool_st])
                self.v("gpsimd", "tensor_copy", [Bspool_st], [Bextp], out=extv[:, :, :, 0:PP], in_=spool_st[:])
            for g in range(4):
                ps, Bp = proj(l, ("pool", g), 64, nt)
                self.act(extv[:, g, :, PP:PP + S], ps[0:64, 0:nt].rearrange("p (b s) -> p b s", s=S), AF.Copy, [Bp], [Bextp])
            if prompt:
                self.v("gpsimd", "tensor_copy", [Bextp], [Bpoolpref], out=poolpref[:, l, :, :], in_=extv[:, :, 0, S:S + PP])
                if ti == cfg.NTILE - 1:
                    P.dma(OQ, o_poolp[l], poolpref[:, l, :, :].rearrange("p g e -> p (g e)"), reads=[Bpoolpref])
            else:
                self.v("gpsimd", "tensor_copy", [Bextp], [Bspool_st], out=spool_st[:], in_=extv[:, :, :, S:S + PP])
                P.dma(OQ, o_pools[l], spool_st[:].rearrange("p g b e -> p (g b e)"), reads=[Bspool_st])
            for g in range(4):
                cur = extv[:, g, :, :]
                off = 0
                width = EW
                for step in range(g + 1):
                    sh = 1 << step
                    t_i = step % 2
                    nw = width - sh
                    dst = ptmp[t_i][:, 0:nseq * nw].rearrange("p (b e) -> p b e", e=nw)
                    self.v("vector", "tensor_tensor", [Bextp, Bptmp[1 - t_i]] if step else [Bextp], [Bptmp[t_i]],
                           out=dst, in0=cur[:, :, sh:width], in1=cur[:, :, 0:nw], op=ALU.add)
                    cur = dst
                    off += sh
                    width = nw
                w = 2 << g
                j0 = PP - off
                self.v("vector", "scalar_tensor_tensor", [Bptmp[g % 2], Bextp], [Bpooled],
                       out=pooled[:, g, 0:nt].rearrange("p (b s) -> p b s", s=S), in0=cur[:, :, j0:j0 + S], scalar=1.0 / w,
                       in1=extv[:, g, :, PP:PP + S], op0=ALU.mult, op1=ALU.subtract)
                if prompt and ti == 0:
                    self.v("vector", "tensor_tensor", [Bptmp[g % 2], Brcnt], [Bptmp[g % 2]], out=cur[:, 0, j0:j0 + 16],
                           in0=cur[:, 0, j0:j0 + 16], in1=rcnt[:, g, :], op=ALU.mult)
                    self.v("vector", "tensor_tensor", [Bptmp[g % 2], Bextp], [Bpooled], out=pooled[:, g, 0:16],
                           in0=cur[:, 0, j0:j0 + 16], in1=extv[:, g, 0, PP:PP + 16], op=ALU.subtract)
            wpw, Bwpw = wslot(l, ("pool_w", 0))
            for g in range(4):
                ps, Bp = next_psa()
                self.mm(ps[0:64, 0:nt], wpw[0:64, g, 0:64], pooled[:, g, 0:nt], True, True, [Bwpw, Bpooled], [Bp])
                self.v("vector", "tensor_scalar", [Bp, Bpscale], [Bypool], out=ypool[:, g, 0:nt], in0=ps[0:64, 0:nt],
                       scalar1=pscale[:, l, g:g + 1], scalar2=None, op0=ALU.mult)
            CW = PC + S
            for j in range(2):
                ps, Bp = proj(l, ("cb", j), 128, nt)
                self.act(cbuf[:, j, 0:nt], ps[:, 0:nt], AF.Copy, [Bp], [Bcb])
            extcv = extc[:, :, 0:nseq * CW].rearrange("p j (b e) -> p j b e", e=CW)
            if prompt:
                self.v("gpsimd", "tensor_copy", [Bconvpref], [Bextc], out=extcv[:, :, 0, 0:PC], in_=convpref[:, l, :, :])
            else:
                P.dma("sync", sconv_st[:].rearrange("p j b e -> p (j b e)"), d_sconv[l], writes=[Bsconv_st])
                self.v("gpsimd", "tensor_copy", [Bsconv_st], [Bextc], out=extcv[:, :, :, 0:PC], in_=sconv_st[:])
            for j in range(2):
                ps, Bp = proj(l, ("cc", j), 128, nt)
                self.act(cctmp[:, 0:nt], ps[:, 0:nt], AF.Copy, [Bp], [Bcct])
                ps2, Bp2 = proj(l, ("cx", j), 128, nt)
                self.v("vector", "tensor_tensor", [Bp2, Bcct], [Bextc], out=extcv[:, j, :, PC:PC + S],
                       in0=ps2[:, 0:nt].rearrange("p (b s) -> p b s", s=S), in1=cctmp[:, 0:nt].rearrange("p (b s) -> p b s", s=S), op=ALU.mult)
            if prompt:
                self.v("gpsimd", "tensor_copy", [Bextc], [Bconvpref], out=convpref[:, l, :, :], in_=extcv[:, :, 0, S:S + PC])
                if ti == cfg.NTILE - 1:
                    P.dma(OQ, o_convp[l], convpref[:, l, :, :].rearrange("p j e -> p (j e)"), reads=[Bconvpref])
            else:
                self.v("gpsimd", "tensor_copy", [Bextc], [Bsconv_st], out=sconv_st[:], in_=extcv[:, :, :, S:S + PC])
                P.dma(OQ, o_convs[l], sconv_st[:].rearrange("p j b e -> p (j b e)"), reads=[Bsconv_st])
            for j in range(2):
                cv = ctmp[:, 0:nt].rearrange("p (b s) -> p b s", s=S)
                self.v("vector", "tensor_scalar", [Bextc, Bconvw], [Bctmp], out=cv, in0=extcv[:, j, :, 0:S],
                       scalar1=convw[:, l, j, 0:1], scalar2=None, op0=ALU.mult)
                for k in (1, 2):
                    self.v("vector", "scalar_tensor_tensor", [Bextc, Bconvw, Bctmp], [Bctmp], out=cv, in0=extcv[:, j, :, k:k + S],
                           scalar=convw[:, l, j, k:k + 1], in1=cv, op0=ALU.mult, op1=ALU.add)
                self.v("vector", "tensor_tensor", [Bctmp, Bcb], [Byconv], out=yconv[:, j, 0:nt], in0=ctmp[:, 0:nt], in1=cbuf[:, j, 0:nt], op=ALU.mult)
            if prompt:
                t0 = ti * NT
                P.dma("sync", cosT[:, 0:nt], dC["cos_p"][:, t0:t0 + nt], writes=[Bcs])
                P.dma("sync", sinT[:, 0:nt], dC["sin_p"][:, t0:t0 + nt], writes=[Bcs])
                cos_ap, sin_ap, Bcos = cosT[:, 0:nt], sinT[:, 0:nt], Bcs
            else:
                cos_ap, sin_ap, Bcos = cos_s[:, 0:nt], sin_s[:, 0:nt], Bcs1
            for s in range(6):
                ps, Bp = proj(l, ("kv", s), 128, nt)
                if s in (2, 4):
                    rope_apply(ps[:, 0:nt], Bp, kvf[:, s, 0:nt], None, [Bkvf[s]], nt, cos_ap, sin_ap, Bcos)
                else:
                    self.act(kvf[:, s, 0:nt], ps[:, 0:nt], AF.Copy, [Bp], [Bkvf[s]])
            if prompt:
                t0 = ti * NT
                for s in range(4):
                    P.dma(OQ, o_kvp[l, s, :, t0:t0 + nt], kvf[:, s, 0:nt], reads=[Bkvf[s]])
                if t0 >= SEQ - 512:
                    w0 = t0 - (SEQ - 512)
                    for s in range(2):
                        P.dma(OQ, o_winp[l, s, :, w0:w0 + nt], kvf[:, 4 + s, 0:nt], reads=[Bkvf[4 + s]])
            else:
                for s in range(4):
                    P.dma(OQ, o_kvs[l, s], kvf[:, s, 0:nt], reads=[Bkvf[s]])
                for s in range(2):
                    P.dma(OQ, o_wins_new[l, s], kvf[:, 4 + s, 0:nt], reads=[Bkvf[4 + s]])
            for r in range(4):
                ps, Bp = proj(l, ("q", r), 128, nt)
                rope_apply(ps[:, 0:nt], Bp, None, (QTz[0][0:64, r, 0:nt], QTz[1][64:128, r, 0:nt]), [BQT], nt, cos_ap, sin_ap, Bcos)
            wg, Bwg = wslot(l, ("ng", 0))
            if prompt:
                for sub in range(NSUB):
                    for kc in range(8):
                        self.mm(PSM[:, 0:24], hT[:, kc, sub * 128:(sub + 1) * 128], wg[:, kc, 0:24], kc == 0, kc == 7, [BhT, Bwg], [BPSM])
                    self.act(gsig[:, sub, :], PSM[:, 0:24], AF.Sigmoid, [BPSM], [Bgsig])
            else:
                for b in range(NB):
                    for kc in range(8):
                        self.mm(PSM[0:4, b * 24:(b + 1) * 24], hT[:, kc, b * 4:(b + 1) * 4], wg[:, kc, 0:24], kc == 0, kc == 7, [BhT, Bwg], [BPSM])
                self.act(gsig[0:4, 0:NB, :], PSM[0:4, 0:NB * 24].rearrange("p (b c) -> p b c", c=24), AF.Sigmoid, [BPSM], [Bgsig])
            if cfg.nsa:
                if prompt:
                    nsa_prompt(l, ti)
                else:
                    nsa_sample(l)
            else:
                self.v("gpsimd", "memset", [], [Bynsa], ynsa[:], 0.0)
            for m in range(8):
                for bi, gk in enumerate(("gp", "gn", "gc")):
                    ps, Bp = proj(l, (gk, m), 128, nt)
                    sg, Bsg = sgate[bi % 2], Bsgate[bi % 2]
                    self.act(sg[:, 0:nt], ps[:, 0:nt], AF.Sigmoid, [Bp], [Bsg])
                    ps2, Bp2 = next_psa()
                    if bi == 0:
                        wb, Bwb = wslot(l, ("brp", m))
                        for g in range(4):
                            self.mm(ps2[:, 0:nt], wb[0:64, g, :], ypool[:, g, 0:nt], g == 0, g == 3, [Bwb, Bypool], [Bp2])
                        self.v("vector", "tensor_tensor", [Bp2, Bsg], [Bmacc], out=macc[:, 0:nt], in0=ps2[:, 0:nt], in1=sg[:, 0:nt], op=ALU.mult)
                    elif bi == 1:
                        wb, Bwb = wslot(l, ("brnc", m))
                        for k in range(4):
                            self.mm(ps2[:, 0:nt], wb[:, k, :], ynsa[:, k, 0:nt], k == 0, k == 3, [Bwb, Bynsa], [Bp2])
                        self.v("vector", "tensor_tensor", [Bp2, Bsg], [Bmtmp], out=mtmp[:, 0:nt], in0=ps2[:, 0:nt], in1=sg[:, 0:nt], op=ALU.mult)
                        self.v("gpsimd", "tensor_tensor", [Bmtmp, Bmacc], [Bmacc], out=macc[:, 0:nt], in0=macc[:, 0:nt], in1=mtmp[:, 0:nt], op=ALU.add)
                    else:
                        for k in range(2):
                            self.mm(ps2[:, 0:nt], wb[:, 4 + k, :], yconv[:, k, 0:nt], k == 0, k == 1, [Bwb, Byconv], [Bp2])
                        self.v("vector", "tensor_tensor", [Bp2, Bsg], [Bmtmp], out=mtmp[:, 0:nt], in0=ps2[:, 0:nt], in1=sg[:, 0:nt], op=ALU.mult)
                        self.v("gpsimd", "tensor_tensor", [Bmtmp, Bmacc], [Bsq], out=sq[:, m, 0:nt], in0=macc[:, 0:nt], in1=mtmp[:, 0:nt], op=ALU.add)
            for m in range(8):
                wt, Bw = wslot(l, ("wo", m))
                ps, Bp = next_psa()
                for kc in range(8):
                    self.mm(ps[:, 0:nt], wt[:, kc, :], sq[:, kc, 0:nt], kc == 0, kc == 7, [Bw, Bsq], [Bp])
                self.v("vector", "tensor_tensor", [Bp, Bx], [Bx], out=x[:, m, 0:nt], in0=x[:, m, 0:nt], in1=ps[:, 0:nt], op=ALU.add)
            rmsnorm(lambda kc: gffn[:, l, kc:kc + 1], Bgffn, nt)
            AW = 2 + S
            if not prompt:
                P.dma("sync", sffn[:].rearrange("p f b e -> p (f b e)"), d_sffn[l], writes=[Bsffn])
            for f in range(22):
                ea, Bea = exta[f % 2], Bexta[f % 2]
                eav = ea[:, 0:nseq * AW].rearrange("p (b e) -> p b e", e=AW)
                if prompt:
                    self.v("gpsimd", "tensor_copy", [Bffnpref], [Bea], out=eav[:, 0, 0:2], in_=ffnpref[:, l, f, :])
                else:
                    self.v("gpsimd", "tensor_copy", [Bsffn], [Bea], out=eav[:, :, 0:2], in_=sffn[:, f, :, :])
                ps, Bp = proj(l, ("fa", f), 128, nt)
                self.act(eav[:, :, 2:2 + S], ps[:, 0:nt].rearrange("p (b s) -> p b s", s=S), AF.Copy, [Bp], [Bea])
                if prompt:
                    self.v("gpsimd", "tensor_copy", [Bea], [Bffnpref], out=ffnpref[:, l, f, :], in_=eav[:, 0, S:S + 2])
                else:
                    self.v("gpsimd", "tensor_copy", [Bea], [Bffnnew], out=ffnnew[:, f, :, :], in_=eav[:, :, S:S + 2])
                at, Bat = atmp[f % 2], Batmp[f % 2]
                av = at[:, 0:nt].rearrange("p (b s) -> p b s", s=S)
                self.v("vector", "tensor_scalar", [Bea, Bffnw], [Bat], out=av, in0=eav[:, :, 0:S], scalar1=ffnw[:, l, f, 0:1], scalar2=None, op0=ALU.mult)
                for k in (1, 2):
                    self.v("vector", "scalar_tensor_tensor", [Bea, Bffnw, Bat], [Bat], out=av, in0=eav[:, :, k:k + S],
                           scalar=ffnw[:, l, f, k:k + 1], in1=av, op0=ALU.mult, op1=ALU.add)
                self.act(at[:, 0:nt], at[:, 0:nt], AF.Silu, [Bat], [Bat])
                ps2, Bp2 = proj(l, ("fb", f), 128, nt)
                self.v("vector", "tensor_tensor", [Bp2, Bat], [BgT], out=gT[:, f, 0:nt], in0=ps2[:, 0:nt], in1=at[:, 0:nt], op=ALU.mult)
            if prompt:
                if ti == cfg.NTILE - 1:
                    P.dma(OQ, o_ffnp[l], ffnpref[:, l, :, :].rearrange("p f e -> p (f e)"), reads=[Bffnpref])
            else:
                P.dma(OQ, o_ffns[l], ffnnew[:].rearrange("p f b e -> p (f b e)"), reads=[Bffnnew])
            for m in range(8):
                ps, Bp = next_psa()
                for part in range(3):
                    wt, Bw = wslot(l, ("fd", m, part))
                    kcs = list(range(part * 8, min(22, part * 8 + 8)))
                    for i, kc in enumerate(kcs):
                        self.mm(ps[:, 0:nt], wt[:, i, :], gT[:, kc, 0:nt], kc == 0, kc == 21, [Bw, BgT], [Bp])
                self.v("vector", "tensor_tensor", [Bp, Bx], [Bx], out=x[:, m, 0:nt], in0=x[:, m, 0:nt], in1=ps[:, 0:nt], op=ALU.add)

        def compress(l, src_fn, col0, nblk, Bsrc, kc_dst_fn, vc_dst_fn, Bkcd, Bvcd, cosc, sinc, Bcc):
            for s in range(2):
                w1 = [wslot(l, ("w1", s, q4)) for q4 in range(4)]
                w2, Bw2 = wslot(l, ("w2", 0))
                for li in range(32):
                    wt, Bw = w1[li // 8]
                    self.mm(PSM[:, 0:1], wt[0:64, li % 8, :], peT[:, l, s, li:li + 1], li == 0, li == 31, [Bw, BpeT], [BPSM])
                self.act(cvec[:, s:s + 1], PSM[:, 0:1], AF.Copy, [BPSM], [Bcvec])
                if cfg.nsa_sub < 2:
                    continue
                ps, Bp = next_psa()
                for g in range(2):
                    if cfg.nsa_var == 1 and g == 1:
                        continue
                    for li in range(32):
                        wt, Bw = w1[li // 8]
                        rhs_ = src_fn(s, g, li)
                        if cfg.nsa_var == 2:
                            rhs_ = cgb[g * 64:(g + 1) * 64, 0, 0:nblk]
                        self.mm(ps[:, g * 128:g * 128 + nblk], wt[:, li % 8, :], rhs_, li == 0, li == 31,
                                [Bw, Bsrc], [Bp])
                pv = ps[:, 0:256].rearrange("p (g n) -> p g n", n=128)[:, :, 0:nblk]
                xg = cgel[:, :, 0:nblk]
                x2 = cgel2[:, :, 0:nblk]
                if cfg.nsa_var == 3:
                    self.act(xg, pv, AF.Copy, [Bp, Bcvec], [Bcgel])
                else:
                    self.act(xg, pv, AF.Identity, [Bp, Bcvec], [Bcgel], bias=cvec[:, s:s + 1])
                self.act(x2, xg, AF.Square, [Bcgel], [Bcgel2])
                self.v("vector", "tensor_scalar", [Bcgel2], [Bcgel2], out=x2, in0=x2, scalar1=0.044715, scalar2=1.0, op0=ALU.mult, op1=ALU.add)
                self.v("vector", "tensor_tensor", [Bcgel2, Bcgel], [Bcgel2], out=x2, in0=x2, in1=xg, op=ALU.mult)
                self.act(x2, x2, AF.Sigmoid, [Bcgel2], [Bcgel2], scale=1.5957691216057308)
                self.v("vector", "tensor_tensor", [Bcgel2, Bcgel], [Bcgb], out=cgb[:, :, 0:nblk], in0=x2, in1=xg, op=ALU.mult)
                if cfg.nsa_sub < 3:
                    continue
                for g in range(2):
                    r = slice(g * 64, (g + 1) * 64)
                    self.mm(PST[:, g * 128:g * 128 + nblk], w2[:, s, :], cgb[:, g, 0:nblk], True, True, [Bw2, Bcgb], [BPST])
                    if s == 0:
                        rope_apply(PST[r, g * 128:g * 128 + nblk], BPST, None, kc_dst_fn(g), [Bkcd], nblk,
                                   cosc[r, col0:col0 + nblk], sinc[r, col0:col0 + nblk], Bcc, rows=r)
                    else:
                        self.act(vc_dst_fn(g), PST[r, g * 128:g * 128 + nblk], AF.Copy, [BPST], [Bvcd])

        def attend(q_rhs, nq, key_tiles, BQ):
            pso, Bpso = next_pso()
            nkt = len(key_tiles)
            pts = []
            for i, kt in enumerate(key_tiles):
                pss, Bpss = next_pss()
                M = kt["M"]
                nm = len(kt["masks"])
                self.mm(pss[0:M, 0:nq], kt["k"], q_rhs, True, nm == 0, [kt["Bk"], BQ], [Bpss])
                for mi, (ml, mr, mreads) in enumerate(kt["masks"]):
                    self.mm(pss[0:M, 0:nq], ml, mr, False, mi == nm - 1, mreads, [Bpss])
                pt, Bpt = next_pt()
                self.act(pt[0:M, 0:nq], pss[0:M, 0:nq], AF.Exp, [Bpss], [Bpt], scale=0.125)
                self.mm(pso[0:65, 0:nq], kt["v"], pt[0:M, 0:nq], i == 0, i == nkt - 1, [kt["Bv"], Bpt], [Bpso])
                pts.append((pt, Bpt, M))
            return pso, Bpso, pts

        def combine(pso, Bpso, nqtok, nr, branch, g, sub, first):
            ob, Bob = osb[self.osb_i % 2], Bosb[self.osb_i % 2]
            self.osb_i += 1
            nq = nr * nqtok
            self.act(ob[:, 0:nq], pso[0:65, 0:nq], AF.Copy, [Bpso], [Bob])
            for r in range(nr):
                self.tr(PST[0:nqtok, r * 65:(r + 1) * 65], ob[:, r * nqtok:(r + 1) * nqtok], identf[0:65, 0:65], [Bob, Bidentf], [BPST])
            tv = PST[0:nqtok, 0:nr * 65].rearrange("p (r e) -> p r e", e=65)
            self.v("vector", "tensor_scalar", [BPST], [Bsmal], out=smal[0:nqtok, 0:nr], in0=tv[:, :, 64], scalar1=1e-30, scalar2=None, op0=ALU.max)
            self.v("vector", "reciprocal", [Bsmal], [Bsmal], out=smal[0:nqtok, 0:nr], in_=smal[0:nqtok, 0:nr])
            c0 = branch * 8 + 4 * g
            self.v("vector", "tensor_tensor", [Bsmal, Bgsig], [Bsmal], out=smal[0:nqtok, 8:8 + nr], in0=smal[0:nqtok, 0:nr],
                   in1=gsig[0:nqtok, sub, c0:c0 + nr], op=ALU.mult)
            for r in range(nr):
                h = 4 * g + r
                if first:
                    self.v("vector", "tensor_scalar", [BPST, Bsmal], [Botok], out=otok[0:nqtok, h * 64:(h + 1) * 64], in0=tv[:, r, 0:64],
                           scalar1=smal[0:nqtok, 8 + r:9 + r], scalar2=None, op0=ALU.mult)
                else:
                    self.v("vector", "scalar_tensor_tensor", [BPST, Bsmal, Botok], [Botok], out=otok[0:nqtok, h * 64:(h + 1) * 64], in0=tv[:, r, 0:64],
                           scalar=smal[0:nqtok, 8 + r:9 + r], in1=otok[0:nqtok, h * 64:(h + 1) * 64], op0=ALU.mult, op1=ALU.add)

        def importance(pts, cover_fn, Bcov, nqtok, nr):
            first = True
            n = len(pts)
            for ci, (pt, Bpt, M) in enumerate(pts):
                for r in range(nr):
                    self.mm(PST[0:nqtok, r * 65:(r + 1) * 65], pt[0:M, r * nqtok:(r + 1) * nqtok], cover_fn(ci, M), first, (ci == n - 1) and (r == nr - 1),
                            [Bpt, Bcov], [BPST], skip=True)
                    first = False
            tv = PST[0:nqtok, 0:nr * 65].rearrange("p (r e) -> p r e", e=65)
            self.v("vector", "tensor_scalar", [BPST], [Bsmal], out=smal[0:nqtok, 16:16 + nr], in0=tv[:, :, 64], scalar1=1e-30, scalar2=None, op0=ALU.max)
            self.v("vector", "reciprocal", [Bsmal], [Bsmal], out=smal[0:nqtok, 16:16 + nr], in_=smal[0:nqtok, 16:16 + nr])
            for r in range(nr):
                if r == 0:
                    self.v("vector", "tensor_scalar", [BPST, Bsmal], [Bimp], out=imp[0:nqtok, :], in0=tv[:, r, 0:64], scalar1=smal[0:nqtok, 16:17], scalar2=None, op0=ALU.mult)
                else:
                    self.v("vector", "scalar_tensor_tensor", [BPST, Bsmal, Bimp], [Bimp], out=imp[0:nqtok, :], in0=tv[:, r, 0:64],
                           scalar=smal[0:nqtok, 16 + r:17 + r], in1=imp[0:nqtok, :], op0=ALU.mult, op1=ALU.add)

        def select_blocks(A_ap, B_ap, BAB, nqtok, nsel):
            self.v("vector", "tensor_tensor", [Bimp, BAB], [Bscore], out=score[0:nqtok, :], in0=imp[0:nqtok, :], in1=A_ap, op=ALU.mult)
            self.v("vector", "tensor_tensor", [Bscore, BAB], [Bscore], out=score[0:nqtok, :], in0=score[0:nqtok, :], in1=B_ap, op=ALU.add)
            if nsel > 16:
                self.v("vector", "max", [Bscore], [Bmx], out=mx[0:nqtok, 0:8], in_=score[0:nqtok, :])
                self.v("vector", "match_replace", [Bscore, Bmx], [Bscw], out=scw[0:nqtok, :], in_to_replace=mx[0:nqtok, 0:8], in_values=score[0:nqtok, :], imm_value=-1e30)
                self.v("vector", "max", [Bscw], [Bmx], out=mx[0:nqtok, 8:16], in_=scw[0:nqtok, :])
                self.v("vector", "tensor_scalar", [Bmx], [Bmx], out=mx[0:nqtok, 15:16], in0=mx[0:nqtok, 15:16], scalar1=0.0, scalar2=None, op0=ALU.max)
            else:
                self.v("vector", "memset", [], [Bmx], mx[0:nqtok, 15:16], 0.0)
            self.v("vector", "tensor_scalar", [Bscore, Bmx], [Bselb], out=selb[0:nqtok, 0:64], in0=score[0:nqtok, :], scalar1=mx[0:nqtok, 15:16], scalar2=None,
                   op0=ALU.is_lt)
            self.v("vector", "tensor_scalar", [Bselb], [Bselb], out=selb[0:nqtok, 0:64], in0=selb[0:nqtok, 0:64], scalar1=NEG, scalar2=None, op0=ALU.mult)
            self.tr(PSM[0:64, 0:nqtok], selb[0:nqtok, 0:64], identf[0:nqtok, 0:nqtok], [Bselb, Bidentf], [BPSM])
            self.act(selbT[:, 0:nqtok], PSM[0:64, 0:nqtok], AF.Copy, [BPSM], [BselbT])

        def nsa_prompt(l, ti):
            t0 = ti * NT
            if ti > 0:
                P.dma("sync", Kwk[:, 0:t0], hist_k[l, :, 0:t0], reads=[Bhk[l]], writes=[BKwk])
                P.dma("sync", Vwk[:, 0:ti * NSUB, :, :].rearrange("p k g e -> p (k g e)"), hist_v[l, :, 0:ti * NSUB * 130], reads=[Bhv[l]], writes=[BVwk])
            self.v("gpsimd", "tensor_copy", [Bkvf[2]], [BKwk], out=Kwk[:, t0:t0 + NT], in_=kvf[:, 2, :])
            wsl = ti % NWS
            self.v("gpsimd", "tensor_copy", [Bkvf[4]], [BKw[l]], out=KwT[l][:, wsl, :], in_=kvf[:, 4, :])
            for sub in range(NSUB):
                for (s, dst, Bd, idx) in ((3, Vwk, BVwk, ti * NSUB + sub), (5, Vw[l], BVw[l], wsl * NSUB + sub)):
                    self.tr(PST[:, 0:128], kvf[:, s, sub * 128:(sub + 1) * 128], identf[:], [Bkvf[s], Bidentf], [BPST])
                    self.act(dst[:, idx, :, 0:64], PST[:, 0:128].rearrange("p (g d) -> p g d", d=64), AF.Copy, [BPST], [Bd])
            if ti < cfg.NTILE - 1:
                P.dma(OQ, hist_k[l, :, t0:t0 + NT], Kwk[:, t0:t0 + NT], reads=[BKwk], writes=[Bhk[l]])
                P.dma(OQ, hist_v[l, :, ti * NSUB * 130:(ti + 1) * NSUB * 130],
                      Vwk[:, ti * NSUB:(ti + 1) * NSUB, :, :].rearrange("p k g e -> p (k g e)"), reads=[BVwk], writes=[Bhv[l]])
            if cfg.nsa_stage < 2:
                self.v("gpsimd", "memset", [], [Bynsa], ynsa[:], 0.0)
                return
            nb_t = NT // 16
            for g in range(2):
                r = slice(g * 64, (g + 1) * 64)
                self.v("gpsimd", "tensor_copy", [Bctail[l]], [Bcext], out=cext[r, g, :, 0:16], in_=ctail[l][r, :, :])
                for s in range(2):
                    self.v("gpsimd" if g else "vector", "tensor_copy", [Bkvf[s]], [Bcext], out=cext[r, g, s, 16:16 + NT], in_=kvf[r, s, :])
                self.v("gpsimd", "tensor_copy", [Bcext], [Bctail[l]], out=ctail[l][r, :, :], in_=cext[r, g, :, NT:NT + 16])
            c0 = nb_t * ti
            ch = c0 // 128

            def src_fn(s, g, li):
                a = cext[:, g, s, li:li + 1]
                return bass.AP(a.tensor, a.offset, [list(a.ap[0]), [16, nb_t]])

            compress(l, src_fn, c0, nb_t, Bcext,
                     lambda g: kcT[l][g * 64:(g + 1) * 64, c0:c0 + nb_t],
                     lambda g: vcT[l][g * 64:(g + 1) * 64, c0:c0 + nb_t],
                     Bkc[l], BvcT[l], cosc_p, sinc_p, Bcc1)
            if cfg.nsa_sub < 4:
                self.v("gpsimd", "memset", [], [Bynsa], ynsa[:], 0.0)
                return
            self.tr(PSTb[:, 0:128], vcT[l][:, ch * 128:(ch + 1) * 128], identb[:], [BvcT[l], Bidentb], [BPST])
            self.act(vcp[l][:, ch, :, 0:64], PSTb[:, 0:128].rearrange("p (g d) -> p g d", d=64), AF.Copy, [BPST], [Bvc[l]])
            nch = ch + 1
            if cfg.nsa_stage < 3:
                self.v("gpsimd", "memset", [], [Bynsa], ynsa[:], 0.0)
                return
            for sub in range(NSUB):
                qi = ti * NSUB + sub
                P.dma("sync", scAB[:, 0, :], dC["scA"][qi], writes=[BscAB])
                P.dma("sync", scAB[:, 1, :], dC["scB"][qi], writes=[BscAB])
                P.dma("gpsimd", cmpb[:], dC["cmpb"][qi], writes=[Bcmpb])
                for g in range(2):
                    r = slice(g * 64, (g + 1) * 64)
                    qa = QTz[g][:, :, sub * 128:(sub + 1) * 128]
                    kts = []
                    for c in range(nch):
                        kts.append(dict(k=kcT[l][:, c * 128:(c + 1) * 128], Bk=Bkc[l], M=128, v=vcp[l][:, c, g, :], Bv=Bvc[l],
                                        masks=[(identb[:], bc_mid(cmpb[:, c, :], 4), [Bidentb, Bcmpb])]))
                    pso, Bpso, pts = attend(qa, 512, kts, BQT)
                    importance(pts, lambda ci, M: cover_p[0:M, ci, :], Bcover_p, 128, 4)
                    combine(pso, Bpso, 128, 4, 0, g, sub, True)
                    if cfg.nsa_stage < 4:
                        continue
                    select_blocks(scAB[:, 0, :], scAB[:, 1, :], BscAB, 128, cfg.NSEL_P)
                    if cfg.nsa_stage < 5:
                        continue
                    kts = []
                    for kt in range(qi + 1):
                        masks = [(eexp[:, kt * 128:(kt + 1) * 128], bc_mid(selbT[:, 0:128], 4), [Beexp, BselbT])]
                        if kt == qi:
                            masks.append((identb[:], bc_mid(tri[:], 4), [Bidentb, Btri]))
                        kts.append(dict(k=Kwk[:, kt * 128:(kt + 1) * 128], Bk=BKwk, M=128, v=Vwk[:, kt, g, :], Bv=BVwk, masks=masks))
                    pso, Bpso, pts = attend(qa, 512, kts, BQT)
                    combine(pso, Bpso, 128, 4, 1, g, sub, False)
                    if cfg.nsa_stage < 6:
                        continue
                    kts = []
                    for kt in range(max(0, qi - 4), qi + 1):
                        masks = []
                        if kt == qi - 4:
                            masks.append((identb[:], bc_mid(winlo[:], 4), [Bidentb, Bwinlo]))
                        if kt == qi:
                            masks.append((identb[:], bc_mid(tri[:], 4), [Bidentb, Btri]))
                        ws_, wsub = (kt // NSUB) % NWS, kt % NSUB
                        kts.append(dict(k=KwT[l][:, ws_, wsub * 128:(wsub + 1) * 128], Bk=BKw[l], M=128, v=Vw[l][:, ws_ * NSUB + wsub, g, :], Bv=BVw[l], masks=masks))
                    pso, Bpso, pts = attend(qa, 512, kts, BQT)
                    combine(pso, Bpso, 128, 4, 2, g, sub, False)
                self.v("gpsimd", "tensor_copy", [Botok], [Botokb], out=otokb[:], in_=otok[:])
                for jj in range(4):
                    self.tr(PSMb[:, jj * 128:(jj + 1) * 128], otokb[:, jj * 128:(jj + 1) * 128], identb[:], [Botokb, Bidentb], [BPSM])
                self.act(ynsa[:, :, sub * 128:(sub + 1) * 128], PSMb[:, 0:512].rearrange("p (j q) -> p j q", q=128), AF.Copy, [BPSM], [Bynsa])

        TSP = cfg.PAST + 64
        NPG = cfg.NPG
        NKA = 1
        KA = [P.sb(f"KA{i}", [128, 5, TSP], BF16) for i in range(NKA)]
        BKA = [P.buf(f"KA{i}") for i in range(NKA)]
        VB = [P.sb(f"VB{i}", [128, NPG + 1, 2, 65], BF16) for i in range(NKA)]
        BVB = [P.buf(f"VB{i}") for i in range(NKA)]
        KWs = [P.sb(f"KWs{i}", [128, 512 + 4], BF16) for i in range(NKA)]
        BKWs = [P.buf(f"KWs{i}") for i in range(NKA)]
        VWs = [P.sb(f"VWs{i}", [128, 5, 2, 65], BF16) for i in range(NKA)]
        BVWs = [P.buf(f"VWs{i}") for i in range(NKA)]
        NSTG = 2
        stgA = [P.sb(f"stgA{i}", [128, 384], F32) for i in range(NSTG)]
        BstgA = [P.buf(f"stgA{i}") for i in range(NSTG)]
        stgB = [P.sb(f"stgB{i}", [128, 128], F32) for i in range(NSTG)]
        BstgB = [P.buf(f"stgB{i}") for i in range(NSTG)]
        stgW = P.sb("stgW", [128, 512], F32); BstgW = P.buf("stgW")
        stgV = P.sb("stgV", [128, 4, 128], F32); BstgV = P.buf("stgV")
        kcTs = P.sb("kcTs", [128, 128], BF16); BkcTs = P.buf("kcTs")
        vcTs = P.sb("vcTs", [128, 128], BF16); BvcTs = P.buf("vcTs")
        vcs = P.sb("vcs", [128, 2, 65], BF16); Bvcs = P.buf("vcs")
        vnew = P.sb("vnew", [4, 2, 2, 65], BF16); Bvnew = P.buf("vnew")
        idxf = P.sb("idxf", [128, NB * NPG], F32); Bidxf = P.buf("idxf")
        idxi = P.sb("idxi", [128, NB * NPG], I32); Bidxi = P.buf("idxi")
        pti = P.sb("pti", [128, NB * NPG], I32); Bpti = P.buf("pti")
        self.v("gpsimd", "memset", [], [Bvcs], vcs[:], 1.0)
        self.v("gpsimd", "memset", [], [BvcTs], vcTs[:], 0.0)
        self.v("gpsimd", "memset", [], [BkcTs], kcTs[:], 0.0)
        self.v("gpsimd", "memset", [], [Bvnew], vnew[:], 1.0)
        for i in range(NKA):
            self.v("gpsimd", "memset", [], [BKA[i]], KA[i][:], 0.0)
            self.v("gpsimd", "memset", [], [BVB[i]], VB[i][:], 1.0)
            self.v("gpsimd", "memset", [], [BVWs[i]], VWs[i][:], 1.0)
        self.stg_i = 0
        self.seq_i = 0

        def nsa_sample(l):
            if l > 0:
                self.v("vector", "tensor_scalar", [Bidxf], [Bidxf], out=idxf[:], in0=idxf[:], scalar1=float(cfg.NPOOL * 128), scalar2=None, op0=ALU.add)
            self.v("vector", "tensor_copy", [Bidxf], [Bidxi], out=idxi[:, :], in_=idxf[:])
            for b in range(NB):
                par = self.seq_i % NKA
                self.seq_i += 1
                ka, Bka, vb, Bvb = KA[par], BKA[par], VB[par], BVB[par]
                kw, Bkw, vw, Bvw = KWs[par], BKWs[par], VWs[par], BVWs[par]
                for pg in range(NPG):
                    si = self.stg_i % NSTG
                    self.stg_i += 1
                    col = b * NPG + pg
                    P.dma_fn("gpsimd", (lambda e, si=si, col=col: e.indirect_dma_start(
                        out=stgA[si][:, :], out_offset=None, in_=d_poolA[:, :],
                        in_offset=bass.IndirectOffsetOnAxis(ap=idxi[:, col:col + 1], axis=0))), [Bidxi], [BstgA[si]])
                    P.dma_fn("gpsimd", (lambda e, si=si, col=col: e.indirect_dma_start(
                        out=stgB[si][:, :], out_offset=None, in_=d_poolB[:, :],
                        in_offset=bass.IndirectOffsetOnAxis(ap=idxi[:, col:col + 1], axis=0))), [Bidxi], [BstgB[si]])
                    sv = stgA[si][:, :].rearrange("p (s t) -> p s t", t=128)
                    self.v("vector", "tensor_copy", [BstgA[si]], [Bka], out=ka[0:64, 0:2, pg * 128:(pg + 1) * 128], in_=sv[0:64, 0:2, :])
                    self.v("gpsimd", "tensor_copy", [BstgA[si]], [Bka], out=ka[64:128, 2:4, pg * 128:(pg + 1) * 128], in_=sv[64:128, 0:2, :])
                    self.v("vector", "tensor_copy", [BstgA[si]], [Bka], out=ka[:, 4, pg * 128:(pg + 1) * 128], in_=sv[:, 2, :])
                    self.act(vb[:, pg, :, 0:64], stgB[si][:, :].rearrange("p (g d) -> p g d", d=64), AF.Copy, [BstgB[si]], [Bvb])
                for s in range(2):
                    self.v("gpsimd", "tensor_copy", [Bkvf[s]], [Bka], out=ka[0:64, s, cfg.PAST:cfg.PAST + 4], in_=kvf[0:64, s, b * 4:(b + 1) * 4])
                    self.v("gpsimd", "tensor_copy", [Bkvf[s]], [Bka], out=ka[64:128, 2 + s, cfg.PAST:cfg.PAST + 4], in_=kvf[64:128, s, b * 4:(b + 1) * 4])
                self.v("gpsimd", "tensor_copy", [Bkvf[2]], [Bka], out=ka[:, 4, cfg.PAST:cfg.PAST + 4], in_=kvf[:, 2, b * 4:(b + 1) * 4])
                for (s, wi) in ((3, 0), (5, 1)):
                    self.tr(PST[0:4, 0:128], kvf[:, s, b * 4:(b + 1) * 4], identf[:], [Bkvf[s], Bidentf], [BPST])
                    self.act(vnew[:, wi, :, 0:64], PST[0:4, 0:128].rearrange("p (g d) -> p g d", d=64), AF.Copy, [BPST], [Bvnew])
                P.dma("sync", stgW[:], d_kwinT[l, b], writes=[BstgW])
                P.dma("sync", stgV[:], d_cwin[l, b].rearrange("(k p) c -> p k c", p=128)[:, :, 128:256], writes=[BstgV])
                self.v("vector", "tensor_copy", [BstgW], [Bkw], out=kw[:, 0:512], in_=stgW[:])
                self.v("gpsimd", "tensor_copy", [Bkvf[4]], [Bkw], out=kw[:, 512:516], in_=kvf[:, 4, b * 4:(b + 1) * 4])
                self.act(vw[:, 0:4, :, 0:64], stgV[:].rearrange("p k (g d) -> p k g d", d=64), AF.Copy, [BstgV], [Bvw])
                P.dma("sync", o_wins_old[l, b], d_cwin[l, b, 4:512, :])
                nblk = cfg.NCMP_S

                def src_fn(s, g, li, ka=ka):
                    a = ka[:, 2 * g + s, li:li + 1]
                    return bass.AP(a.tensor, a.offset, [list(a.ap[0]), [16, nblk]])

                compress(l, src_fn, 0, nblk, Bka,
                         lambda g: kcTs[g * 64:(g + 1) * 64, 0:nblk],
                         lambda g: vcTs[g * 64:(g + 1) * 64, 0:nblk],
                         BkcTs, BvcTs, cosc_s, sinc_s, Bcc3)
                self.tr(PSTb[:, 0:128], vcTs[:, :], identb[:], [BvcTs, Bidentb], [BPST])
                self.act(vcs[:, :, 0:64], PSTb[:, 0:128].rearrange("p (g d) -> p g d", d=64), AF.Copy, [BPST], [Bvcs])
                for g in range(2):
                    r = slice(g * 64, (g + 1) * 64)
                    qa = QTz[g][:, :, b * 4:(b + 1) * 4]
                    kts = [dict(k=kcTs[:, 0:nblk], Bk=BkcTs, M=nblk, v=vcs[0:nblk, g, :], Bv=Bvcs, masks=[])]
                    pso, Bpso, pts = attend(qa, 16, kts, BQT)
                    importance(pts, lambda ci, M: cover_s[0:M, :], Bcover_s, 4, 4)
                    combine(pso, Bpso, 4, 4, 0, g, b, True)
                    select_blocks(scA_s[:, :], scB_s[:, :], BscA_s, 4, cfg.NSEL_S)
                    kts = []
                    for kt in range(NPG):
                        kts.append(dict(k=ka[:, 4, kt * 128:(kt + 1) * 128], Bk=Bka, M=128, v=vb[:, kt, g, :], Bv=Bvb,
                                        masks=[(eexp[:, kt * 128:(kt + 1) * 128], bc_mid(selbT[:, 0:4], 4), [Beexp, BselbT])]))
                    kts.append(dict(k=ka[:, 4, cfg.PAST:cfg.PAST + 4], Bk=Bka, M=4, v=vnew[:, 0, g, :], Bv=Bvnew,
                                    masks=[(eexp[:, cfg.PAST:cfg.PAST + 4], bc_mid(selbT[:, 0:4], 4), [Beexp, BselbT]),
                                           (identb[0:4, 0:4], bc_mid(newtri[:, :], 4), [Bidentb, Bnewtri])]))
                    pso, Bpso, pts = attend(qa, 16, kts, BQT)
                    combine(pso, Bpso, 4, 4, 1, g, b, False)
                    kts = []
                    for kt in range(4):
                        masks = [(identb[:], bc_mid(winlo_s[:, :], 4), [Bidentb, Bwinlo_s])] if kt == 0 else []
                        kts.append(dict(k=kw[:, kt * 128:(kt + 1) * 128], Bk=Bkw, M=128, v=vw[:, kt, g, :], Bv=Bvw, masks=masks))
                    kts.append(dict(k=kw[:, 512:516], Bk=Bkw, M=4, v=vnew[:, 1, g, :], Bv=Bvnew,
                                    masks=[(identb[0:4, 0:4], bc_mid(newtri[:, :], 4), [Bidentb, Bnewtri])]))
                    pso, Bpso, pts = attend(qa, 16, kts, BQT)
                    combine(pso, Bpso, 4, 4, 2, g, b, False)
                self.v("gpsimd", "tensor_copy", [Botok], [Botokb], out=otokb[0:4, :], in_=otok[0:4, :])
                for jj in range(4):
                    self.tr(PSMb[:, jj * 4:(jj + 1) * 4], otokb[0:4, jj * 128:(jj + 1) * 128], identb[0:4, 0:4], [Botokb, Bidentb], [BPSM])
                self.act(ynsa[:, :, b * 4:(b + 1) * 4], PSMb[:, 0:16].rearrange("p (j q) -> p j q", q=4), AF.Copy, [BPSM], [Bynsa])

        P.dma("sync", pti[:], d_pt.partition_broadcast(128), writes=[Bpti])
        self.v("vector", "tensor_copy", [Bpti], [Bidxf], out=idxf[:], in_=pti[:])
        self.v("vector", "tensor_scalar", [Bidxf], [Bidxf], out=idxf[:], in0=idxf[:], scalar1=128.0, scalar2=None, op0=ALU.mult)
        self.v("vector", "tensor_scalar", [Bidxf, Biota], [Bidxf], out=idxf[:], in0=idxf[:], scalar1=iota_p[:, 0:1], scalar2=None, op0=ALU.add)

        xp_v = d_xp.rearrange("(k p) t -> p k t", p=128)
        yp_v = o_yp.rearrange("(k p) t -> p k t", p=128)
        for ti in range(cfg.ntile_run if cfg.do_prompt else 0):
            t0 = ti * NT
            P.dma("sync", x[:], xp_v[:, :, t0:t0 + NT], writes=[Bx])
            for l in range(L):
                layer_tile(l, NT, 1, NT, True, ti)
            final_out(NT, lambda kc, t0=t0: yp_v[:, kc, t0:t0 + NT])
        nts = NB * 4
        P.dma("sync", x[:, :, 0:nts], d_xs.rearrange("(k p) t -> p k t", p=128), writes=[Bx])
        for l in range(L if cfg.do_sample else 0):
            layer_tile(l, nts, NB, 4, False, 0)
        ys_v = o_ys.rearrange("(k p) t -> p k t", p=128)
        final_out(nts, lambda kc: ys_v[:, kc, :])
        return P.finalize()


_CACHE = {}


def get_builder(cfg, key):
    if key not in _CACHE:
        b = Builder(cfg)
        b.stats = b.build()
        _CACHE[key] = b
    return _CACHE[key]


def run(cfg, key, inputs, n_cores=8):
    bld = get_builder(cfg, key)
    L, NB = cfg.L, cfg.NB
    f32 = np.float32
    A = lambda a: np.ascontiguousarray(np.asarray(a))
    consts = make_consts(cfg)
    shared = {}
    shared["w_in"] = A(inputs["w_in"])
    shared["pool_w"] = A(inputs["pool_w"]).reshape(L, 256, 64)
    for s in range(2):
        shared[f"cmp_w1_{s}"] = A(np.asarray(inputs["cmp_w1"])[:, s])
        shared[f"cmp_w2_{s}"] = A(np.asarray(inputs["cmp_w2"])[:, s])
    for k in ("w_br_pool", "w_br_nsa", "w_br_conv", "w_out", "ffn_up", "ffn_down"):
        shared[k] = A(inputs[k])
    shared["gmix"] = A(np.asarray(inputs["norm_mix"]).reshape(L, 8, 128).transpose(2, 0, 1))
    shared["gffn"] = A(np.asarray(inputs["norm_ffn"]).reshape(L, 8, 128).transpose(2, 0, 1))
    shared["gfin"] = A(np.asarray(inputs["norm_final"]).reshape(8, 128).transpose(1, 0))
    shared["pscale"] = A(np.asarray(inputs["pool_scale"]).reshape(L, 4, 64).transpose(2, 0, 1))
    shared["convw"] = A(np.asarray(inputs["conv_w"]).reshape(L, 3, 2, 128).transpose(3, 0, 2, 1))
    shared["ffnw"] = A(np.asarray(inputs["ffn_conv"]).reshape(L, 3, 22, 128).transpose(3, 0, 2, 1))
    shared["peT"] = A(np.asarray(inputs["cmp_pe"]).transpose(3, 0, 1, 2))
    for k, a in consts.items():
        shared["c_" + k] = A(a.astype(f32))
    ckv = np.asarray(inputs["cache_kv"])
    npool = ckv.shape[1]
    shared["poolA"] = A(ckv[:, :, :, 0:3].transpose(0, 1, 4, 5, 3, 2)).reshape(L * npool * 128, 384)
    shared["poolB"] = A(ckv[:, :, :, 3]).reshape(L * npool * 128, 128)
    xp = np.asarray(inputs["x_prompt"])
    xs = np.asarray(inputs["x_sample"])
    cw = np.asarray(inputs["cache_win"])
    sp = np.asarray(inputs["state_pool"])
    sc = np.asarray(inputs["state_conv"])
    sf = np.asarray(inputs["state_ffn"])
    pt = np.asarray(inputs["page_table"]).astype(np.int32)
    in_maps = []
    for c in range(n_cores):
        m = dict(shared)
        m["xpT"] = A(xp[c % xp.shape[0]].T)
        sl = slice(c * NB, (c + 1) * NB)
        m["xsT"] = A(xs[sl].reshape(NB * 4, 1024).T)
        m["ptab"] = A(pt[sl].reshape(1, -1))
        m["cwin"] = A(cw[:, sl].reshape(L, NB, 512, 256))
        m["kwinT"] = A(cw[:, sl, :, 0].reshape(L, NB, 512, 128).transpose(0, 1, 3, 2))
        m["spoolT"] = A(sp[:, sl].reshape(L, NB, 15, 4, 64).transpose(0, 4, 3, 1, 2)).reshape(L, 64, -1)
        m["sconvT"] = A(sc[:, sl].reshape(L, NB, 2, 2, 128).transpose(0, 4, 3, 1, 2)).reshape(L, 128, -1)
        m["sffnT"] = A(sf[:, sl].reshape(L, NB, 2, 22, 128).transpose(0, 4, 3, 1, 2)).reshape(L, 128, -1)
        in_maps.append(m)
    res = run_bass_kernel_spmd(bld.nc, in_maps, core_ids=list(range(n_cores)))
    R = res.results
    nbp = xp.shape[0]
    SEQ = cfg.SEQ
    y_prompt = np.stack([R[c]["o_yp"].T for c in range(nbp)])
    y_sample = np.concatenate([R[c]["o_ys"].T.reshape(NB, 4, 1024) for c in range(n_cores)])
    kv_prompt = np.stack([R[c]["o_kvp"].reshape(L, 4, 2, 64, SEQ).transpose(0, 4, 1, 2, 3) for c in range(nbp)], axis=1)
    kv_sample = np.concatenate([R[c]["o_kvs"].reshape(L, 4, 2, 64, NB, 4).transpose(0, 4, 5, 1, 2, 3) for c in range(n_cores)], axis=1)
    win_prompt = np.stack([R[c]["o_winp"].reshape(L, 2, 2, 64, 512).transpose(0, 4, 1, 2, 3) for c in range(nbp)], axis=1)
    wn = [R[c]["o_wins_new"].reshape(L, 2, 2, 64, NB, 4).transpose(0, 4, 5, 1, 2, 3) for c in range(n_cores)]
    wo = [R[c]["o_wins_old"].reshape(L, NB, 508, 2, 2, 64) for c in range(n_cores)]
    win_sample = np.concatenate([np.concatenate([wo[c], wn[c]], axis=2) for c in range(n_cores)], axis=1)
    pool_prompt = np.stack([R[c]["o_poolp"].reshape(L, 64, 4, 15).transpose(0, 3, 2, 1).reshape(L, 15, 256) for c in range(nbp)], axis=1)
    pool_sample = np.concatenate([R[c]["o_pools"].reshape(L, 64, 4, NB, 15).transpose(0, 3, 4, 2, 1).reshape(L, NB, 15, 256) for c in range(n_cores)], axis=1)
    conv_prompt = np.stack([R[c]["o_convp"].reshape(L, 128, 2, 2).transpose(0, 3, 2, 1).reshape(L, 2, 256) for c in range(nbp)], axis=1)
    conv_sample = np.concatenate([R[c]["o_convs"].reshape(L, 128, 2, NB, 2).transpose(0, 3, 4, 2, 1).reshape(L, NB, 2, 256) for c in range(n_cores)], axis=1)
    ffn_prompt = np.stack([R[c]["o_ffnp"].reshape(L, 128, 22, 2).transpose(0, 3, 2, 1).reshape(L, 2, 2816) for c in range(nbp)], axis=1)
    ffn_sample = np.concatenate([R[c]["o_ffns"].reshape(L, 128, 22, NB, 2).transpose(0, 3, 4, 2, 1).reshape(L, NB, 2, 2816) for c in range(n_cores)], axis=1)
    outs = (y_prompt, y_sample, kv_prompt, kv_sample, win_prompt, win_sample, pool_prompt, pool_sample,
            conv_prompt, conv_sample, ffn_prompt, ffn_sample)
    return tuple(np.ascontiguousarray(o.astype(np.float32)) for o in outs)


def kernel(**inputs):
    return run(FULL, "full", inputs)
```

```python
import numpy as np
import ml_dtypes
from contextlib import ExitStack
import concourse.bass as bass
import concourse.mybir as mybir
from concourse.bass_utils import run_bass_kernel_spmd

F32 = mybir.dt.float32
BF16 = mybir.dt.bfloat16
I32 = mybir.dt.int32
AF = mybir.ActivationFunctionType
ALU = mybir.AluOpType

SEM_EPOCH = 30000
N_DSEM = 80
N_DSEM_SW = 48
NEG = -30000.0


class Buf:
    __slots__ = ("name", "last_w", "readers")

    def __init__(self, name):
        self.name = name
        self.last_w = None
        self.readers = []


class Op:
    __slots__ = ("eng", "fn", "reads", "writes", "dma", "deps", "sig", "needed", "idx")

    def __init__(self, eng, fn, reads, writes, dma):
        self.eng = eng
        self.fn = fn
        self.reads = reads
        self.writes = writes
        self.dma = dma
        self.deps = []
        self.sig = None
        self.needed = False


class Prog:
    ENGS = ("tensor", "vector", "scalar", "gpsimd", "sync")

    def __init__(self, nc):
        self.nc = nc
        self.ops = []
        self.stack = ExitStack()
        self.nbuf = 0

    def buf(self, name="b"):
        self.nbuf += 1
        return Buf(name)

    def sb(self, name, shape, dt):
        return self.stack.enter_context(self.nc.sbuf_tensor("s_" + name, list(shape), dt))

    def ps(self, name, shape, dt):
        return self.stack.enter_context(self.nc.psum_tensor("p_" + name, list(shape), dt))

    def op(self, eng, fn, reads=(), writes=()):
        o = Op(eng, fn, [b for b in reads if b is not None], [b for b in writes if b is not None], False)
        self.ops.append(o)
        return o

    def dma(self, q, out, in_, reads=(), writes=(), **kw):
        def fn(e):
            return e.dma_start(out=out, in_=in_, **kw)
        o = Op(q, fn, [b for b in reads if b is not None], [b for b in writes if b is not None], True)
        self.ops.append(o)
        return o

    def dma_fn(self, q, fn, reads=(), writes=()):
        o = Op(q, fn, [b for b in reads if b is not None], [b for b in writes if b is not None], True)
        self.ops.append(o)
        return o

    def finalize(self):
        nc = self.nc
        ops = self.ops
        for o in ops:
            deps = set()
            for b in o.reads:
                if b.last_w is not None:
                    deps.add(b.last_w)
            for b in o.writes:
                if b.last_w is not None:
                    deps.add(b.last_w)
                for r in b.readers:
                    deps.add(r)
            deps.discard(o)
            dl = []
            for d in deps:
                if (not o.dma) and (not d.dma) and o.eng == "tensor" and d.eng == "tensor":
                    continue
                d.needed = True
                dl.append(d)
            o.deps = dl
            for b in o.reads:
                b.readers.append(o)
            for b in o.writes:
                b.last_w = o
                b.readers = []
        cnt = {e: 0 for e in self.ENGS}
        epoch = {e: 0 for e in self.ENGS}
        dcount = [0] * N_DSEM
        qrange = {"gpsimd": (0, N_DSEM_SW), "sync": (N_DSEM_SW, N_DSEM)}
        qi = {q: 0 for q in qrange}
        for o in ops:
            if o.dma:
                lo, hi = qrange[o.eng]
                s = lo + qi[o.eng] % (hi - lo)
                qi[o.eng] += 1
                prev = dcount[s]
                dcount[s] += 16
                o.sig = ("d", s, dcount[s], prev)
            elif o.needed:
                e = o.eng
                if cnt[e] >= SEM_EPOCH:
                    epoch[e] += 1
                    cnt[e] = 0
                cnt[e] += 1
                o.sig = ("e", (e, epoch[e]), cnt[e])
        st = self.stack
        esem = {}
        for e in self.ENGS:
            for k in range(epoch[e] + 1):
                esem[(e, k)] = st.enter_context(nc.semaphore(f"s_{e}_{k}"))
        dsem = [st.enter_context(nc.semaphore(f"d_{k}")) for k in range(N_DSEM)]
        per_eng = {e: [] for e in self.ENGS}
        for o in ops:
            per_eng[o.eng].append(o)
        final_d = {}
        for o in ops:
            if o.dma:
                final_d[o.sig[1]] = o.sig[2]

        def emit(ename, e):
            seen = {}
            for o in per_eng[ename]:
                for d in o.deps:
                    if d.sig[0] == "d":
                        key, val, sem = ("d", d.sig[1]), d.sig[2], dsem[d.sig[1]]
                    else:
                        key, val, sem = d.sig[1], d.sig[2], esem[d.sig[1]]
                    if seen.get(key, 0) >= val:
                        continue
                    e.wait_ge(sem, val)
                    seen[key] = val
                if o.dma:
                    _, s, val, prev = o.sig
                    if prev > 0 and seen.get(("d", s), 0) < prev:
                        e.wait_ge(dsem[s], prev)
                        seen[("d", s)] = prev
                    o.fn(e).then_inc(dsem[s], 16)
                else:
                    ins = o.fn(e)
                    if o.sig is not None:
                        ins.then_inc(esem[o.sig[1]], 1)
            if ename == "sync":
                for s, val in final_d.items():
                    if seen.get(("d", s), 0) < val:
                        e.wait_ge(dsem[s], val)

        with nc.Block() as block:
            @block.tensor
            def _(e):
                emit("tensor", e)

            @block.vector
            def _(e):
                emit("vector", e)

            @block.scalar
            def _(e):
                emit("scalar", e)

            @block.gpsimd
            def _(e):
                emit("gpsimd", e)

            @block.sync
            def _(e):
                emit("sync", e)
        self.stack.close()
        return {e: len(per_eng[e]) for e in self.ENGS}


class Cfg:
    def __init__(self, SEQ=4096, L=4, NB=16, PAST=2048, NPOOL=2560, nsa=True, do_prompt=True, do_sample=True, ntile=None):
        self.D = 1024
        self.SEQ = SEQ
        self.L = L
        self.NB = NB
        self.PAST = PAST
        self.NPOOL = NPOOL
        self.NT = 256
        self.NTILE = SEQ // 256
        self.NPG = PAST // 128
        self.TS = PAST + 4
        self.NCMP_S = (self.TS - 32) // 16 + 1
        self.NSEL_S = -(-self.TS // 64)
        self.NSEL_P = SEQ // 64
        self.NCMP_P = (SEQ - 32) // 16 + 1
        self.NCH_P = -(-(self.NCMP_P + 1) // 128)
        self.DFF = 2816
        self.NF = 22
        self.INW = 5400
        self.nsa = nsa
        self.do_prompt = do_prompt
        self.do_sample = do_sample
        self.ntile_run = ntile if ntile is not None else self.NTILE
        import os
        self.nsa_stage = int(os.environ.get('NSA_STAGE', '99'))
        self.nsa_sub = int(os.environ.get('NSA_SUB', '99'))
        self.nsa_var = int(os.environ.get('NSA_VAR', '0'))


FULL = Cfg()

OFF_POOL = 0
OFF_Q = 256
OFF_KV = 768
OFF_NG = 1536
OFF_CB = 1560
OFF_CC = 1816
OFF_CX = 2072
OFF_GP = 2328
OFF_GN = 3352
OFF_GC = 4376


def slot_plan(cfg):
    slots = []
    names = {}

    def full_k(wname, cols, key, K=1024):
        pieces = []
        m0 = 0
        for (c0, n) in cols:
            for kc in range(K // 128):
                pieces.append((wname, kc * 128, 128, 0, kc, c0, n, m0))
            m0 += n
        names[key] = len(slots)
        slots.append(pieces)

    for g in range(4):
        full_k("w_in", [(OFF_POOL + 64 * g, 64)], ("pool", g))
    for j in range(2):
        full_k("w_in", [(OFF_CB + 128 * j, 128)], ("cb", j))
    for j in range(2):
        full_k("w_in", [(OFF_CC + 128 * j, 128)], ("cc", j))
    for j in range(2):
        full_k("w_in", [(OFF_CX + 128 * j, 128)], ("cx", j))
    for s in range(6):
        full_k("w_in", [(OFF_KV + 128 * s, 128)], ("kv", s))
    for r in range(4):
        full_k("w_in", [(OFF_Q + 64 * r, 64), (OFF_Q + 64 * (4 + r), 64)], ("q", r))
    full_k("w_in", [(OFF_NG, 24)], ("ng", 0))
    names[("pool_w", 0)] = len(slots)
    slots.append([("pool_w", g * 64, 64, 0, g, 0, 64, 0) for g in range(4)])
    for s in range(2):
        for q4 in range(4):
            pieces = []
            for l8 in range(8):
                l = q4 * 8 + l8
                for dup in range(2):
                    pieces.append((f"cmp_w1_{s}", l * 64, 64, dup * 64, l8, 0, 128, 0))
            names[("w1", s, q4)] = len(slots)
            slots.append(pieces)
    names[("w2", 0)] = len(slots)
    slots.append([(f"cmp_w2_{s}", 0, 128, 0, s, 0, 64, 0) for s in range(2)]
                 + [(f"cmp_w2_{s}", 0, 128, 0, s, 0, 64, 64) for s in range(2)])
    for m in range(8):
        full_k("w_in", [(OFF_GP + 128 * m, 128)], ("gp", m))
        names[("brp", m)] = len(slots)
        slots.append([("w_br_pool", g * 64, 64, 0, g, 128 * m, 128, 0) for g in range(4)])
        full_k("w_in", [(OFF_GN + 128 * m, 128)], ("gn", m))
        names[("brnc", m)] = len(slots)
        slots.append([("w_br_nsa", k * 128, 128, 0, k, 128 * m, 128, 0) for k in range(4)]
                     + [("w_br_conv", k * 128, 128, 0, 4 + k, 128 * m, 128, 0) for k in range(2)])
        full_k("w_in", [(OFF_GC + 128 * m, 128)], ("gc", m))
    for m in range(8):
        full_k("w_out", [(128 * m, 128)], ("wo", m))
    for f in range(22):
        full_k("ffn_up", [(128 * f, 128)], ("fa", f))
        full_k("ffn_up", [(2816 + 128 * f, 128)], ("fb", f))
    for m in range(8):
        for part in range(3):
            kcs = list(range(part * 8, min(22, part * 8 + 8)))
            names[("fd", m, part)] = len(slots)
            slots.append([("ffn_down", kc * 128, 128, 0, i, 128 * m, 128, 0) for i, kc in enumerate(kcs)])
    return slots, names


def make_consts(cfg):
    c = {}
    half = 32
    inv = (10000.0 ** (-np.arange(half, dtype=np.float32) / half)).astype(np.float32)

    def cs_tab(pos):
        ang = pos.astype(np.float32)[None, :] * inv[:, None]
        cos = np.cos(ang).astype(np.float32)
        sin = np.sin(ang).astype(np.float32)
        return np.tile(cos, (4, 1)), np.tile(sin, (4, 1))

    c["cos_p"], c["sin_p"] = cs_tab(np.arange(cfg.SEQ))
    ps = cfg.PAST + np.arange(4)
    cs, sn = cs_tab(ps)
    c["cos_s"] = np.tile(cs, (1, cfg.NB)).astype(np.float32)
    c["sin_s"] = np.tile(sn, (1, cfg.NB)).astype(np.float32)
    ncolp = cfg.NCH_P * 128
    c["cosc_p"], c["sinc_p"] = cs_tab(16 * (np.arange(ncolp) - 1) + 31)
    c["cosc_s"], c["sinc_s"] = cs_tab(16 * np.arange(128) + 31)
    R = np.zeros((128, 128), np.float32)
    for hh in range(2):
        for d in range(64):
            m = hh * 64 + d
            if d < 32:
                R[hh * 64 + d + 32, m] = -1.0
            else:
                R[hh * 64 + d - 32, m] = 1.0
    c["rot"] = R
    c["ident"] = np.eye(128, dtype=np.float32)
    c["iota_p"] = np.arange(128, dtype=np.float32).reshape(128, 1)
    nk = max(cfg.SEQ, cfg.NSEL_S * 64)
    e = np.zeros((64, nk), np.float32)
    for j in range(64):
        e[j, j * 64:(j + 1) * 64] = 1.0
    c["eexp"] = e
    k = np.arange(128)[:, None]
    q = np.arange(128)[None, :]
    c["tri"] = np.where(k <= q, 0.0, NEG).astype(np.float32)
    c["winlo"] = np.where(k >= q, 0.0, NEG).astype(np.float32)
    nq = cfg.SEQ // 128
    cmpb = np.zeros((nq, 128, cfg.NCH_P, 128), np.float32)
    scA = np.zeros((nq, 128, 64), np.float32)
    scB = np.zeros((nq, 128, 64), np.float32)
    for qi in range(nq):
        pos = qi * 128 + np.arange(128)
        for ch in range(cfg.NCH_P):
            col = ch * 128 + np.arange(128)
            ok = (16 * col[:, None] + 15 <= pos[None, :]) & (col[:, None] >= 1) & (col[:, None] <= cfg.NCMP_P)
            cmpb[qi, :, ch, :] = np.where(ok, 0.0, NEG)
        jj = np.arange(64)[None, :]
        cur = (pos // 64)[:, None]
        causal = (jj * 64 <= pos[:, None]) & (jj < cfg.NSEL_P)
        forced = ((jj == 0) | ((jj <= cur) & (jj > cur - 2))) & (jj < cfg.NSEL_P)
        scA[qi] = np.where(forced, 0.0, np.where(causal, 1.0, 0.0))
        scB[qi] = np.where(forced, 1e4, np.where(causal, 0.0, -1.0))
    c["cmpb"] = cmpb
    c["scA"] = scA
    c["scB"] = scB
    cov = np.zeros((128, cfg.NCH_P, 65), np.float32)
    for ch in range(cfg.NCH_P):
        for cc in range(128):
            n = ch * 128 + cc - 1
            if n < 0 or n >= cfg.NCMP_P:
                continue
            cst = 16 * n
            for j in range(cfg.NSEL_P):
                if cst < j * 64 + 64 and cst + 32 > j * 64:
                    cov[cc, ch, j] = 1.0
            cov[cc, ch, 64] = 1.0
    c["cover_p"] = cov
    covs = np.zeros((128, 65), np.float32)
    for n in range(cfg.NCMP_S):
        cst = 16 * n
        for j in range(cfg.NSEL_S):
            if cst < j * 64 + 64 and cst + 32 > j * 64:
                covs[n, j] = 1.0
        covs[n, 64] = 1.0
    c["cover_s"] = covs
    cur = cfg.PAST // 64
    jj = np.arange(64)
    forced = (jj == 0) | ((jj <= cur) & (jj > cur - 2))
    exist = jj < cfg.NSEL_S
    A = np.where(forced | ~exist, 0.0, 1.0)
    B = np.where(~exist, -1.0, np.where(forced, 1e4, 0.0))
    c["scA_s"] = np.tile(A[None, :], (4, 1)).astype(np.float32)
    c["scB_s"] = np.tile(B[None, :], (4, 1)).astype(np.float32)
    kk = np.arange(4)[:, None]
    qq = np.arange(4)[None, :]
    c["newtri"] = np.where(kk <= qq, 0.0, NEG).astype(np.float32)
    c["winlo_s"] = np.where(np.arange(128)[:, None] >= qq, 0.0, NEG).astype(np.float32)
    rc = np.zeros((64, 4, 16), np.float32)
    for g, w in enumerate((2, 4, 8, 16)):
        rc[:, g, :] = 1.0 / np.minimum(float(w), np.arange(16) + 1.0)[None, :]
    c["rcnt"] = rc
    return c


CONST_BF = ("ident", "eexp", "tri", "winlo", "cmpb", "cover_p", "cover_s", "newtri", "winlo_s")


def bc_mid(ap, r):
    a = [list(x) for x in ap.ap]
    return bass.AP(ap.tensor, ap.offset, [a[0], [0, r]] + a[1:])


class Builder:
    def __init__(self, cfg):
        self.cfg = cfg
        self.nc = bass.Bass("TRN2", target_bir_lowering=False)
        self.P = Prog(self.nc)
        self.din = {}
        self.dout = {}
        self.slots, self.sname = slot_plan(cfg)
        self.NS = len(self.slots)
        self.sdims = [(max(p[3] + p[2] for p in ps), max(p[4] for p in ps) + 1, max(p[7] + p[6] for p in ps)) for ps in self.slots]

    def inp(self, name, shape, dt=F32):
        t = self.nc.dram_tensor(name, list(shape), dt, kind="ExternalInput").ap()
        self.din[name] = (tuple(shape), dt)
        return t

    def outp(self, name, shape, dt=F32):
        t = self.nc.dram_tensor(name, list(shape), dt, kind="ExternalOutput").ap()
        self.dout[name] = (tuple(shape), dt)
        return t

    def mm(self, out, lhsT, rhs, start, stop, reads, writes, skip=False):
        kw = {}
        if skip:
            kw["skip_group_check"] = True
        self.P.op("tensor", lambda e: e.matmul(out, lhsT=lhsT, rhs=rhs, start=start, stop=stop, **kw), reads, writes)

    def tr(self, out, in_, ident, reads, writes):
        self.P.op("tensor", lambda e: e.transpose(out, in_, ident), reads, writes)

    def act(self, out, in_, func, reads, writes, **kw):
        self.P.op("scalar", lambda e: e.activation(out, in_, func, **kw), reads, writes)

    def v(self, eng, name, reads, writes, *a, **kw):
        self.P.op(eng, lambda e: getattr(e, name)(*a, **kw), reads, writes)

    def build(self):
        cfg = self.cfg
        P = self.P
        nc = self.nc
        L, NT, NB = cfg.L, cfg.NT, cfg.NB
        NS = self.NS
        SEQ = cfg.SEQ
        NKT = SEQ // 128
        NSUB = NT // 128
        W = {}
        W["w_in"] = self.inp("w_in", [L, 1024, 5400])
        W["pool_w"] = self.inp("pool_w", [L, 256, 64])
        for s in range(2):
            W[f"cmp_w1_{s}"] = self.inp(f"cmp_w1_{s}", [L, 2048, 128])
            W[f"cmp_w2_{s}"] = self.inp(f"cmp_w2_{s}", [L, 128, 64])
        W["w_br_pool"] = self.inp("w_br_pool", [L, 256, 1024])
        W["w_br_nsa"] = self.inp("w_br_nsa", [L, 512, 1024])
        W["w_br_conv"] = self.inp("w_br_conv", [L, 256, 1024])
        W["w_out"] = self.inp("w_out", [L, 1024, 1024])
        W["ffn_up"] = self.inp("ffn_up", [L, 1024, 5632])
        W["ffn_down"] = self.inp("ffn_down", [L, 2816, 1024])
        ws = nc.dram_tensor("ws", [L * NS, 128, 1024], BF16, kind="Internal").ap()
        hist_k = nc.dram_tensor("hist_k", [L, 128, SEQ], BF16, kind="Internal").ap()
        hist_v = nc.dram_tensor("hist_v", [L, 128, NKT * 130], BF16, kind="Internal").ap()
        d_gmix = self.inp("gmix", [128, L, 8])
        d_gffn = self.inp("gffn", [128, L, 8])
        d_gfin = self.inp("gfin", [128, 8])
        d_pscale = self.inp("pscale", [64, L, 4])
        d_convw = self.inp("convw", [128, L, 2, 3])
        d_ffnw = self.inp("ffnw", [128, L, 22, 3])
        d_peT = self.inp("peT", [64, L, 2, 32])
        consts = make_consts(cfg)
        dC = {}
        for k, a in consts.items():
            dC[k] = self.inp("c_" + k, a.shape, F32)
        d_xp = self.inp("xpT", [1024, SEQ])
        d_xs = self.inp("xsT", [1024, NB * 4])
        d_poolA = self.inp("poolA", [L * cfg.NPOOL * 128, 384])
        d_poolB = self.inp("poolB", [L * cfg.NPOOL * 128, 128])
        d_pt = self.inp("ptab", [1, NB * cfg.NPG], I32)
        d_cwin = self.inp("cwin", [L, NB, 512, 256])
        d_kwinT = self.inp("kwinT", [L, NB, 128, 512])
        d_spool = self.inp("spoolT", [L, 64, 4 * NB * 15])
        d_sconv = self.inp("sconvT", [L, 128, 2 * NB * 2])
        d_sffn = self.inp("sffnT", [L, 128, 22 * NB * 2])
        o_yp = self.outp("o_yp", [1024, SEQ])
        o_ys = self.outp("o_ys", [1024, NB * 4])
        o_kvp = self.outp("o_kvp", [L, 4, 128, SEQ])
        o_kvs = self.outp("o_kvs", [L, 4, 128, NB * 4])
        o_winp = self.outp("o_winp", [L, 2, 128, 512])
        o_wins_new = self.outp("o_wins_new", [L, 2, 128, NB * 4])
        o_wins_old = self.outp("o_wins_old", [L, NB, 508, 256])
        o_poolp = self.outp("o_poolp", [L, 64, 4 * 15])
        o_pools = self.outp("o_pools", [L, 64, 4 * NB * 15])
        o_convp = self.outp("o_convp", [L, 128, 2 * 2])
        o_convs = self.outp("o_convs", [L, 128, 2 * NB * 2])
        o_ffnp = self.outp("o_ffnp", [L, 128, 22 * 2])
        o_ffns = self.outp("o_ffns", [L, 128, 22 * NB * 2])

        OQ = "gpsimd"
        Bws = [[P.buf(f"ws{l}_{si}") for si in range(NS)] for l in range(L)]

        def emit_casts(l):
            for si, pieces in enumerate(self.slots):
                groups = {}
                for (wn, r0, nr, p0, kc, c0, ncol, m0) in pieces:
                    groups.setdefault((wn, nr, p0, c0, ncol, m0), []).append((r0, kc))
                for (wn, nr, p0, c0, ncol, m0), lst in groups.items():
                    lst.sort(key=lambda t: t[1])
                    r0s = [t[0] for t in lst]
                    kcs = [t[1] for t in lst]
                    n = len(lst)
                    step_r = (r0s[1] - r0s[0]) if n > 1 else 0
                    step_k = (kcs[1] - kcs[0]) if n > 1 else 1
                    assert all(r0s[i] - r0s[0] == i * step_r and kcs[i] - kcs[0] == i * step_k for i in range(n))
                    wt = W[wn]
                    rowlen = wt.shape[2]
                    src0 = wt[l, r0s[0]:r0s[0] + nr, c0:c0 + ncol]
                    src = bass.AP(src0.tensor, src0.offset, [[rowlen, nr], [step_r * rowlen, n], [1, ncol]])
                    ph, kh, mh = self.sdims[si]
                    dst0 = ws[l * NS + si, p0:p0 + nr, kcs[0] * mh + m0: kcs[0] * mh + m0 + ncol]
                    dst = bass.AP(dst0.tensor, dst0.offset, [[1024, nr], [step_k * mh, n], [1, ncol]])
                    P.dma("gpsimd", dst, src, writes=[Bws[l][si]])

        for l in range(L):
            emit_casts(l)

        def load_const(name, dram, shape, dt, buf=None):
            t = P.sb(name, shape, dt)
            b = buf if buf is not None else P.buf(name)
            if dt == F32:
                P.dma("sync", t[:], dram, writes=[b])
            else:
                P.dma("gpsimd", t[:], dram, writes=[b])
            return t, b

        gmix, Bgmix = load_const("gmix", d_gmix, [128, L, 8], F32)
        gffn, Bgffn = load_const("gffn", d_gffn, [128, L, 8], F32)
        gfin, Bgfin = load_const("gfin", d_gfin, [128, 8], F32)
        pscale, Bpscale = load_const("pscale", d_pscale, [64, L, 4], F32)
        convw, Bconvw = load_const("convw", d_convw, [128, L, 2, 3], F32)
        ffnw, Bffnw = load_const("ffnw", d_ffnw, [128, L, 22, 3], F32)
        peT, BpeT = load_const("peT", d_peT, [64, L, 2, 32], BF16)
        rot, Brot = load_const("rot", dC["rot"], [128, 128], F32)
        identf, Bidentf = load_const("identf", dC["ident"], [128, 128], F32)
        identb, Bidentb = load_const("identb", dC["ident"], [128, 128], BF16)
        eexp, Beexp = load_const("eexp", dC["eexp"], list(consts["eexp"].shape), BF16)
        tri, Btri = load_const("tri", dC["tri"], [128, 128], BF16)
        winlo, Bwinlo = load_const("winlo", dC["winlo"], [128, 128], BF16)
        cover_p, Bcover_p = load_const("cover_p", dC["cover_p"], [128, cfg.NCH_P, 65], BF16)
        cover_s, Bcover_s = load_const("cover_s", dC["cover_s"], [128, 65], BF16)
        cosc_p, Bcc1 = load_const("cosc_p", dC["cosc_p"], [128, cfg.NCH_P * 128], F32)
        sinc_p, _ = load_const("sinc_p", dC["sinc_p"], [128, cfg.NCH_P * 128], F32, Bcc1)
        cosc_s, Bcc3 = load_const("cosc_s", dC["cosc_s"], [128, 128], F32)
        sinc_s, _ = load_const("sinc_s", dC["sinc_s"], [128, 128], F32, Bcc3)
        cos_s, Bcs1 = load_const("cos_s", dC["cos_s"], [128, NB * 4], F32)
        sin_s, _ = load_const("sin_s", dC["sin_s"], [128, NB * 4], F32, Bcs1)
        scA_s, BscA_s = load_const("scA_s", dC["scA_s"], [4, 64], F32)
        scB_s, _ = load_const("scB_s", dC["scB_s"], [4, 64], F32, BscA_s)
        newtri, Bnewtri = load_const("newtri", dC["newtri"], [4, 4], BF16)
        winlo_s, Bwinlo_s = load_const("winlo_s", dC["winlo_s"], [128, 4], BF16)
        rcnt, Brcnt = load_const("rcnt", dC["rcnt"], [64, 4, 16], F32)
        iota_p, Biota = load_const("iota_p", dC["iota_p"], [128, 1], F32)
        ones_b = P.sb("ones_b", [128, 128], BF16)
        Bones = P.buf("ones")
        self.v("vector", "memset", [], [Bones], ones_b[:], 1.0)

        x = P.sb("x", [128, 8, NT], F32); Bx = P.buf("x")
        hT = P.sb("hT", [128, 8, NT], BF16); BhT = P.buf("hT")
        sq = P.sb("sq", [128, 8, NT], BF16); Bsq = P.buf("sq")
        rstd = P.sb("rstd", [128, NT], F32); Brstd = P.buf("rstd")
        NRING = 8
        ring = [P.sb(f"wr{i}", [128, 8, 128], BF16) for i in range(NRING)]
        Bring = [P.buf(f"wr{i}") for i in range(NRING)]
        self.ring_i = 0
        PSA = [P.ps(f"psA{i}", [128, 512], F32) for i in range(2)]
        BPSA = [P.buf(f"psA{i}") for i in range(2)]
        PSS = [P.ps(f"psS{i}", [128, 512], F32) for i in range(2)]
        BPSS = [P.buf(f"psS{i}") for i in range(2)]
        PSO = [P.ps(f"psO{i}", [128, 512], F32) for i in range(2)]
        BPSO = [P.buf(f"psO{i}") for i in range(2)]
        PST = P.ps("psT", [128, 512], F32); BPST = P.buf("psT")
        PSM = P.ps("psM", [128, 512], F32); BPSM = P.buf("psM")
        PSMb = PSM[:].bitcast(BF16)
        PSTb = PST[:].bitcast(BF16)
        self.psa_i = 0
        self.pss_i = 0
        self.pso_i = 0

        def next_psa():
            i = self.psa_i % 2
            self.psa_i += 1
            return PSA[i], BPSA[i]

        def next_pss():
            i = self.pss_i % 2
            self.pss_i += 1
            return PSS[i], BPSS[i]

        def next_pso():
            i = self.pso_i % 2
            self.pso_i += 1
            return PSO[i], BPSO[i]

        def wslot(l, key):
            si = self.sname[key]
            i = self.ring_i % NRING
            self.ring_i += 1
            ph, kh, mh = self.sdims[si]
            flat = ring[i][:].rearrange("p k m -> p (k m)")
            P.dma("sync", flat[0:ph, 0:kh * mh], ws[l * NS + si, 0:ph, 0:kh * mh], reads=[Bws[l][si]], writes=[Bring[i]])
            return flat[:, 0:kh * mh].rearrange("p (k m) -> p k m", m=mh), Bring[i]

        EWMAX = max(NB * 19, 15 + NT)
        extp = P.sb("extp", [64, 4, EWMAX], F32); Bextp = P.buf("extp")
        ptmp = [P.sb(f"ptmp{i}", [64, EWMAX], F32) for i in range(2)]
        Bptmp = [P.buf(f"ptmp{i}") for i in range(2)]
        pooled = P.sb("pooled", [64, 4, NT], BF16); Bpooled = P.buf("pooled")
        ypool = P.sb("ypool", [64, 4, NT], BF16); Bypool = P.buf("ypool")
        cbuf = P.sb("cbuf", [128, 2, NT], F32); Bcb = P.buf("cb")
        cctmp = P.sb("cctmp", [128, NT], F32); Bcct = P.buf("cct")
        extc = P.sb("extc", [128, 2, NT + 2 * NB], F32); Bextc = P.buf("extc")
        ctmp = P.sb("ctmp", [128, NT], F32); Bctmp = P.buf("ctmp")
        yconv = P.sb("yconv", [128, 2, NT], BF16); Byconv = P.buf("yconv")
        kvf = P.sb("kvf", [128, 6, NT], F32); Bkvf = [P.buf(f"kvf{s}") for s in range(6)]
        ropet = P.sb("ropet", [128, 2, NT], F32); Bropet = P.buf("ropet")
        cosT = P.sb("cosT", [128, NT], F32); sinT = P.sb("sinT", [128, NT], F32); Bcs = P.buf("cossin")
        QTz = [P.sb(f"QTz{g}", [128, 4, NT], BF16) for g in range(2)]; BQT = P.buf("QT")
        for g in range(2):
            self.v("gpsimd", "memset", [], [BQT], QTz[g][:], 0.0)
        NG = max(NSUB, NB)
        gsig = P.sb("gsig", [128, NG, 24], F32); Bgsig = P.buf("gsig")
        ynsa = P.sb("ynsa", [128, 4, NT], BF16); Bynsa = P.buf("ynsa")
        macc = P.sb("macc", [128, NT], F32); Bmacc = P.buf("macc")
        ostg = [P.sb(f"ostg{i}", [128, NT], F32) for i in range(2)]
        Bostg = [P.buf(f"ostg{i}") for i in range(2)]
        sgate = [P.sb(f"sgate{i}", [128, NT], F32) for i in range(2)]
        Bsgate = [P.buf(f"sgate{i}") for i in range(2)]
        mtmp = P.sb("mtmp", [128, NT], F32); Bmtmp = P.buf("mtmp")
        exta = [P.sb(f"exta{i}", [128, NT + 2 * NB], F32) for i in range(2)]
        Bexta = [P.buf(f"exta{i}") for i in range(2)]
        atmp = [P.sb(f"atmp{i}", [128, NT], F32) for i in range(2)]
        Batmp = [P.buf(f"atmp{i}") for i in range(2)]
        gT = P.sb("gT", [128, 22, NT], BF16); BgT = P.buf("gT")
        poolpref = P.sb("poolpref", [64, L, 4, 15], F32); Bpoolpref = P.buf("poolpref")
        convpref = P.sb("convpref", [128, L, 2, 2], F32); Bconvpref = P.buf("convpref")
        ffnpref = P.sb("ffnpref", [128, L, 22, 2], F32); Bffnpref = P.buf("ffnpref")
        sffn = P.sb("sffn", [128, 22, NB, 2], F32); Bsffn = P.buf("sffn")
        ffnnew = P.sb("ffnnew", [128, 22, NB, 2], F32); Bffnnew = P.buf("ffnnew")
        spool_st = P.sb("spool_st", [64, 4, NB, 15], F32); Bspool_st = P.buf("spool_st")
        sconv_st = P.sb("sconv_st", [128, 2, NB, 2], F32); Bsconv_st = P.buf("sconv_st")
        self.v("gpsimd", "memset", [], [Bpoolpref], poolpref[:], 0.0)
        self.v("gpsimd", "memset", [], [Bconvpref], convpref[:], 0.0)
        self.v("gpsimd", "memset", [], [Bffnpref], ffnpref[:], 0.0)

        NWS = 512 // NT + 1
        NCC = cfg.NCH_P * 128
        TSP = cfg.PAST + 64
        NPG = cfg.NPG
        p_sizes = [SEQ, NKT * 130] + [NWS * NT, NWS * NSUB * 130, NCC, NCC, cfg.NCH_P * 130] * L + [4 * (16 + NT)]
        s_sizes = [5 * TSP, (NPG + 1) * 130, 516, 5 * 130, 1024, 1024]
        arena = P.sb("arena", [128, max(sum(p_sizes), sum(s_sizes))], BF16)
        self.ar_o = 0

        def carve(n):
            o = self.ar_o
            self.ar_o += n
            return arena[:, o:o + n]

        Kwk = carve(SEQ); BKwk = P.buf("Kwk")
        Vwk = carve(NKT * 130).rearrange("p (k g e) -> p k g e", g=2, e=65); BVwk = P.buf("Vwk")
        Bhk = [P.buf(f"hk{l}") for l in range(L)]
        Bhv = [P.buf(f"hv{l}") for l in range(L)]
        KwT, Vw, kcT, vcT, vcp = [], [], [], [], []
        for l in range(L):
            KwT.append(carve(NWS * NT).rearrange("p (w t) -> p w t", t=NT))
            Vw.append(carve(NWS * NSUB * 130).rearrange("p (k g e) -> p k g e", g=2, e=65))
            kcT.append(carve(NCC))
            vcT.append(carve(NCC))
            vcp.append(carve(cfg.NCH_P * 130).rearrange("p (k g e) -> p k g e", g=2, e=65))
        ctail = [P.sb(f"ctail{l}", [128, 2, 16], BF16) for l in range(L)]
        BKw = [P.buf() for l in range(L)]; BVw = [P.buf() for l in range(L)]
        Bkc = [P.buf() for l in range(L)]; BvcT = [P.buf() for l in range(L)]; Bvc = [P.buf() for l in range(L)]
        Bctail = [P.buf() for l in range(L)]
        self.v("gpsimd", "memset", [], [BKwk], Kwk[:], 0.0)
        self.v("gpsimd", "memset", [], [BVwk], Vwk[:], 0.0)
        self.v("gpsimd", "memset", [], [BVwk], Vwk[:, :, :, 64:65], 1.0)
        for l in range(L):
            self.v("gpsimd", "memset", [], [BKw[l]], KwT[l][:], 0.0)
            self.v("gpsimd", "memset", [], [BVw[l]], Vw[l][:], 0.0)
            self.v("gpsimd", "memset", [], [BVw[l]], Vw[l][:, :, :, 64:65], 1.0)
            self.v("gpsimd", "memset", [], [Bkc[l]], kcT[l][:], 0.0)
            self.v("gpsimd", "memset", [], [BvcT[l]], vcT[l][:], 0.0)
            self.v("gpsimd", "memset", [], [Bvc[l]], vcp[l][:], 0.0)
            self.v("gpsimd", "memset", [], [Bvc[l]], vcp[l][:, :, :, 64:65], 1.0)
            self.v("gpsimd", "memset", [], [Bctail[l]], ctail[l][:], 0.0)
        cext = carve(4 * (16 + NT)).rearrange("p (g s t) -> p g s t", g=2, s=2); Bcext = P.buf("cext")
        self.v("gpsimd", "memset", [], [Bcext], cext[:], 0.0)
        cgel = P.sb("cgel", [128, 2, 128], F32); Bcgel = P.buf("cgel")
        cgel2 = P.sb("cgel2", [128, 2, 128], F32); Bcgel2 = P.buf("cgel2")
        cgb = P.sb("cgb", [128, 2, 128], BF16); Bcgb = P.buf("cgb")
        cvec = P.sb("cvec", [128, 2], F32); Bcvec = P.buf("cvec")
        NPT = 4
        PT = [P.sb(f"PT{i}", [128, 512], BF16) for i in range(NPT)]
        BPT = [P.buf(f"PT{i}") for i in range(NPT)]
        self.pt_i = 0

        def next_pt():
            i = self.pt_i % NPT
            self.pt_i += 1
            return PT[i], BPT[i]

        osb = [P.sb(f"osb{i}", [65, 512], F32) for i in range(2)]
        Bosb = [P.buf(f"osb{i}") for i in range(2)]
        self.osb_i = 0
        otok = P.sb("otok", [128, 512], F32); Botok = P.buf("otok")
        otokb = P.sb("otokb", [128, 512], BF16); Botokb = P.buf("otokb")
        smal = P.sb("smal", [128, 64], F32); Bsmal = P.buf("smal")
        imp = P.sb("imp", [128, 64], F32); Bimp = P.buf("imp")
        score = P.sb("score", [128, 64], F32); Bscore = P.buf("score")
        scw = P.sb("scw", [128, 64], F32); Bscw = P.buf("scw")
        mx = P.sb("mx", [128, 16], F32); Bmx = P.buf("mx")
        selb = P.sb("selb", [128, 64], F32); Bselb = P.buf("selb")
        selbT = P.sb("selbT", [64, 128], BF16); BselbT = P.buf("selbT")
        scAB = P.sb("scAB", [128, 2, 64], F32); BscAB = P.buf("scAB")
        cmpb = P.sb("cmpb", [128, cfg.NCH_P, 128], BF16); Bcmpb = P.buf("cmpb")

        def rms_stats(nt):
            for kc in range(8):
                self.act(sq[:, kc, 0:nt], x[:, kc, 0:nt], AF.Square, [Bx], [Bsq])
            for kc in range(8):
                self.mm(PSM[:, 0:nt], ones_b[:], sq[:, kc, 0:nt], kc == 0, kc == 7, [Bones, Bsq], [BPSM])
            self.v("vector", "tensor_scalar", [BPSM], [Brstd], out=rstd[:, 0:nt], in0=PSM[:, 0:nt],
                   scalar1=1.0 / 1024.0, scalar2=1e-6, op0=ALU.mult, op1=ALU.add)
            self.act(rstd[:, 0:nt], rstd[:, 0:nt], AF.Sqrt, [Brstd], [Brstd])
            self.v("vector", "reciprocal", [Brstd], [Brstd], out=rstd[:, 0:nt], in_=rstd[:, 0:nt])

        def rmsnorm(g_ap_fn, Bg, nt):
            rms_stats(nt)
            for kc in range(8):
                self.v("vector", "scalar_tensor_tensor", [Bx, Brstd, Bg], [BhT], out=hT[:, kc, 0:nt], in0=x[:, kc, 0:nt],
                       scalar=g_ap_fn(kc), in1=rstd[:, 0:nt], op0=ALU.mult, op1=ALU.mult)

        def final_out(nt, dst_fn):
            rms_stats(nt)
            for kc in range(8):
                st, Bst = ostg[kc % 2], Bostg[kc % 2]
                self.v("vector", "scalar_tensor_tensor", [Bx, Brstd, Bgfin], [Bst], out=st[:, 0:nt], in0=x[:, kc, 0:nt],
                       scalar=gfin[:, kc:kc + 1], in1=rstd[:, 0:nt], op0=ALU.mult, op1=ALU.mult)
                P.dma(OQ, dst_fn(kc), st[:, 0:nt], reads=[Bst])

        def proj(l, key, M, nt, K=8):
            wt, Bw = wslot(l, key)
            ps, Bp = next_psa()
            for kc in range(K):
                self.mm(ps[0:M, 0:nt], wt[:, kc, 0:M], hT[:, kc, 0:nt], kc == 0, kc == K - 1, [Bw, BhT], [Bp])
            return ps, Bp

        def rope_apply(src_ps, Bsrc, dst_f32, dst_bf, Bdst_list, nt, cos_ap, sin_ap, Bcos, rows=slice(0, 128)):
            r = rows
            self.act(ropet[r, 0, 0:nt], src_ps, AF.Copy, [Bsrc], [Bropet])
            self.mm(PSM[r, 0:nt], rot[r, r], ropet[r, 0, 0:nt], True, True, [Brot, Bropet], [BPSM])
            self.v("vector", "tensor_tensor", [BPSM, Bcos], [Bropet], out=ropet[r, 1, 0:nt], in0=PSM[r, 0:nt], in1=sin_ap, op=ALU.mult)
            self.v("gpsimd", "tensor_tensor", [Bropet, Bcos], [Bropet], out=ropet[r, 0, 0:nt], in0=ropet[r, 0, 0:nt], in1=cos_ap, op=ALU.mult)
            if isinstance(dst_bf, tuple):
                for hh, d_ in enumerate(dst_bf):
                    rr = slice(hh * 64, (hh + 1) * 64)
                    self.v("vector", "tensor_tensor", [Bropet], Bdst_list, out=d_, in0=ropet[rr, 0, 0:nt], in1=ropet[rr, 1, 0:nt], op=ALU.add)
                return
            dst = dst_f32 if dst_f32 is not None else dst_bf
            self.v("vector", "tensor_tensor", [Bropet], Bdst_list, out=dst, in0=ropet[r, 0, 0:nt], in1=ropet[r, 1, 0:nt], op=ALU.add)

        def layer_tile(l, nt, nseq, S, prompt, ti):
            PP, PC = 15, 2
            rmsnorm(lambda kc: gmix[:, l, kc:kc + 1], Bgmix, nt)
            EW = PP + S
            extv = extp[:, :, 0:nseq * EW].rearrange("p g (b e) -> p g b e", e=EW)
            if prompt:
                self.v("gpsimd", "tensor_copy", [Bpoolpref], [Bextp], out=extv[:, :, 0, 0:PP], in_=poolpref[:, l, :, :])
            else:
                P.dma("sync", spool_st[:].rearrange("p g b e -> p (g b e)"), d_spool[l], writes=[Bspool_st])
                self.v("gpsimd", "tensor_copy", [Bspool_st], [Bextp], out=extv[:, :, :, 0:PP], in_=spool_st[:])
            for g in range(4):
                ps, Bp = proj(l, ("pool", g), 64, nt)
                self.act(extv[:, g, :, PP:PP + S], ps[0:64, 0:nt].rearrange("p (b s) -> p b s", s=S), AF.Copy, [Bp], [Bextp])
            if prompt:
                self.v("gpsimd", "tensor_copy", [Bextp], [Bpoolpref], out=poolpref[:, l, :, :], in_=extv[:, :, 0, S:S + PP])
                if ti == cfg.NTILE - 1:
                    P.dma(OQ, o_poolp[l], poolpref[:, l, :, :].rearrange("p g e -> p (g e)"), reads=[Bpoolpref])
            else:
                self.v("gpsimd", "tensor_copy", [Bextp], [Bspool_st], out=spool_st[:], in_=extv[:, :, :, S:S + PP])
                P.dma(OQ, o_pools[l], spool_st[:].rearrange("p g b e -> p (g b e)"), reads=[Bspool_st])
            for g in range(4):
                cur = extv[:, g, :, :]
                off = 0
                width = EW
                for step in range(g + 1):
                    sh = 1 << step
                    t_i = step % 2
                    nw = width - sh
                    dst = ptmp[t_i][:, 0:nseq * nw].rearrange("p (b e) -> p b e", e=nw)
                    self.v("vector", "tensor_tensor", [Bextp, Bptmp[1 - t_i]] if step else [Bextp], [Bptmp[t_i]],
                           out=dst, in0=cur[:, :, sh:width], in1=cur[:, :, 0:nw], op=ALU.add)
                    cur = dst
                    off += sh
                    width = nw
                w = 2 << g
                j0 = PP - off
                self.v("vector", "scalar_tensor_tensor", [Bptmp[g % 2], Bextp], [Bpooled],
                       out=pooled[:, g, 0:nt].rearrange("p (b s) -> p b s", s=S), in0=cur[:, :, j0:j0 + S], scalar=1.0 / w,
                       in1=extv[:, g, :, PP:PP + S], op0=ALU.mult, op1=ALU.subtract)
                if prompt and ti == 0:
                    self.v("vector", "tensor_tensor", [Bptmp[g % 2], Brcnt], [Bptmp[g % 2]], out=cur[:, 0, j0:j0 + 16],
                           in0=cur[:, 0, j0:j0 + 16], in1=rcnt[:, g, :], op=ALU.mult)
                    self.v("vector", "tensor_tensor", [Bptmp[g % 2], Bextp], [Bpooled], out=pooled[:, g, 0:16],
                           in0=cur[:, 0, j0:j0 + 16], in1=extv[:, g, 0, PP:PP + 16], op=ALU.subtract)
            wpw, Bwpw = wslot(l, ("pool_w", 0))
            for g in range(4):
                ps, Bp = next_psa()
                self.mm(ps[0:64, 0:nt], wpw[0:64, g, 0:64], pooled[:, g, 0:nt], True, True, [Bwpw, Bpooled], [Bp])
                self.v("vector", "tensor_scalar", [Bp, Bpscale], [Bypool], out=ypool[:, g, 0:nt], in0=ps[0:64, 0:nt],
                       scalar1=pscale[:, l, g:g + 1], scalar2=None, op0=ALU.mult)
            CW = PC + S
            for j in range(2):
                ps, Bp = proj(l, ("cb", j), 128, nt)
                self.act(cbuf[:, j, 0:nt], ps[:, 0:nt], AF.Copy, [Bp], [Bcb])
            extcv = extc[:, :, 0:nseq * CW].rearrange("p j (b e) -> p j b e", e=CW)
            if prompt:
                self.v("gpsimd", "tensor_copy", [Bconvpref], [Bextc], out=extcv[:, :, 0, 0:PC], in_=convpref[:, l, :, :])
            else:
                P.dma("sync", sconv_st[:].rearrange("p j b e -> p (j b e)"), d_sconv[l], writes=[Bsconv_st])
                self.v("gpsimd", "tensor_copy", [Bsconv_st], [Bextc], out=extcv[:, :, :, 0:PC], in_=sconv_st[:])
            for j in range(2):
                ps, Bp = proj(l, ("cc", j), 128, nt)
                self.act(cctmp[:, 0:nt], ps[:, 0:nt], AF.Copy, [Bp], [Bcct])
                ps2, Bp2 = proj(l, ("cx", j), 128, nt)
                self.v("vector", "tensor_tensor", [Bp2, Bcct], [Bextc], out=extcv[:, j, :, PC:PC + S],
                       in0=ps2[:, 0:nt].rearrange("p (b s) -> p b s", s=S), in1=cctmp[:, 0:nt].rearrange("p (b s) -> p b s", s=S), op=ALU.mult)
            if prompt:
                self.v("gpsimd", "tensor_copy", [Bextc], [Bconvpref], out=convpref[:, l, :, :], in_=extcv[:, :, 0, S:S + PC])
                if ti == cfg.NTILE - 1:
                    P.dma(OQ, o_convp[l], convpref[:, l, :, :].rearrange("p j e -> p (j e)"), reads=[Bconvpref])
            else:
                self.v("gpsimd", "tensor_copy", [Bextc], [Bsconv_st], out=sconv_st[:], in_=extcv[:, :, :, S:S + PC])
                P.dma(OQ, o_convs[l], sconv_st[:].rearrange("p j b e -> p (j b e)"), reads=[Bsconv_st])
            for j in range(2):
                cv = ctmp[:, 0:nt].rearrange("p (b s) -> p b s", s=S)
                self.v("vector", "tensor_scalar", [Bextc, Bconvw], [Bctmp], out=cv, in0=extcv[:, j, :, 0:S],
                       scalar1=convw[:, l, j, 0:1], scalar2=None, op0=ALU.mult)
                for k in (1, 2):
                    self.v("vector", "scalar_tensor_tensor", [Bextc, Bconvw, Bctmp], [Bctmp], out=cv, in0=extcv[:, j, :, k:k + S],
                           scalar=convw[:, l, j, k:k + 1], in1=cv, op0=ALU.mult, op1=ALU.add)
                self.v("vector", "tensor_tensor", [Bctmp, Bcb], [Byconv], out=yconv[:, j, 0:nt], in0=ctmp[:, 0:nt], in1=cbuf[:, j, 0:nt], op=ALU.mult)
            if prompt:
                t0 = ti * NT
                P.dma("sync", cosT[:, 0:nt], dC["cos_p"][:, t0:t0 + nt], writes=[Bcs])
                P.dma("sync", sinT[:, 0:nt], dC["sin_p"][:, t0:t0 + nt], writes=[Bcs])
                cos_ap, sin_ap, Bcos = cosT[:, 0:nt], sinT[:, 0:nt], Bcs
            else:
                cos_ap, sin_ap, Bcos = cos_s[:, 0:nt], sin_s[:, 0:nt], Bcs1
            for s in range(6):
                ps, Bp = proj(l, ("kv", s), 128, nt)
                if s in (2, 4):
                    rope_apply(ps[:, 0:nt], Bp, kvf[:, s, 0:nt], None, [Bkvf[s]], nt, cos_ap, sin_ap, Bcos)
                else:
                    self.act(kvf[:, s, 0:nt], ps[:, 0:nt], AF.Copy, [Bp], [Bkvf[s]])
            if prompt:
                t0 = ti * NT
                for s in range(4):
                    P.dma(OQ, o_kvp[l, s, :, t0:t0 + nt], kvf[:, s, 0:nt], reads=[Bkvf[s]])
                if t0 >= SEQ - 512:
                    w0 = t0 - (SEQ - 512)
                    for s in range(2):
                        P.dma(OQ, o_winp[l, s, :, w0:w0 + nt], kvf[:, 4 + s, 0:nt], reads=[Bkvf[4 + s]])
            else:
                for s in range(4):
                    P.dma(OQ, o_kvs[l, s], kvf[:, s, 0:nt], reads=[Bkvf[s]])
                for s in range(2):
                    P.dma(OQ, o_wins_new[l, s], kvf[:, 4 + s, 0:nt], reads=[Bkvf[4 + s]])
            for r in range(4):
                ps, Bp = proj(l, ("q", r), 128, nt)
                rope_apply(ps[:, 0:nt], Bp, None, (QTz[0][0:64, r, 0:nt], QTz[1][64:128, r, 0:nt]), [BQT], nt, cos_ap, sin_ap, Bcos)
            wg, Bwg = wslot(l, ("ng", 0))
            if prompt:
                for sub in range(NSUB):
                    for kc in range(8):
                        self.mm(PSM[:, 0:24], hT[:, kc, sub * 128:(sub + 1) * 128], wg[:, kc, 0:24], kc == 0, kc == 7, [BhT, Bwg], [BPSM])
                    self.act(gsig[:, sub, :], PSM[:, 0:24], AF.Sigmoid, [BPSM], [Bgsig])
            else:
                for b in range(NB):
                    for kc in range(8):
                        self.mm(PSM[0:4, b * 24:(b + 1) * 24], hT[:, kc, b * 4:(b + 1) * 4], wg[:, kc, 0:24], kc == 0, kc == 7, [BhT, Bwg], [BPSM])
                self.act(gsig[0:4, 0:NB, :], PSM[0:4, 0:NB * 24].rearrange("p (b c) -> p b c", c=24), AF.Sigmoid, [BPSM], [Bgsig])
            if cfg.nsa:
                if prompt:
                    nsa_prompt(l, ti)
                else:
                    nsa_sample(l)
            else:
                self.v("gpsimd", "memset", [], [Bynsa], ynsa[:], 0.0)
            for m in range(8):
                for bi, gk in enumerate(("gp", "gn", "gc")):
                    ps, Bp = proj(l, (gk, m), 128, nt)
                    sg, Bsg = sgate[bi % 2], Bsgate[bi % 2]
                    self.act(sg[:, 0:nt], ps[:, 0:nt], AF.Sigmoid, [Bp], [Bsg])
                    ps2, Bp2 = next_psa()
                    if bi == 0:
                        wb, Bwb = wslot(l, ("brp", m))
                        for g in range(4):
                            self.mm(ps2[:, 0:nt], wb[0:64, g, :], ypool[:, g, 0:nt], g == 0, g == 3, [Bwb, Bypool], [Bp2])
                        self.v("vector", "tensor_tensor", [Bp2, Bsg], [Bmacc], out=macc[:, 0:nt], in0=ps2[:, 0:nt], in1=sg[:, 0:nt], op=ALU.mult)
                    elif bi == 1:
                        wb, Bwb = wslot(l, ("brnc", m))
                        for k in range(4):
                            self.mm(ps2[:, 0:nt], wb[:, k, :], ynsa[:, k, 0:nt], k == 0, k == 3, [Bwb, Bynsa], [Bp2])
                        self.v("vector", "tensor_tensor", [Bp2, Bsg], [Bmtmp], out=mtmp[:, 0:nt], in0=ps2[:, 0:nt], in1=sg[:, 0:nt], op=ALU.mult)
                        self.v("gpsimd", "tensor_tensor", [Bmtmp, Bmacc], [Bmacc], out=macc[:, 0:nt], in0=macc[:, 0:nt], in1=mtmp[:, 0:nt], op=ALU.add)
                    else:
                        for k in range(2):
                            self.mm(ps2[:, 0:nt], wb[:, 4 + k, :], yconv[:, k, 0:nt], k == 0, k == 1, [Bwb, Byconv], [Bp2])
                        self.v("vector", "tensor_tensor", [Bp2, Bsg], [Bmtmp], out=mtmp[:, 0:nt], in0=ps2[:, 0:nt], in1=sg[:, 0:nt], op=ALU.mult)
                        self.v("gpsimd", "tensor_tensor", [Bmtmp, Bmacc], [Bsq], out=sq[:, m, 0:nt], in0=macc[:, 0:nt], in1=mtmp[:, 0:nt], op=ALU.add)
            for m in range(8):
                wt, Bw = wslot(l, ("wo", m))
                ps, Bp = next_psa()
                for kc in range(8):
                    self.mm(ps[:, 0:nt], wt[:, kc, :], sq[:, kc, 0:nt], kc == 0, kc == 7, [Bw, Bsq], [Bp])
                self.v("vector", "tensor_tensor", [Bp, Bx], [Bx], out=x[:, m, 0:nt], in0=x[:, m, 0:nt], in1=ps[:, 0:nt], op=ALU.add)
            rmsnorm(lambda kc: gffn[:, l, kc:kc + 1], Bgffn, nt)
            AW = 2 + S
            if not prompt:
                P.dma("sync", sffn[:].rearrange("p f b e -> p (f b e)"), d_sffn[l], writes=[Bsffn])
            for f in range(22):
                ea, Bea = exta[f % 2], Bexta[f % 2]
                eav = ea[:, 0:nseq * AW].rearrange("p (b e) -> p b e", e=AW)
                if prompt:
                    self.v("gpsimd", "tensor_copy", [Bffnpref], [Bea], out=eav[:, 0, 0:2], in_=ffnpref[:, l, f, :])
                else:
                    self.v("gpsimd", "tensor_copy", [Bsffn], [Bea], out=eav[:, :, 0:2], in_=sffn[:, f, :, :])
                ps, Bp = proj(l, ("fa", f), 128, nt)
                self.act(eav[:, :, 2:2 + S], ps[:, 0:nt].rearrange("p (b s) -> p b s", s=S), AF.Copy, [Bp], [Bea])
                if prompt:
                    self.v("gpsimd", "tensor_copy", [Bea], [Bffnpref], out=ffnpref[:, l, f, :], in_=eav[:, 0, S:S + 2])
                else:
                    self.v("gpsimd", "tensor_copy", [Bea], [Bffnnew], out=ffnnew[:, f, :, :], in_=eav[:, :, S:S + 2])
                at, Bat = atmp[f % 2], Batmp[f % 2]
                av = at[:, 0:nt].rearrange("p (b s) -> p b s", s=S)
                self.v("vector", "tensor_scalar", [Bea, Bffnw], [Bat], out=av, in0=eav[:, :, 0:S], scalar1=ffnw[:, l, f, 0:1], scalar2=None, op0=ALU.mult)
                for k in (1, 2):
                    self.v("vector", "scalar_tensor_tensor", [Bea, Bffnw, Bat], [Bat], out=av, in0=eav[:, :, k:k + S],
                           scalar=ffnw[:, l, f, k:k + 1], in1=av, op0=ALU.mult, op1=ALU.add)
                self.act(at[:, 0:nt], at[:, 0:nt], AF.Silu, [Bat], [Bat])
                ps2, Bp2 = proj(l, ("fb", f), 128, nt)
                self.v("vector", "tensor_tensor", [Bp2, Bat], [BgT], out=gT[:, f, 0:nt], in0=ps2[:, 0:nt], in1=at[:, 0:nt], op=ALU.mult)
            if prompt:
                if ti == cfg.NTILE - 1:
                    P.dma(OQ, o_ffnp[l], ffnpref[:, l, :, :].rearrange("p f e -> p (f e)"), reads=[Bffnpref])
            else:
                P.dma(OQ, o_ffns[l], ffnnew[:].rearrange("p f b e -> p (f b e)"), reads=[Bffnnew])
            for m in range(8):
                ps, Bp = next_psa()
                for part in range(3):
                    wt, Bw = wslot(l, ("fd", m, part))
                    kcs = list(range(part * 8, min(22, part * 8 + 8)))
                    for i, kc in enumerate(kcs):
                        self.mm(ps[:, 0:nt], wt[:, i, :], gT[:, kc, 0:nt], kc == 0, kc == 21, [Bw, BgT], [Bp])
                self.v("vector", "tensor_tensor", [Bp, Bx], [Bx], out=x[:, m, 0:nt], in0=x[:, m, 0:nt], in1=ps[:, 0:nt], op=ALU.add)

        def compress(l, src_fn, col0, nblk, Bsrc, kc_dst_fn, vc_dst_fn, Bkcd, Bvcd, cosc, sinc, Bcc):
            for s in range(2):
                w1 = [wslot(l, ("w1", s, q4)) for q4 in range(4)]
                w2, Bw2 = wslot(l, ("w2", 0))
                for li in range(32):
                    wt, Bw = w1[li // 8]
                    self.mm(PSM[:, 0:1], wt[0:64, li % 8, :], peT[:, l, s, li:li + 1], li == 0, li == 31, [Bw, BpeT], [BPSM])
                self.act(cvec[:, s:s + 1], PSM[:, 0:1], AF.Copy, [BPSM], [Bcvec])
                if cfg.nsa_sub < 2:
                    continue
                ps, Bp = next_psa()
                for g in range(2):
                    if cfg.nsa_var == 1 and g == 1:
                        continue
                    for li in range(32):
                        wt, Bw = w1[li // 8]
                        rhs_ = src_fn(s, g, li)
                        if cfg.nsa_var == 2:
                            rhs_ = cgb[g * 64:(g + 1) * 64, 0, 0:nblk]
                        self.mm(ps[:, g * 128:g * 128 + nblk], wt[:, li % 8, :], rhs_, li == 0, li == 31,
                                [Bw, Bsrc], [Bp])
                pv = ps[:, 0:256].rearrange("p (g n) -> p g n", n=128)[:, :, 0:nblk]
                xg = cgel[:, :, 0:nblk]
                x2 = cgel2[:, :, 0:nblk]
                if cfg.nsa_var == 3:
                    self.act(xg, pv, AF.Copy, [Bp, Bcvec], [Bcgel])
                else:
                    self.act(xg, pv, AF.Identity, [Bp, Bcvec], [Bcgel], bias=cvec[:, s:s + 1])
                self.act(x2, xg, AF.Square, [Bcgel], [Bcgel2])
                self.v("vector", "tensor_scalar", [Bcgel2], [Bcgel2], out=x2, in0=x2, scalar1=0.044715, scalar2=1.0, op0=ALU.mult, op1=ALU.add)
                self.v("vector", "tensor_tensor", [Bcgel2, Bcgel], [Bcgel2], out=x2, in0=x2, in1=xg, op=ALU.mult)
                self.act(x2, x2, AF.Sigmoid, [Bcgel2], [Bcgel2], scale=1.5957691216057308)
                self.v("vector", "tensor_tensor", [Bcgel2, Bcgel], [Bcgb], out=cgb[:, :, 0:nblk], in0=x2, in1=xg, op=ALU.mult)
                if cfg.nsa_sub < 3:
                    continue
                for g in range(2):
                    r = slice(g * 64, (g + 1) * 64)
                    self.mm(PST[:, g * 128:g * 128 + nblk], w2[:, s, :], cgb[:, g, 0:nblk], True, True, [Bw2, Bcgb], [BPST])
                    if s == 0:
                        rope_apply(PST[r, g * 128:g * 128 + nblk], BPST, None, kc_dst_fn(g), [Bkcd], nblk,
                                   cosc[r, col0:col0 + nblk], sinc[r, col0:col0 + nblk], Bcc, rows=r)
                    else:
                        self.act(vc_dst_fn(g), PST[r, g * 128:g * 128 + nblk], AF.Copy, [BPST], [Bvcd])

        def attend(q_rhs, nq, key_tiles, BQ):
            pso, Bpso = next_pso()
            nkt = len(key_tiles)
            pts = []
            for i, kt in enumerate(key_tiles):
                pss, Bpss = next_pss()
                M = kt["M"]
                nm = len(kt["masks"])
                self.mm(pss[0:M, 0:nq], kt["k"], q_rhs, True, nm == 0, [kt["Bk"], BQ], [Bpss])
                for mi, (ml, mr, mreads) in enumerate(kt["masks"]):
                    self.mm(pss[0:M, 0:nq], ml, mr, False, mi == nm - 1, mreads, [Bpss])
                pt, Bpt = next_pt()
                self.act(pt[0:M, 0:nq], pss[0:M, 0:nq], AF.Exp, [Bpss], [Bpt], scale=0.125)
                self.mm(pso[0:65, 0:nq], kt["v"], pt[0:M, 0:nq], i == 0, i == nkt - 1, [kt["Bv"], Bpt], [Bpso])
                pts.append((pt, Bpt, M))
            return pso, Bpso, pts

        def combine(pso, Bpso, nqtok, nr, branch, g, sub, first):
            ob, Bob = osb[self.osb_i % 2], Bosb[self.osb_i % 2]
            self.osb_i += 1
            nq = nr * nqtok
            self.act(ob[:, 0:nq], pso[0:65, 0:nq], AF.Copy, [Bpso], [Bob])
            for r in range(nr):
                self.tr(PST[0:nqtok, r * 65:(r + 1) * 65], ob[:, r * nqtok:(r + 1) * nqtok], identf[0:65, 0:65], [Bob, Bidentf], [BPST])
            tv = PST[0:nqtok, 0:nr * 65].rearrange("p (r e) -> p r e", e=65)
            self.v("vector", "tensor_scalar", [BPST], [Bsmal], out=smal[0:nqtok, 0:nr], in0=tv[:, :, 64], scalar1=1e-30, scalar2=None, op0=ALU.max)
            self.v("vector", "reciprocal", [Bsmal], [Bsmal], out=smal[0:nqtok, 0:nr], in_=smal[0:nqtok, 0:nr])
            c0 = branch * 8 + 4 * g
            self.v("vector", "tensor_tensor", [Bsmal, Bgsig], [Bsmal], out=smal[0:nqtok, 8:8 + nr], in0=smal[0:nqtok, 0:nr],
                   in1=gsig[0:nqtok, sub, c0:c0 + nr], op=ALU.mult)
            for r in range(nr):
                h = 4 * g + r
                if first:
                    self.v("vector", "tensor_scalar", [BPST, Bsmal], [Botok], out=otok[0:nqtok, h * 64:(h + 1) * 64], in0=tv[:, r, 0:64],
                           scalar1=smal[0:nqtok, 8 + r:9 + r], scalar2=None, op0=ALU.mult)
                else:
                    self.v("vector", "scalar_tensor_tensor", [BPST, Bsmal, Botok], [Botok], out=otok[0:nqtok, h * 64:(h + 1) * 64], in0=tv[:, r, 0:64],
                           scalar=smal[0:nqtok, 8 + r:9 + r], in1=otok[0:nqtok, h * 64:(h + 1) * 64], op0=ALU.mult, op1=ALU.add)

        def importance(pts, cover_fn, Bcov, nqtok, nr):
            first = True
            n = len(pts)
            for ci, (pt, Bpt, M) in enumerate(pts):
                for r in range(nr):
                    self.mm(PST[0:nqtok, r * 65:(r + 1) * 65], pt[0:M, r * nqtok:(r + 1) * nqtok], cover_fn(ci, M), first, (ci == n - 1) and (r == nr - 1),
                            [Bpt, Bcov], [BPST], skip=True)
                    first = False
            tv = PST[0:nqtok, 0:nr * 65].rearrange("p (r e) -> p r e", e=65)
            self.v("vector", "tensor_scalar", [BPST], [Bsmal], out=smal[0:nqtok, 16:16 + nr], in0=tv[:, :, 64], scalar1=1e-30, scalar2=None, op0=ALU.max)
            self.v("vector", "reciprocal", [Bsmal], [Bsmal], out=smal[0:nqtok, 16:16 + nr], in_=smal[0:nqtok, 16:16 + nr])
            for r in range(nr):
                if r == 0:
                    self.v("vector", "tensor_scalar", [BPST, Bsmal], [Bimp], out=imp[0:nqtok, :], in0=tv[:, r, 0:64], scalar1=smal[0:nqtok, 16:17], scalar2=None, op0=ALU.mult)
                else:
                    self.v("vector", "scalar_tensor_tensor", [BPST, Bsmal, Bimp], [Bimp], out=imp[0:nqtok, :], in0=tv[:, r, 0:64],
                           scalar=smal[0:nqtok, 16 + r:17 + r], in1=imp[0:nqtok, :], op0=ALU.mult, op1=ALU.add)

        def select_blocks(A_ap, B_ap, BAB, nqtok, nsel):
            self.v("vector", "tensor_tensor", [Bimp, BAB], [Bscore], out=score[0:nqtok, :], in0=imp[0:nqtok, :], in1=A_ap, op=ALU.mult)
            self.v("vector", "tensor_tensor", [Bscore, BAB], [Bscore], out=score[0:nqtok, :], in0=score[0:nqtok, :], in1=B_ap, op=ALU.add)
            if nsel > 16:
                self.v("vector", "max", [Bscore], [Bmx], out=mx[0:nqtok, 0:8], in_=score[0:nqtok, :])
                self.v("vector", "match_replace", [Bscore, Bmx], [Bscw], out=scw[0:nqtok, :], in_to_replace=mx[0:nqtok, 0:8], in_values=score[0:nqtok, :], imm_value=-1e30)
                self.v("vector", "max", [Bscw], [Bmx], out=mx[0:nqtok, 8:16], in_=scw[0:nqtok, :])
                self.v("vector", "tensor_scalar", [Bmx], [Bmx], out=mx[0:nqtok, 15:16], in0=mx[0:nqtok, 15:16], scalar1=0.0, scalar2=None, op0=ALU.max)
            else:
                self.v("vector", "memset", [], [Bmx], mx[0:nqtok, 15:16], 0.0)
            self.v("vector", "tensor_scalar", [Bscore, Bmx], [Bselb], out=selb[0:nqtok, 0:64], in0=score[0:nqtok, :], scalar1=mx[0:nqtok, 15:16], scalar2=None,
                   op0=ALU.is_lt)
            self.v("vector", "tensor_scalar", [Bselb], [Bselb], out=selb[0:nqtok, 0:64], in0=selb[0:nqtok, 0:64], scalar1=NEG, scalar2=None, op0=ALU.mult)
            self.tr(PSM[0:64, 0:nqtok], selb[0:nqtok, 0:64], identf[0:nqtok, 0:nqtok], [Bselb, Bidentf], [BPSM])
            self.act(selbT[:, 0:nqtok], PSM[0:64, 0:nqtok], AF.Copy, [BPSM], [BselbT])

        def nsa_prompt(l, ti):
            t0 = ti * NT
            if ti > 0:
                P.dma("sync", Kwk[:, 0:t0], hist_k[l, :, 0:t0], reads=[Bhk[l]], writes=[BKwk])
                P.dma("sync", Vwk[:, 0:ti * NSUB, :, :].rearrange("p k g e -> p (k g e)"), hist_v[l, :, 0:ti * NSUB * 130], reads=[Bhv[l]], writes=[BVwk])
            self.v("gpsimd", "tensor_copy", [Bkvf[2]], [BKwk], out=Kwk[:, t0:t0 + NT], in_=kvf[:, 2, :])
            wsl = ti % NWS
            self.v("gpsimd", "tensor_copy", [Bkvf[4]], [BKw[l]], out=KwT[l][:, wsl, :], in_=kvf[:, 4, :])
            for sub in range(NSUB):
                for (s, dst, Bd, idx) in ((3, Vwk, BVwk, ti * NSUB + sub), (5, Vw[l], BVw[l], wsl * NSUB + sub)):
                    self.tr(PST[:, 0:128], kvf[:, s, sub * 128:(sub + 1) * 128], identf[:], [Bkvf[s], Bidentf], [BPST])
                    self.act(dst[:, idx, :, 0:64], PST[:, 0:128].rearrange("p (g d) -> p g d", d=64), AF.Copy, [BPST], [Bd])
            if ti < cfg.NTILE - 1:
                P.dma(OQ, hist_k[l, :, t0:t0 + NT], Kwk[:, t0:t0 + NT], reads=[BKwk], writes=[Bhk[l]])
                P.dma(OQ, hist_v[l, :, ti * NSUB * 130:(ti + 1) * NSUB * 130],
                      Vwk[:, ti * NSUB:(ti + 1) * NSUB, :, :].rearrange("p k g e -> p (k g e)"), reads=[BVwk], writes=[Bhv[l]])
            if cfg.nsa_stage < 2:
                self.v("gpsimd", "memset", [], [Bynsa], ynsa[:], 0.0)
                return
            nb_t = NT // 16
            for g in range(2):
                r = slice(g * 64, (g + 1) * 64)
                self.v("gpsimd", "tensor_copy", [Bctail[l]], [Bcext], out=cext[r, g, :, 0:16], in_=ctail[l][r, :, :])
                for s in range(2):
                    self.v("gpsimd" if g else "vector", "tensor_copy", [Bkvf[s]], [Bcext], out=cext[r, g, s, 16:16 + NT], in_=kvf[r, s, :])
                self.v("gpsimd", "tensor_copy", [Bcext], [Bctail[l]], out=ctail[l][r, :, :], in_=cext[r, g, :, NT:NT + 16])
            c0 = nb_t * ti
            ch = c0 // 128

            def src_fn(s, g, li):
                a = cext[:, g, s, li:li + 1]
                return bass.AP(a.tensor, a.offset, [list(a.ap[0]), [16, nb_t]])

            compress(l, src_fn, c0, nb_t, Bcext,
                     lambda g: kcT[l][g * 64:(g + 1) * 64, c0:c0 + nb_t],
                     lambda g: vcT[l][g * 64:(g + 1) * 64, c0:c0 + nb_t],
                     Bkc[l], BvcT[l], cosc_p, sinc_p, Bcc1)
            if cfg.nsa_sub < 4:
                self.v("gpsimd", "memset", [], [Bynsa], ynsa[:], 0.0)
                return
            self.tr(PSTb[:, 0:128], vcT[l][:, ch * 128:(ch + 1) * 128], identb[:], [BvcT[l], Bidentb], [BPST])
            self.act(vcp[l][:, ch, :, 0:64], PSTb[:, 0:128].rearrange("p (g d) -> p g d", d=64), AF.Copy, [BPST], [Bvc[l]])
            nch = ch + 1
            if cfg.nsa_stage < 3:
                self.v("gpsimd", "memset", [], [Bynsa], ynsa[:], 0.0)
                return
            for sub in range(NSUB):
                qi = ti * NSUB + sub
                P.dma("sync", scAB[:, 0, :], dC["scA"][qi], writes=[BscAB])
                P.dma("sync", scAB[:, 1, :], dC["scB"][qi], writes=[BscAB])
                P.dma("gpsimd", cmpb[:], dC["cmpb"][qi], writes=[Bcmpb])
                for g in range(2):
                    r = slice(g * 64, (g + 1) * 64)
                    qa = QTz[g][:, :, sub * 128:(sub + 1) * 128]
                    kts = []
                    for c in range(nch):
                        kts.append(dict(k=kcT[l][:, c * 128:(c + 1) * 128], Bk=Bkc[l], M=128, v=vcp[l][:, c, g, :], Bv=Bvc[l],
                                        masks=[(identb[:], bc_mid(cmpb[:, c, :], 4), [Bidentb, Bcmpb])]))
                    pso, Bpso, pts = attend(qa, 512, kts, BQT)
                    importance(pts, lambda ci, M: cover_p[0:M, ci, :], Bcover_p, 128, 4)
                    combine(pso, Bpso, 128, 4, 0, g, sub, True)
                    if cfg.nsa_stage < 4:
                        continue
                    select_blocks(scAB[:, 0, :], scAB[:, 1, :], BscAB, 128, cfg.NSEL_P)
                    if cfg.nsa_stage < 5:
                        continue
                    kts = []
                    for kt in range(qi + 1):
                        masks = [(eexp[:, kt * 128:(kt + 1) * 128], bc_mid(selbT[:, 0:128], 4), [Beexp, BselbT])]
                        if kt == qi:
                            masks.append((identb[:], bc_mid(tri[:], 4), [Bidentb, Btri]))
                        kts.append(dict(k=Kwk[:, kt * 128:(kt + 1) * 128], Bk=BKwk, M=128, v=Vwk[:, kt, g, :], Bv=BVwk, masks=masks))
                    pso, Bpso, pts = attend(qa, 512, kts, BQT)
                    combine(pso, Bpso, 128, 4, 1, g, sub, False)
                    if cfg.nsa_stage < 6:
                        continue
                    kts = []
                    for kt in range(max(0, qi - 4), qi + 1):
                        masks = []
                        if kt == qi - 4:
                            masks.append((identb[:], bc_mid(winlo[:], 4), [Bidentb, Bwinlo]))
                        if kt == qi:
                            masks.append((identb[:], bc_mid(tri[:], 4), [Bidentb, Btri]))
                        ws_, wsub = (kt // NSUB) % NWS, kt % NSUB
                        kts.append(dict(k=KwT[l][:, ws_, wsub * 128:(wsub + 1) * 128], Bk=BKw[l], M=128, v=Vw[l][:, ws_ * NSUB + wsub, g, :], Bv=BVw[l], masks=masks))
                    pso, Bpso, pts = attend(qa, 512, kts, BQT)
                    combine(pso, Bpso, 128, 4, 2, g, sub, False)
                self.v("gpsimd", "tensor_copy", [Botok], [Botokb], out=otokb[:], in_=otok[:])
                for jj in range(4):
                    self.tr(PSMb[:, jj * 128:(jj + 1) * 128], otokb[:, jj * 128:(jj + 1) * 128], identb[:], [Botokb, Bidentb], [BPSM])
                self.act(ynsa[:, :, sub * 128:(sub + 1) * 128], PSMb[:, 0:512].rearrange("p (j q) -> p j q", q=128), AF.Copy, [BPSM], [Bynsa])

        NKA = 2
        self.ar_o = 0
        KA = [P.sb("KA0", [128, 5, TSP], BF16), carve(5 * TSP).rearrange("p (s t) -> p s t", t=TSP)]
        BKA = [P.buf(f"KA{i}") for i in range(NKA)]
        VB = [P.sb("VB0", [128, NPG + 1, 2, 65], BF16), carve((NPG + 1) * 130).rearrange("p (k g e) -> p k g e", g=2, e=65)]
        BVB = [P.buf(f"VB{i}") for i in range(NKA)]
        KWs = [P.sb("KWs0", [128, 512 + 4], BF16), carve(516)]
        BKWs = [P.buf(f"KWs{i}") for i in range(NKA)]
        VWs = [P.sb("VWs0", [128, 5, 2, 65], BF16), carve(5 * 130).rearrange("p (k g e) -> p k g e", g=2, e=65)]
        BVWs = [P.buf(f"VWs{i}") for i in range(NKA)]
        NSTG = 2
        stgA = [P.sb(f"stgA{i}", [128, 384], F32) for i in range(NSTG)]
        BstgA = [P.buf(f"stgA{i}") for i in range(NSTG)]
        stgB = [P.sb(f"stgB{i}", [128, 128], F32) for i in range(NSTG)]
        BstgB = [P.buf(f"stgB{i}") for i in range(NSTG)]
        stgW = carve(1024).bitcast(F32); BstgW = P.buf("stgW")
        stgV = carve(1024).bitcast(F32).rearrange("p (k c) -> p k c", c=128); BstgV = P.buf("stgV")
        kcTs = P.sb("kcTs", [128, 128], BF16); BkcTs = P.buf("kcTs")
        vcTs = P.sb("vcTs", [128, 128], BF16); BvcTs = P.buf("vcTs")
        vcs = P.sb("vcs", [128, 2, 65], BF16); Bvcs = P.buf("vcs")
        vnew = P.sb("vnew", [4, 2, 2, 65], BF16); Bvnew = P.buf("vnew")
        idxf = P.sb("idxf", [128, NB * NPG], F32); Bidxf = P.buf("idxf")
        idxi = P.sb("idxi", [128, NB * NPG], I32); Bidxi = P.buf("idxi")
        pti = P.sb("pti", [128, NB * NPG], I32); Bpti = P.buf("pti")
        self.v("gpsimd", "memset", [], [Bvcs], vcs[:], 1.0)
        self.v("gpsimd", "memset", [], [BvcTs], vcTs[:], 0.0)
        self.v("gpsimd", "memset", [], [BkcTs], kcTs[:], 0.0)
        self.v("gpsimd", "memset", [], [Bvnew], vnew[:], 1.0)
        def init_sample_set(i):
            self.v("gpsimd", "memset", [], [BKA[i]], KA[i][:], 0.0)
            self.v("gpsimd", "memset", [], [BVB[i]], VB[i][:], 1.0)
            self.v("gpsimd", "memset", [], [BVWs[i]], VWs[i][:], 1.0)

        init_sample_set(0)
        fdummy = P.sb("fdummy", [128, 1], F32)
        self.stg_i = 0
        self.seq_i = 0

        def nsa_sample(l):
            if l > 0:
                self.v("vector", "tensor_scalar", [Bidxf], [Bidxf], out=idxf[:], in0=idxf[:], scalar1=float(cfg.NPOOL * 128), scalar2=None, op0=ALU.add)
            self.v("vector", "tensor_copy", [Bidxf], [Bidxi], out=idxi[:, :], in_=idxf[:])
            for b in range(NB):
                par = self.seq_i % NKA
                self.seq_i += 1
                ka, Bka, vb, Bvb = KA[par], BKA[par], VB[par], BVB[par]
                kw, Bkw, vw, Bvw = KWs[par], BKWs[par], VWs[par], BVWs[par]
                for pg in range(NPG):
                    si = self.stg_i % NSTG
                    self.stg_i += 1
                    col = b * NPG + pg
                    P.dma_fn("gpsimd", (lambda e, si=si, col=col: e.indirect_dma_start(
                        out=stgA[si][:, :], out_offset=None, in_=d_poolA[:, :],
                        in_offset=bass.IndirectOffsetOnAxis(ap=idxi[:, col:col + 1], axis=0))), [Bidxi], [BstgA[si]])
                    P.dma_fn("gpsimd", (lambda e, si=si, col=col: e.indirect_dma_start(
                        out=stgB[si][:, :], out_offset=None, in_=d_poolB[:, :],
                        in_offset=bass.IndirectOffsetOnAxis(ap=idxi[:, col:col + 1], axis=0))), [Bidxi], [BstgB[si]])
                    sv = stgA[si][:, :].rearrange("p (s t) -> p s t", t=128)
                    self.v("vector", "tensor_copy", [BstgA[si]], [Bka], out=ka[0:64, 0:2, pg * 128:(pg + 1) * 128], in_=sv[0:64, 0:2, :])
                    self.v("gpsimd", "tensor_copy", [BstgA[si]], [Bka], out=ka[64:128, 2:4, pg * 128:(pg + 1) * 128], in_=sv[64:128, 0:2, :])
                    self.v("vector", "tensor_copy", [BstgA[si]], [Bka], out=ka[:, 4, pg * 128:(pg + 1) * 128], in_=sv[:, 2, :])
                    self.act(vb[:, pg, :, 0:64], stgB[si][:, :].rearrange("p (g d) -> p g d", d=64), AF.Copy, [BstgB[si]], [Bvb])
                for s in range(2):
                    self.v("gpsimd", "tensor_copy", [Bkvf[s]], [Bka], out=ka[0:64, s, cfg.PAST:cfg.PAST + 4], in_=kvf[0:64, s, b * 4:(b + 1) * 4])
                    self.v("gpsimd", "tensor_copy", [Bkvf[s]], [Bka], out=ka[64:128, 2 + s, cfg.PAST:cfg.PAST + 4], in_=kvf[64:128, s, b * 4:(b + 1) * 4])
                self.v("gpsimd", "tensor_copy", [Bkvf[2]], [Bka], out=ka[:, 4, cfg.PAST:cfg.PAST + 4], in_=kvf[:, 2, b * 4:(b + 1) * 4])
                for (s, wi) in ((3, 0), (5, 1)):
                    self.tr(PST[0:4, 0:128], kvf[:, s, b * 4:(b + 1) * 4], identf[:], [Bkvf[s], Bidentf], [BPST])
                    self.act(vnew[:, wi, :, 0:64], PST[0:4, 0:128].rearrange("p (g d) -> p g d", d=64), AF.Copy, [BPST], [Bvnew])
                P.dma("sync", stgW[:], d_kwinT[l, b], writes=[BstgW])
                P.dma("sync", stgV[:], d_cwin[l, b].rearrange("(k p) c -> p k c", p=128)[:, :, 128:256], writes=[BstgV])
                self.v("vector", "tensor_copy", [BstgW], [Bkw], out=kw[:, 0:512], in_=stgW[:])
                self.v("gpsimd", "tensor_copy", [Bkvf[4]], [Bkw], out=kw[:, 512:516], in_=kvf[:, 4, b * 4:(b + 1) * 4])
                self.act(vw[:, 0:4, :, 0:64], stgV[:].rearrange("p k (g d) -> p k g d", d=64), AF.Copy, [BstgV], [Bvw])
                P.dma("sync", o_wins_old[l, b], d_cwin[l, b, 4:512, :])
                nblk = cfg.NCMP_S

                def src_fn(s, g, li, ka=ka):
                    a = ka[:, 2 * g + s, li:li + 1]
                    return bass.AP(a.tensor, a.offset, [list(a.ap[0]), [16, nblk]])

                compress(l, src_fn, 0, nblk, Bka,
                         lambda g: kcTs[g * 64:(g + 1) * 64, 0:nblk],
                         lambda g: vcTs[g * 64:(g + 1) * 64, 0:nblk],
                         BkcTs, BvcTs, cosc_s, sinc_s, Bcc3)
                self.tr(PSTb[:, 0:128], vcTs[:, :], identb[:], [BvcTs, Bidentb], [BPST])
                self.act(vcs[:, :, 0:64], PSTb[:, 0:128].rearrange("p (g d) -> p g d", d=64), AF.Copy, [BPST], [Bvcs])
                for g in range(2):
                    r = slice(g * 64, (g + 1) * 64)
                    qa = QTz[g][:, :, b * 4:(b + 1) * 4]
                    kts = [dict(k=kcTs[:, 0:nblk], Bk=BkcTs, M=nblk, v=vcs[0:nblk, g, :], Bv=Bvcs, masks=[])]
                    pso, Bpso, pts = attend(qa, 16, kts, BQT)
                    importance(pts, lambda ci, M: cover_s[0:M, :], Bcover_s, 4, 4)
                    combine(pso, Bpso, 4, 4, 0, g, b, True)
                    select_blocks(scA_s[:, :], scB_s[:, :], BscA_s, 4, cfg.NSEL_S)
                    kts = []
                    for kt in range(NPG):
                        kts.append(dict(k=ka[:, 4, kt * 128:(kt + 1) * 128], Bk=Bka, M=128, v=vb[:, kt, g, :], Bv=Bvb,
                                        masks=[(eexp[:, kt * 128:(kt + 1) * 128], bc_mid(selbT[:, 0:4], 4), [Beexp, BselbT])]))
                    kts.append(dict(k=ka[:, 4, cfg.PAST:cfg.PAST + 4], Bk=Bka, M=4, v=vnew[:, 0, g, :], Bv=Bvnew,
                                    masks=[(eexp[:, cfg.PAST:cfg.PAST + 4], bc_mid(selbT[:, 0:4], 4), [Beexp, BselbT]),
                                           (identb[0:4, 0:4], bc_mid(newtri[:, :], 4), [Bidentb, Bnewtri])]))
                    pso, Bpso, pts = attend(qa, 16, kts, BQT)
                    combine(pso, Bpso, 4, 4, 1, g, b, False)
                    kts = []
                    for kt in range(4):
                        masks = [(identb[:], bc_mid(winlo_s[:, :], 4), [Bidentb, Bwinlo_s])] if kt == 0 else []
                        kts.append(dict(k=kw[:, kt * 128:(kt + 1) * 128], Bk=Bkw, M=128, v=vw[:, kt, g, :], Bv=Bvw, masks=masks))
                    kts.append(dict(k=kw[:, 512:516], Bk=Bkw, M=4, v=vnew[:, 1, g, :], Bv=Bvnew,
                                    masks=[(identb[0:4, 0:4], bc_mid(newtri[:, :], 4), [Bidentb, Bnewtri])]))
                    pso, Bpso, pts = attend(qa, 16, kts, BQT)
                    combine(pso, Bpso, 4, 4, 2, g, b, False)
                self.v("gpsimd", "tensor_copy", [Botok], [Botokb], out=otokb[0:4, :], in_=otok[0:4, :])
                for jj in range(4):
                    self.tr(PSMb[:, jj * 4:(jj + 1) * 4], otokb[0:4, jj * 128:(jj + 1) * 128], identb[0:4, 0:4], [Botokb, Bidentb], [BPSM])
                self.act(ynsa[:, :, b * 4:(b + 1) * 4], PSMb[:, 0:16].rearrange("p (j q) -> p j q", q=4), AF.Copy, [BPSM], [Bynsa])

        P.dma("sync", pti[:], d_pt.partition_broadcast(128), writes=[Bpti])
        self.v("vector", "tensor_copy", [Bpti], [Bidxf], out=idxf[:], in_=pti[:])
        self.v("vector", "tensor_scalar", [Bidxf], [Bidxf], out=idxf[:], in0=idxf[:], scalar1=128.0, scalar2=None, op0=ALU.mult)
        self.v("vector", "tensor_scalar", [Bidxf, Biota], [Bidxf], out=idxf[:], in0=idxf[:], scalar1=iota_p[:, 0:1], scalar2=None, op0=ALU.add)

        xp_v = d_xp.rearrange("(k p) t -> p k t", p=128)
        yp_v = o_yp.rearrange("(k p) t -> p k t", p=128)
        for ti in range(cfg.ntile_run if cfg.do_prompt else 0):
            t0 = ti * NT
            P.dma("sync", x[:], xp_v[:, :, t0:t0 + NT], writes=[Bx])
            for l in range(L):
                layer_tile(l, NT, 1, NT, True, ti)
            final_out(NT, lambda kc, t0=t0: yp_v[:, kc, t0:t0 + NT])
        self.v("gpsimd", "memset", [], [BKwk, BVwk, Bcext] + BKw + BVw + Bkc + BvcT + Bvc + [BKA[1], BVB[1], BKWs[1], BVWs[1], BstgW, BstgV],
               fdummy[:], 0.0)
        init_sample_set(1)
        nts = NB * 4
        P.dma("sync", x[:, :, 0:nts], d_xs.rearrange("(k p) t -> p k t", p=128), writes=[Bx])
        for l in range(L if cfg.do_sample else 0):
            layer_tile(l, nts, NB, 4, False, 0)
        ys_v = o_ys.rearrange("(k p) t -> p k t", p=128)
        final_out(nts, lambda kc: ys_v[:, kc, :])
        return P.finalize()


_CACHE = {}


def get_builder(cfg, key):
    if key not in _CACHE:
        b = Builder(cfg)
        b.stats = b.build()
        _CACHE[key] = b
    return _CACHE[key]


def run(cfg, key, inputs, n_cores=8):
    bld = get_builder(cfg, key)
    L, NB = cfg.L, cfg.NB
    f32 = np.float32
    A = lambda a: np.ascontiguousarray(np.asarray(a))
    consts = make_consts(cfg)
    shared = {}
    shared["w_in"] = A(inputs["w_in"])
    shared["pool_w"] = A(inputs["pool_w"]).reshape(L, 256, 64)
    for s in range(2):
        shared[f"cmp_w1_{s}"] = A(np.asarray(inputs["cmp_w1"])[:, s])
        shared[f"cmp_w2_{s}"] = A(np.asarray(inputs["cmp_w2"])[:, s])
    for k in ("w_br_pool", "w_br_nsa", "w_br_conv", "w_out", "ffn_up", "ffn_down"):
        shared[k] = A(inputs[k])
    shared["gmix"] = A(np.asarray(inputs["norm_mix"]).reshape(L, 8, 128).transpose(2, 0, 1))
    shared["gffn"] = A(np.asarray(inputs["norm_ffn"]).reshape(L, 8, 128).transpose(2, 0, 1))
    shared["gfin"] = A(np.asarray(inputs["norm_final"]).reshape(8, 128).transpose(1, 0))
    shared["pscale"] = A(np.asarray(inputs["pool_scale"]).reshape(L, 4, 64).transpose(2, 0, 1))
    shared["convw"] = A(np.asarray(inputs["conv_w"]).reshape(L, 3, 2, 128).transpose(3, 0, 2, 1))
    shared["ffnw"] = A(np.asarray(inputs["ffn_conv"]).reshape(L, 3, 22, 128).transpose(3, 0, 2, 1))
    shared["peT"] = A(np.asarray(inputs["cmp_pe"]).transpose(3, 0, 1, 2))
    for k, a in consts.items():
        shared["c_" + k] = A(a.astype(f32))
    ckv = np.asarray(inputs["cache_kv"])
    npool = ckv.shape[1]
    shared["poolA"] = A(ckv[:, :, :, 0:3].transpose(0, 1, 4, 5, 3, 2)).reshape(L * npool * 128, 384)
    shared["poolB"] = A(ckv[:, :, :, 3]).reshape(L * npool * 128, 128)
    xp = np.asarray(inputs["x_prompt"])
    xs = np.asarray(inputs["x_sample"])
    cw = np.asarray(inputs["cache_win"])
    sp = np.asarray(inputs["state_pool"])
    sc = np.asarray(inputs["state_conv"])
    sf = np.asarray(inputs["state_ffn"])
    pt = np.asarray(inputs["page_table"]).astype(np.int32)
    in_maps = []
    for c in range(n_cores):
        m = dict(shared)
        m["xpT"] = A(xp[c % xp.shape[0]].T)
        sl = slice(c * NB, (c + 1) * NB)
        m["xsT"] = A(xs[sl].reshape(NB * 4, 1024).T)
        m["ptab"] = A(pt[sl].reshape(1, -1))
        m["cwin"] = A(cw[:, sl].reshape(L, NB, 512, 256))
        m["kwinT"] = A(cw[:, sl, :, 0].reshape(L, NB, 512, 128).transpose(0, 1, 3, 2))
        m["spoolT"] = A(sp[:, sl].reshape(L, NB, 15, 4, 64).transpose(0, 4, 3, 1, 2)).reshape(L, 64, -1)
        m["sconvT"] = A(sc[:, sl].reshape(L, NB, 2, 2, 128).transpose(0, 4, 3, 1, 2)).reshape(L, 128, -1)
        m["sffnT"] = A(sf[:, sl].reshape(L, NB, 2, 22, 128).transpose(0, 4, 3, 1, 2)).reshape(L, 128, -1)
        in_maps.append(m)
    res = run_bass_kernel_spmd(bld.nc, in_maps, core_ids=list(range(n_cores)))
    R = res.results
    nbp = xp.shape[0]
    SEQ = cfg.SEQ
    y_prompt = np.stack([R[c]["o_yp"].T for c in range(nbp)])
    y_sample = np.concatenate([R[c]["o_ys"].T.reshape(NB, 4, 1024) for c in range(n_cores)])
    kv_prompt = np.stack([R[c]["o_kvp"].reshape(L, 4, 2, 64, SEQ).transpose(0, 4, 1, 2, 3) for c in range(nbp)], axis=1)
    kv_sample = np.concatenate([R[c]["o_kvs"].reshape(L, 4, 2, 64, NB, 4).transpose(0, 4, 5, 1, 2, 3) for c in range(n_cores)], axis=1)
    win_prompt = np.stack([R[c]["o_winp"].reshape(L, 2, 2, 64, 512).transpose(0, 4, 1, 2, 3) for c in range(nbp)], axis=1)
    wn = [R[c]["o_wins_new"].reshape(L, 2, 2, 64, NB, 4).transpose(0, 4, 5, 1, 2, 3) for c in range(n_cores)]
    wo = [R[c]["o_wins_old"].reshape(L, NB, 508, 2, 2, 64) for c in range(n_cores)]
    win_sample = np.concatenate([np.concatenate([wo[c], wn[c]], axis=2) for c in range(n_cores)], axis=1)
    pool_prompt = np.stack([R[c]["o_poolp"].reshape(L, 64, 4, 15).transpose(0, 3, 2, 1).reshape(L, 15, 256) for c in range(nbp)], axis=1)
    pool_sample = np.concatenate([R[c]["o_pools"].reshape(L, 64, 4, NB, 15).transpose(0, 3, 4, 2, 1).reshape(L, NB, 15, 256) for c in range(n_cores)], axis=1)
    conv_prompt = np.stack([R[c]["o_convp"].reshape(L, 128, 2, 2).transpose(0, 3, 2, 1).reshape(L, 2, 256) for c in range(nbp)], axis=1)
    conv_sample = np.concatenate([R[c]["o_convs"].reshape(L, 128, 2, NB, 2).transpose(0, 3, 4, 2, 1).reshape(L, NB, 2, 256) for c in range(n_cores)], axis=1)
    ffn_prompt = np.stack([R[c]["o_ffnp"].reshape(L, 128, 22, 2).transpose(0, 3, 2, 1).reshape(L, 2, 2816) for c in range(nbp)], axis=1)
    ffn_sample = np.concatenate([R[c]["o_ffns"].reshape(L, 128, 22, NB, 2).transpose(0, 3, 4, 2, 1).reshape(L, NB, 2, 2816) for c in range(n_cores)], axis=1)
    outs = (y_prompt, y_sample, kv_prompt, kv_sample, win_prompt, win_sample, pool_prompt, pool_sample,
            conv_prompt, conv_sample, ffn_prompt, ffn_sample)
    return tuple(np.ascontiguousarray(o.astype(np.float32)) for o in outs)


def kernel(**inputs):
    return run(FULL, "full", inputs)
```

```python
import numpy as np
import ml_dtypes
from contextlib import ExitStack
import concourse.bass as bass
import concourse.mybir as mybir
from concourse.bass_utils import run_bass_kernel_spmd

F32 = mybir.dt.float32
BF16 = mybir.dt.bfloat16
I32 = mybir.dt.int32
AF = mybir.ActivationFunctionType
ALU = mybir.AluOpType

SEM_EPOCH = 30000
N_DSEM = 80
N_DSEM_SW = 48
NEG = -30000.0


class Buf:
    __slots__ = ("name", "last_w", "readers")

    def __init__(self, name):
        self.name = name
        self.last_w = None
        self.readers = []


class Op:
    __slots__ = ("eng", "fn", "reads", "writes", "dma", "deps", "sig", "needed", "idx")

    def __init__(self, eng, fn, reads, writes, dma):
        self.eng = eng
        self.fn = fn
        self.reads = reads
        self.writes = writes
        self.dma = dma
        self.deps = []
        self.sig = None
        self.needed = False


class Prog:
    ENGS = ("tensor", "vector", "scalar", "gpsimd", "sync")

    def __init__(self, nc):
        self.nc = nc
        self.ops = []
        self.stack = ExitStack()
        self.nbuf = 0

    def buf(self, name="b"):
        self.nbuf += 1
        return Buf(name)

    def sb(self, name, shape, dt):
        return self.stack.enter_context(self.nc.sbuf_tensor("s_" + name, list(shape), dt))

    def ps(self, name, shape, dt):
        return self.stack.enter_context(self.nc.psum_tensor("p_" + name, list(shape), dt))

    def op(self, eng, fn, reads=(), writes=()):
        o = Op(eng, fn, [b for b in reads if b is not None], [b for b in writes if b is not None], False)
        self.ops.append(o)
        return o

    def dma(self, q, out, in_, reads=(), writes=(), **kw):
        def fn(e):
            return e.dma_start(out=out, in_=in_, **kw)
        o = Op(q, fn, [b for b in reads if b is not None], [b for b in writes if b is not None], True)
        self.ops.append(o)
        return o

    def dma_fn(self, q, fn, reads=(), writes=()):
        o = Op(q, fn, [b for b in reads if b is not None], [b for b in writes if b is not None], True)
        self.ops.append(o)
        return o

    def finalize(self):
        nc = self.nc
        ops = self.ops
        for o in ops:
            deps = set()
            for b in o.reads:
                if b.last_w is not None:
                    deps.add(b.last_w)
            for b in o.writes:
                if b.last_w is not None:
                    deps.add(b.last_w)
                for r in b.readers:
                    deps.add(r)
            deps.discard(o)
            dl = []
            for d in deps:
                if (not o.dma) and (not d.dma) and o.eng == "tensor" and d.eng == "tensor":
                    continue
                d.needed = True
                dl.append(d)
            o.deps = dl
            for b in o.reads:
                b.readers.append(o)
            for b in o.writes:
                b.last_w = o
                b.readers = []
        cnt = {e: 0 for e in self.ENGS}
        epoch = {e: 0 for e in self.ENGS}
        dcount = [0] * N_DSEM
        qrange = {"gpsimd": (0, N_DSEM_SW), "sync": (N_DSEM_SW, N_DSEM)}
        qi = {q: 0 for q in qrange}
        for o in ops:
            if o.dma:
                lo, hi = qrange[o.eng]
                s = lo + qi[o.eng] % (hi - lo)
                qi[o.eng] += 1
                prev = dcount[s]
                dcount[s] += 16
                o.sig = ("d", s, dcount[s], prev)
            elif o.needed:
                e = o.eng
                if cnt[e] >= SEM_EPOCH:
                    epoch[e] += 1
                    cnt[e] = 0
                cnt[e] += 1
                o.sig = ("e", (e, epoch[e]), cnt[e])
        st = self.stack
        esem = {}
        for e in self.ENGS:
            for k in range(epoch[e] + 1):
                esem[(e, k)] = st.enter_context(nc.semaphore(f"s_{e}_{k}"))
        dsem = [st.enter_context(nc.semaphore(f"d_{k}")) for k in range(N_DSEM)]
        per_eng = {e: [] for e in self.ENGS}
        for o in ops:
            per_eng[o.eng].append(o)
        final_d = {}
        for o in ops:
            if o.dma:
                final_d[o.sig[1]] = o.sig[2]

        def emit(ename, e):
            seen = {}
            for o in per_eng[ename]:
                for d in o.deps:
                    if d.sig[0] == "d":
                        key, val, sem = ("d", d.sig[1]), d.sig[2], dsem[d.sig[1]]
                    else:
                        key, val, sem = d.sig[1], d.sig[2], esem[d.sig[1]]
                    if seen.get(key, 0) >= val:
                        continue
                    e.wait_ge(sem, val)
                    seen[key] = val
                if o.dma:
                    _, s, val, prev = o.sig
                    if prev > 0 and seen.get(("d", s), 0) < prev:
                        e.wait_ge(dsem[s], prev)
                        seen[("d", s)] = prev
                    o.fn(e).then_inc(dsem[s], 16)
                else:
                    ins = o.fn(e)
                    if o.sig is not None:
                        ins.then_inc(esem[o.sig[1]], 1)
            if ename == "sync":
                for s, val in final_d.items():
                    if seen.get(("d", s), 0) < val:
                        e.wait_ge(dsem[s], val)

        with nc.Block() as block:
            @block.tensor
            def _(e):
                emit("tensor", e)

            @block.vector
            def _(e):
                emit("vector", e)

            @block.scalar
            def _(e):
                emit("scalar", e)

            @block.gpsimd
            def _(e):
                emit("gpsimd", e)

            @block.sync
            def _(e):
                emit("sync", e)
        self.stack.close()
        return {e: len(per_eng[e]) for e in self.ENGS}


class Cfg:
    def __init__(self, SEQ=4096, L=4, NB=16, PAST=2048, NPOOL=2560, nsa=True, do_prompt=True, do_sample=True, ntile=None):
        self.D = 1024
        self.SEQ = SEQ
        self.L = L
        self.NB = NB
        self.PAST = PAST
        self.NPOOL = NPOOL
        self.NT = 256
        self.NTILE = SEQ // 256
        self.NPG = PAST // 128
        self.TS = PAST + 4
        self.NCMP_S = (self.TS - 32) // 16 + 1
        self.NSEL_S = -(-self.TS // 64)
        self.NSEL_P = SEQ // 64
        self.NCMP_P = (SEQ - 32) // 16 + 1
        self.NCH_P = -(-(self.NCMP_P + 1) // 128)
        self.DFF = 2816
        self.NF = 22
        self.INW = 5400
        self.nsa = nsa
        self.do_prompt = do_prompt
        self.do_sample = do_sample
        self.ntile_run = ntile if ntile is not None else self.NTILE


FULL = Cfg()

OFF_POOL = 0
OFF_Q = 256
OFF_KV = 768
OFF_NG = 1536
OFF_CB = 1560
OFF_CC = 1816
OFF_CX = 2072
OFF_GP = 2328
OFF_GN = 3352
OFF_GC = 4376


def slot_plan(cfg):
    slots = []
    names = {}

    def full_k(wname, cols, key, K=1024):
        pieces = []
        m0 = 0
        for (c0, n) in cols:
            for kc in range(K // 128):
                pieces.append((wname, kc * 128, 128, 0, kc, c0, n, m0))
            m0 += n
        names[key] = len(slots)
        slots.append(pieces)

    for g in range(4):
        full_k("w_in", [(OFF_POOL + 64 * g, 64)], ("pool", g))
    for j in range(2):
        full_k("w_in", [(OFF_CB + 128 * j, 128)], ("cb", j))
    for j in range(2):
        full_k("w_in", [(OFF_CC + 128 * j, 128)], ("cc", j))
    for j in range(2):
        full_k("w_in", [(OFF_CX + 128 * j, 128)], ("cx", j))
    for s in range(6):
        full_k("w_in", [(OFF_KV + 128 * s, 128)], ("kv", s))
    for r in range(4):
        full_k("w_in", [(OFF_Q + 64 * r, 64), (OFF_Q + 64 * (4 + r), 64)], ("q", r))
    full_k("w_in", [(OFF_NG, 24)], ("ng", 0))
    names[("pool_w", 0)] = len(slots)
    slots.append([("pool_w", g * 64, 64, 0, g, 0, 64, 0) for g in range(4)])
    for s in range(2):
        for q4 in range(4):
            pieces = []
            for l8 in range(8):
                l = q4 * 8 + l8
                for dup in range(2):
                    pieces.append((f"cmp_w1_{s}", l * 64, 64, dup * 64, l8, 0, 128, 0))
            names[("w1", s, q4)] = len(slots)
            slots.append(pieces)
    names[("w2", 0)] = len(slots)
    slots.append([(f"cmp_w2_{s}", 0, 128, 0, s, 0, 64, 0) for s in range(2)]
                 + [(f"cmp_w2_{s}", 0, 128, 0, s, 0, 64, 64) for s in range(2)])
    for m in range(8):
        full_k("w_in", [(OFF_GP + 128 * m, 128)], ("gp", m))
        names[("brp", m)] = len(slots)
        slots.append([("w_br_pool", g * 64, 64, 0, g, 128 * m, 128, 0) for g in range(4)])
        full_k("w_in", [(OFF_GN + 128 * m, 128)], ("gn", m))
        names[("brnc", m)] = len(slots)
        slots.append([("w_br_nsa", k * 128, 128, 0, k, 128 * m, 128, 0) for k in range(4)]
                     + [("w_br_conv", k * 128, 128, 0, 4 + k, 128 * m, 128, 0) for k in range(2)])
        full_k("w_in", [(OFF_GC + 128 * m, 128)], ("gc", m))
    for m in range(8):
        full_k("w_out", [(128 * m, 128)], ("wo", m))
    for f in range(22):
        full_k("ffn_up", [(128 * f, 128)], ("fa", f))
        full_k("ffn_up", [(2816 + 128 * f, 128)], ("fb", f))
    for m in range(8):
        for part in range(3):
            kcs = list(range(part * 8, min(22, part * 8 + 8)))
            names[("fd", m, part)] = len(slots)
            slots.append([("ffn_down", kc * 128, 128, 0, i, 128 * m, 128, 0) for i, kc in enumerate(kcs)])
    return slots, names


def make_consts(cfg):
    c = {}
    half = 32
    inv = (10000.0 ** (-np.arange(half, dtype=np.float32) / half)).astype(np.float32)

    def cs_tab(pos):
        ang = pos.astype(np.float32)[None, :] * inv[:, None]
        cos = np.cos(ang).astype(np.float32)
        sin = np.sin(ang).astype(np.float32)
        return np.tile(cos, (4, 1)), np.tile(sin, (4, 1))

    c["cos_p"], c["sin_p"] = cs_tab(np.arange(cfg.SEQ))
    ps = cfg.PAST + np.arange(4)
    cs, sn = cs_tab(ps)
    c["cos_s"] = np.tile(cs, (1, cfg.NB)).astype(np.float32)
    c["sin_s"] = np.tile(sn, (1, cfg.NB)).astype(np.float32)
    ncolp = cfg.NCH_P * 128
    c["cosc_p"], c["sinc_p"] = cs_tab(16 * (np.arange(ncolp) - 1) + 31)
    c["cosc_s"], c["sinc_s"] = cs_tab(16 * np.arange(128) + 31)
    R = np.zeros((128, 128), np.float32)
    for hh in range(2):
        for d in range(64):
            m = hh * 64 + d
            if d < 32:
                R[hh * 64 + d + 32, m] = -1.0
            else:
                R[hh * 64 + d - 32, m] = 1.0
    c["rot"] = R
    c["ident"] = np.eye(128, dtype=np.float32)
    c["iota_p"] = np.arange(128, dtype=np.float32).reshape(128, 1)
    nk = max(cfg.SEQ, cfg.NSEL_S * 64)
    e = np.zeros((64, nk), np.float32)
    for j in range(64):
        e[j, j * 64:(j + 1) * 64] = 1.0
    c["eexp"] = e
    k = np.arange(128)[:, None]
    q = np.arange(128)[None, :]
    c["tri"] = np.where(k <= q, 0.0, NEG).astype(np.float32)
    c["winlo"] = np.where(k >= q, 0.0, NEG).astype(np.float32)
    nq = cfg.SEQ // 128
    cmpb = np.zeros((nq, 128, cfg.NCH_P, 128), np.float32)
    scA = np.zeros((nq, 128, 64), np.float32)
    scB = np.zeros((nq, 128, 64), np.float32)
    for qi in range(nq):
        pos = qi * 128 + np.arange(128)
        for ch in range(cfg.NCH_P):
            col = ch * 128 + np.arange(128)
            ok = (16 * col[:, None] + 15 <= pos[None, :]) & (col[:, None] >= 1) & (col[:, None] <= cfg.NCMP_P)
            cmpb[qi, :, ch, :] = np.where(ok, 0.0, NEG)
        jj = np.arange(64)[None, :]
        cur = (pos // 64)[:, None]
        causal = (jj * 64 <= pos[:, None]) & (jj < cfg.NSEL_P)
        forced = ((jj == 0) | ((jj <= cur) & (jj > cur - 2))) & (jj < cfg.NSEL_P)
        scA[qi] = np.where(forced, 0.0, np.where(causal, 1.0, 0.0))
        scB[qi] = np.where(forced, 1e4, np.where(causal, 0.0, -1.0))
    c["cmpb"] = cmpb
    c["scA"] = scA
    c["scB"] = scB
    cov = np.zeros((128, cfg.NCH_P, 65), np.float32)
    for ch in range(cfg.NCH_P):
        for cc in range(128):
            n = ch * 128 + cc - 1
            if n < 0 or n >= cfg.NCMP_P:
                continue
            cst = 16 * n
            for j in range(cfg.NSEL_P):
                if cst < j * 64 + 64 and cst + 32 > j * 64:
                    cov[cc, ch, j] = 1.0
            cov[cc, ch, 64] = 1.0
    c["cover_p"] = cov
    covs = np.zeros((128, 65), np.float32)
    for n in range(cfg.NCMP_S):
        cst = 16 * n
        for j in range(cfg.NSEL_S):
            if cst < j * 64 + 64 and cst + 32 > j * 64:
                covs[n, j] = 1.0
        covs[n, 64] = 1.0
    c["cover_s"] = covs
    cur = cfg.PAST // 64
    jj = np.arange(64)
    forced = (jj == 0) | ((jj <= cur) & (jj > cur - 2))
    exist = jj < cfg.NSEL_S
    A = np.where(forced | ~exist, 0.0, 1.0)
    B = np.where(~exist, -1.0, np.where(forced, 1e4, 0.0))
    c["scA_s"] = np.tile(A[None, :], (4, 1)).astype(np.float32)
    c["scB_s"] = np.tile(B[None, :], (4, 1)).astype(np.float32)
    kk = np.arange(4)[:, None]
    qq = np.arange(4)[None, :]
    c["newtri"] = np.where(kk <= qq, 0.0, NEG).astype(np.float32)
    c["winlo_s"] = np.where(np.arange(128)[:, None] >= qq, 0.0, NEG).astype(np.float32)
    rc = np.zeros((64, 4, 16), np.float32)
    for g, w in enumerate((2, 4, 8, 16)):
        rc[:, g, :] = 1.0 / np.minimum(float(w), np.arange(16) + 1.0)[None, :]
    c["rcnt"] = rc
    return c


CONST_BF = ("ident", "eexp", "tri", "winlo", "cmpb", "cover_p", "cover_s", "newtri", "winlo_s")


def bc_mid(ap, r):
    a = [list(x) for x in ap.ap]
    return bass.AP(ap.tensor, ap.offset, [a[0], [0, r]] + a[1:])


class Builder:
    def __init__(self, cfg):
        self.cfg = cfg
        self.nc = bass.Bass("TRN2", target_bir_lowering=False)
        self.P = Prog(self.nc)
        self.din = {}
        self.dout = {}
        self.slots, self.sname = slot_plan(cfg)
        self.NS = len(self.slots)
        self.sdims = [(max(p[3] + p[2] for p in ps), max(p[4] for p in ps) + 1, max(p[7] + p[6] for p in ps)) for ps in self.slots]

    def inp(self, name, shape, dt=F32):
        t = self.nc.dram_tensor(name, list(shape), dt, kind="ExternalInput").ap()
        self.din[name] = (tuple(shape), dt)
        return t

    def outp(self, name, shape, dt=F32):
        t = self.nc.dram_tensor(name, list(shape), dt, kind="ExternalOutput").ap()
        self.dout[name] = (tuple(shape), dt)
        return t

    def mm(self, out, lhsT, rhs, start, stop, reads, writes, skip=False):
        kw = {}
        if skip:
            kw["skip_group_check"] = True
        self.P.op("tensor", lambda e: e.matmul(out, lhsT=lhsT, rhs=rhs, start=start, stop=stop, **kw), reads, writes)

    def tr(self, out, in_, ident, reads, writes):
        self.P.op("tensor", lambda e: e.transpose(out, in_, ident), reads, writes)

    def act(self, out, in_, func, reads, writes, **kw):
        self.P.op("scalar", lambda e: e.activation(out, in_, func, **kw), reads, writes)

    def v(self, eng, name, reads, writes, *a, **kw):
        self.P.op(eng, lambda e: getattr(e, name)(*a, **kw), reads, writes)

    def build(self):
        cfg = self.cfg
        P = self.P
        nc = self.nc
        L, NT, NB = cfg.L, cfg.NT, cfg.NB
        NS = self.NS
        SEQ = cfg.SEQ
        NKT = SEQ // 128
        NSUB = NT // 128
        W = {}
        W["w_in"] = self.inp("w_in", [L, 1024, 5400])
        W["pool_w"] = self.inp("pool_w", [L, 256, 64])
        for s in range(2):
            W[f"cmp_w1_{s}"] = self.inp(f"cmp_w1_{s}", [L, 2048, 128])
            W[f"cmp_w2_{s}"] = self.inp(f"cmp_w2_{s}", [L, 128, 64])
        W["w_br_pool"] = self.inp("w_br_pool", [L, 256, 1024])
        W["w_br_nsa"] = self.inp("w_br_nsa", [L, 512, 1024])
        W["w_br_conv"] = self.inp("w_br_conv", [L, 256, 1024])
        W["w_out"] = self.inp("w_out", [L, 1024, 1024])
        W["ffn_up"] = self.inp("ffn_up", [L, 1024, 5632])
        W["ffn_down"] = self.inp("ffn_down", [L, 2816, 1024])
        ws = nc.dram_tensor("ws", [L * NS, 128, 1024], BF16, kind="Internal").ap()
        hist_k = nc.dram_tensor("hist_k", [L, 128, SEQ], BF16, kind="Internal").ap()
        hist_v = nc.dram_tensor("hist_v", [L, 128, NKT * 130], BF16, kind="Internal").ap()
        d_gmix = self.inp("gmix", [128, L, 8])
        d_gffn = self.inp("gffn", [128, L, 8])
        d_gfin = self.inp("gfin", [128, 8])
        d_pscale = self.inp("pscale", [64, L, 4])
        d_convw = self.inp("convw", [128, L, 2, 3])
        d_ffnw = self.inp("ffnw", [128, L, 22, 3])
        d_peT = self.inp("peT", [64, L, 2, 32])
        consts = make_consts(cfg)
        dC = {}
        for k, a in consts.items():
            dC[k] = self.inp("c_" + k, a.shape, F32)
        d_xp = self.inp("xpT", [1024, SEQ])
        d_xs = self.inp("xsT", [1024, NB * 4])
        d_poolA = self.inp("poolA", [L * cfg.NPOOL * 128, 384])
        d_poolB = self.inp("poolB", [L * cfg.NPOOL * 128, 128])
        d_pt = self.inp("ptab", [1, NB * cfg.NPG], I32)
        d_cwin = self.inp("cwin", [L, NB, 512, 256])
        d_kwinT = self.inp("kwinT", [L, NB, 128, 512])
        d_spool = self.inp("spoolT", [L, 64, 4 * NB * 15])
        d_sconv = self.inp("sconvT", [L, 128, 2 * NB * 2])
        d_sffn = self.inp("sffnT", [L, 128, 22 * NB * 2])
        o_yp = self.outp("o_yp", [1024, SEQ])
        o_ys = self.outp("o_ys", [1024, NB * 4])
        o_kvp = self.outp("o_kvp", [L, 4, 128, SEQ])
        o_kvs = self.outp("o_kvs", [L, 4, 128, NB * 4])
        o_winp = self.outp("o_winp", [L, 2, 128, 512])
        o_wins_new = self.outp("o_wins_new", [L, 2, 128, NB * 4])
        o_wins_old = self.outp("o_wins_old", [L, NB, 508, 256])
        o_poolp = self.outp("o_poolp", [L, 64, 4 * 15])
        o_pools = self.outp("o_pools", [L, 64, 4 * NB * 15])
        o_convp = self.outp("o_convp", [L, 128, 2 * 2])
        o_convs = self.outp("o_convs", [L, 128, 2 * NB * 2])
        o_ffnp = self.outp("o_ffnp", [L, 128, 22 * 2])
        o_ffns = self.outp("o_ffns", [L, 128, 22 * NB * 2])

        OQ = "gpsimd"
        Bws = [[P.buf(f"ws{l}_{si}") for si in range(NS)] for l in range(L)]

        def emit_casts(l):
            for si, pieces in enumerate(self.slots):
                groups = {}
                for (wn, r0, nr, p0, kc, c0, ncol, m0) in pieces:
                    groups.setdefault((wn, nr, p0, c0, ncol, m0), []).append((r0, kc))
                for (wn, nr, p0, c0, ncol, m0), lst in groups.items():
                    lst.sort(key=lambda t: t[1])
                    r0s = [t[0] for t in lst]
                    kcs = [t[1] for t in lst]
                    n = len(lst)
                    step_r = (r0s[1] - r0s[0]) if n > 1 else 0
                    step_k = (kcs[1] - kcs[0]) if n > 1 else 1
                    assert all(r0s[i] - r0s[0] == i * step_r and kcs[i] - kcs[0] == i * step_k for i in range(n))
                    wt = W[wn]
                    rowlen = wt.shape[2]
                    src0 = wt[l, r0s[0]:r0s[0] + nr, c0:c0 + ncol]
                    src = bass.AP(src0.tensor, src0.offset, [[rowlen, nr], [step_r * rowlen, n], [1, ncol]])
                    ph, kh, mh = self.sdims[si]
                    dst0 = ws[l * NS + si, p0:p0 + nr, kcs[0] * mh + m0: kcs[0] * mh + m0 + ncol]
                    dst = bass.AP(dst0.tensor, dst0.offset, [[1024, nr], [step_k * mh, n], [1, ncol]])
                    P.dma("gpsimd", dst, src, writes=[Bws[l][si]])

        for l in range(L):
            emit_casts(l)

        def load_const(name, dram, shape, dt, buf=None):
            t = P.sb(name, shape, dt)
            b = buf if buf is not None else P.buf(name)
            if dt == F32:
                P.dma("sync", t[:], dram, writes=[b])
            else:
                P.dma("gpsimd", t[:], dram, writes=[b])
            return t, b

        gmix, Bgmix = load_const("gmix", d_gmix, [128, L, 8], F32)
        gffn, Bgffn = load_const("gffn", d_gffn, [128, L, 8], F32)
        gfin, Bgfin = load_const("gfin", d_gfin, [128, 8], F32)
        pscale, Bpscale = load_const("pscale", d_pscale, [64, L, 4], F32)
        convw, Bconvw = load_const("convw", d_convw, [128, L, 2, 3], F32)
        ffnw, Bffnw = load_const("ffnw", d_ffnw, [128, L, 22, 3], F32)
        peT, BpeT = load_const("peT", d_peT, [64, L, 2, 32], BF16)
        rot, Brot = load_const("rot", dC["rot"], [128, 128], F32)
        identf, Bidentf = load_const("identf", dC["ident"], [128, 128], F32)
        identb, Bidentb = load_const("identb", dC["ident"], [128, 128], BF16)
        eexp, Beexp = load_const("eexp", dC["eexp"], list(consts["eexp"].shape), BF16)
        tri, Btri = load_const("tri", dC["tri"], [128, 128], BF16)
        winlo, Bwinlo = load_const("winlo", dC["winlo"], [128, 128], BF16)
        cover_p, Bcover_p = load_const("cover_p", dC["cover_p"], [128, cfg.NCH_P, 65], BF16)
        cover_s, Bcover_s = load_const("cover_s", dC["cover_s"], [128, 65], BF16)
        cosc_p, Bcc1 = load_const("cosc_p", dC["cosc_p"], [128, cfg.NCH_P * 128], F32)
        sinc_p, _ = load_const("sinc_p", dC["sinc_p"], [128, cfg.NCH_P * 128], F32, Bcc1)
        cosc_s, Bcc3 = load_const("cosc_s", dC["cosc_s"], [128, 128], F32)
        sinc_s, _ = load_const("sinc_s", dC["sinc_s"], [128, 128], F32, Bcc3)
        cos_s, Bcs1 = load_const("cos_s", dC["cos_s"], [128, NB * 4], F32)
        sin_s, _ = load_const("sin_s", dC["sin_s"], [128, NB * 4], F32, Bcs1)
        scA_s, BscA_s = load_const("scA_s", dC["scA_s"], [4, 64], F32)
        scB_s, _ = load_const("scB_s", dC["scB_s"], [4, 64], F32, BscA_s)
        newtri, Bnewtri = load_const("newtri", dC["newtri"], [4, 4], BF16)
        winlo_s, Bwinlo_s = load_const("winlo_s", dC["winlo_s"], [128, 4], BF16)
        rcnt, Brcnt = load_const("rcnt", dC["rcnt"], [64, 4, 16], F32)
        iota_p, Biota = load_const("iota_p", dC["iota_p"], [128, 1], F32)
        ones_b = P.sb("ones_b", [128, 128], BF16)
        Bones = P.buf("ones")
        self.v("vector", "memset", [], [Bones], ones_b[:], 1.0)

        x = P.sb("x", [128, 8, NT], F32); Bx = P.buf("x")
        hT = P.sb("hT", [128, 8, NT], BF16); BhT = P.buf("hT")
        sq = P.sb("sq", [128, 8, NT], BF16); Bsq = P.buf("sq")
        rstd = P.sb("rstd", [128, NT], F32); Brstd = P.buf("rstd")
        NRING = 8
        ring = [P.sb(f"wr{i}", [128, 8, 128], BF16) for i in range(NRING)]
        Bring = [P.buf(f"wr{i}") for i in range(NRING)]
        self.ring_i = 0
        PSA = [P.ps(f"psA{i}", [128, 512], F32) for i in range(2)]
        BPSA = [P.buf(f"psA{i}") for i in range(2)]
        PSS = [P.ps(f"psS{i}", [128, 512], F32) for i in range(2)]
        BPSS = [P.buf(f"psS{i}") for i in range(2)]
        PSO = [P.ps(f"psO{i}", [128, 512], F32) for i in range(2)]
        BPSO = [P.buf(f"psO{i}") for i in range(2)]
        PST = P.ps("psT", [128, 512], F32); BPST = P.buf("psT")
        PSM = P.ps("psM", [128, 512], F32); BPSM = P.buf("psM")
        PSMb = PSM[:].bitcast(BF16)
        PSTb = PST[:].bitcast(BF16)
        self.psa_i = 0
        self.pss_i = 0
        self.pso_i = 0

        def next_psa():
            i = self.psa_i % 2
            self.psa_i += 1
            return PSA[i], BPSA[i]

        def next_pss():
            i = self.pss_i % 2
            self.pss_i += 1
            return PSS[i], BPSS[i]

        def next_pso():
            i = self.pso_i % 2
            self.pso_i += 1
            return PSO[i], BPSO[i]

        def wslot(l, key):
            si = self.sname[key]
            i = self.ring_i % NRING
            self.ring_i += 1
            ph, kh, mh = self.sdims[si]
            flat = ring[i][:].rearrange("p k m -> p (k m)")
            P.dma("sync", flat[0:ph, 0:kh * mh], ws[l * NS + si, 0:ph, 0:kh * mh], reads=[Bws[l][si]], writes=[Bring[i]])
            return flat[:, 0:kh * mh].rearrange("p (k m) -> p k m", m=mh), Bring[i]

        EWMAX = max(NB * 19, 15 + NT)
        extp = P.sb("extp", [64, 4, EWMAX], F32); Bextp = P.buf("extp")
        ptmp = [P.sb(f"ptmp{i}", [64, EWMAX], F32) for i in range(2)]
        Bptmp = [P.buf(f"ptmp{i}") for i in range(2)]
        pooled = P.sb("pooled", [64, 4, NT], BF16); Bpooled = P.buf("pooled")
        ypool = P.sb("ypool", [64, 4, NT], BF16); Bypool = P.buf("ypool")
        cbuf = P.sb("cbuf", [128, 2, NT], F32); Bcb = P.buf("cb")
        cctmp = P.sb("cctmp", [128, NT], F32); Bcct = P.buf("cct")
        extc = P.sb("extc", [128, 2, NT + 2 * NB], F32); Bextc = P.buf("extc")
        ctmp = P.sb("ctmp", [128, NT], F32); Bctmp = P.buf("ctmp")
        yconv = P.sb("yconv", [128, 2, NT], BF16); Byconv = P.buf("yconv")
        kvf = P.sb("kvf", [128, 6, NT], F32); Bkvf = [P.buf(f"kvf{s}") for s in range(6)]
        ropet = P.sb("ropet", [128, 2, NT], F32); Bropet = P.buf("ropet")
        cosT = P.sb("cosT", [128, NT], F32); sinT = P.sb("sinT", [128, NT], F32); Bcs = P.buf("cossin")
        QTz = [P.sb(f"QTz{g}", [128, 4, NT], BF16) for g in range(2)]; BQT = P.buf("QT")
        for g in range(2):
            self.v("gpsimd", "memset", [], [BQT], QTz[g][:], 0.0)
        NG = max(NSUB, NB)
        gsig = P.sb("gsig", [128, NG, 24], F32); Bgsig = P.buf("gsig")
        ynsa = P.sb("ynsa", [128, 4, NT], BF16); Bynsa = P.buf("ynsa")
        macc = P.sb("macc", [128, NT], F32); Bmacc = P.buf("macc")
        ostg = [P.sb(f"ostg{i}", [128, NT], F32) for i in range(2)]
        Bostg = [P.buf(f"ostg{i}") for i in range(2)]
        sgate = [P.sb(f"sgate{i}", [128, NT], F32) for i in range(2)]
        Bsgate = [P.buf(f"sgate{i}") for i in range(2)]
        mtmp = P.sb("mtmp", [128, NT], F32); Bmtmp = P.buf("mtmp")
        exta = [P.sb(f"exta{i}", [128, NT + 2 * NB], F32) for i in range(2)]
        Bexta = [P.buf(f"exta{i}") for i in range(2)]
        atmp = [P.sb(f"atmp{i}", [128, NT], F32) for i in range(2)]
        Batmp = [P.buf(f"atmp{i}") for i in range(2)]
        gT = P.sb("gT", [128, 22, NT], BF16); BgT = P.buf("gT")
        poolpref = P.sb("poolpref", [64, L, 4, 15], F32); Bpoolpref = P.buf("poolpref")
        convpref = P.sb("convpref", [128, L, 2, 2], F32); Bconvpref = P.buf("convpref")
        ffnpref = P.sb("ffnpref", [128, L, 22, 2], F32); Bffnpref = P.buf("ffnpref")
        sffn = P.sb("sffn", [128, 22, NB, 2], F32); Bsffn = P.buf("sffn")
        ffnnew = P.sb("ffnnew", [128, 22, NB, 2], F32); Bffnnew = P.buf("ffnnew")
        spool_st = P.sb("spool_st", [64, 4, NB, 15], F32); Bspool_st = P.buf("spool_st")
        sconv_st = P.sb("sconv_st", [128, 2, NB, 2], F32); Bsconv_st = P.buf("sconv_st")
        self.v("gpsimd", "memset", [], [Bpoolpref], poolpref[:], 0.0)
        self.v("gpsimd", "memset", [], [Bconvpref], convpref[:], 0.0)
        self.v("gpsimd", "memset", [], [Bffnpref], ffnpref[:], 0.0)

        NWS = 512 // NT + 1
        NCC = cfg.NCH_P * 128
        TSP = cfg.PAST + 64
        NPG = cfg.NPG
        p_sizes = [SEQ, NKT * 130] + [NWS * NT, NWS * NSUB * 130, NCC, NCC, cfg.NCH_P * 130] * L + [4 * (16 + NT)]
        s_sizes = [5 * TSP, (NPG + 1) * 130, 516, 5 * 130, 1024, 1024]
        arena = P.sb("arena", [128, max(sum(p_sizes), sum(s_sizes))], BF16)
        self.ar_o = 0

        def carve(n):
            o = self.ar_o
            self.ar_o += n
            return arena[:, o:o + n]

        Kwk = carve(SEQ); BKwk = P.buf("Kwk")
        Vwk = carve(NKT * 130).rearrange("p (k g e) -> p k g e", g=2, e=65); BVwk = P.buf("Vwk")
        Bhk = [P.buf(f"hk{l}") for l in range(L)]
        Bhv = [P.buf(f"hv{l}") for l in range(L)]
        KwT, Vw, kcT, vcT, vcp = [], [], [], [], []
        for l in range(L):
            KwT.append(carve(NWS * NT).rearrange("p (w t) -> p w t", t=NT))
            Vw.append(carve(NWS * NSUB * 130).rearrange("p (k g e) -> p k g e", g=2, e=65))
            kcT.append(carve(NCC))
            vcT.append(carve(NCC))
            vcp.append(carve(cfg.NCH_P * 130).rearrange("p (k g e) -> p k g e", g=2, e=65))
        ctail = [P.sb(f"ctail{l}", [128, 2, 16], BF16) for l in range(L)]
        BKw = [P.buf() for l in range(L)]; BVw = [P.buf() for l in range(L)]
        Bkc = [P.buf() for l in range(L)]; BvcT = [P.buf() for l in range(L)]; Bvc = [P.buf() for l in range(L)]
        Bctail = [P.buf() for l in range(L)]
        self.v("gpsimd", "memset", [], [BKwk], Kwk[:], 0.0)
        self.v("gpsimd", "memset", [], [BVwk], Vwk[:], 0.0)
        self.v("gpsimd", "memset", [], [BVwk], Vwk[:, :, :, 64:65], 1.0)
        for l in range(L):
            self.v("gpsimd", "memset", [], [BKw[l]], KwT[l][:], 0.0)
            self.v("gpsimd", "memset", [], [BVw[l]], Vw[l][:], 0.0)
            self.v("gpsimd", "memset", [], [BVw[l]], Vw[l][:, :, :, 64:65], 1.0)
            self.v("gpsimd", "memset", [], [Bkc[l]], kcT[l][:], 0.0)
            self.v("gpsimd", "memset", [], [BvcT[l]], vcT[l][:], 0.0)
            self.v("gpsimd", "memset", [], [Bvc[l]], vcp[l][:], 0.0)
            self.v("gpsimd", "memset", [], [Bvc[l]], vcp[l][:, :, :, 64:65], 1.0)
            self.v("gpsimd", "memset", [], [Bctail[l]], ctail[l][:], 0.0)
        cext = carve(4 * (16 + NT)).rearrange("p (g s t) -> p g s t", g=2, s=2); Bcext = P.buf("cext")
        self.v("gpsimd", "memset", [], [Bcext], cext[:], 0.0)
        cgel = P.sb("cgel", [128, 2, 128], F32); Bcgel = P.buf("cgel")
        cgel2 = P.sb("cgel2", [128, 2, 128], F32); Bcgel2 = P.buf("cgel2")
        cgb = P.sb("cgb", [128, 2, 128], BF16); Bcgb = P.buf("cgb")
        cvec = P.sb("cvec", [128, 2], F32); Bcvec = P.buf("cvec")
        NPT = 4
        PT = [P.sb(f"PT{i}", [128, 512], BF16) for i in range(NPT)]
        BPT = [P.buf(f"PT{i}") for i in range(NPT)]
        self.pt_i = 0

        def next_pt():
            i = self.pt_i % NPT
            self.pt_i += 1
            return PT[i], BPT[i]

        osb = [P.sb(f"osb{i}", [65, 512], F32) for i in range(2)]
        Bosb = [P.buf(f"osb{i}") for i in range(2)]
        self.osb_i = 0
        otok = P.sb("otok", [128, 512], F32); Botok = P.buf("otok")
        otokb = P.sb("otokb", [128, 512], BF16); Botokb = P.buf("otokb")
        smal = P.sb("smal", [128, 64], F32); Bsmal = P.buf("smal")
        imp = P.sb("imp", [128, 64], F32); Bimp = P.buf("imp")
        score = P.sb("score", [128, 64], F32); Bscore = P.buf("score")
        scw = P.sb("scw", [128, 64], F32); Bscw = P.buf("scw")
        mx = P.sb("mx", [128, 16], F32); Bmx = P.buf("mx")
        selb = P.sb("selb", [128, 64], F32); Bselb = P.buf("selb")
        selbT = P.sb("selbT", [64, 128], BF16); BselbT = P.buf("selbT")
        scAB = P.sb("scAB", [128, 2, 64], F32); BscAB = P.buf("scAB")
        cmpb = P.sb("cmpb", [128, cfg.NCH_P, 128], BF16); Bcmpb = P.buf("cmpb")

        def rms_stats(nt):
            for kc in range(8):
                self.act(sq[:, kc, 0:nt], x[:, kc, 0:nt], AF.Square, [Bx], [Bsq])
            for kc in range(8):
                self.mm(PSM[:, 0:nt], ones_b[:], sq[:, kc, 0:nt], kc == 0, kc == 7, [Bones, Bsq], [BPSM])
            self.v("vector", "tensor_scalar", [BPSM], [Brstd], out=rstd[:, 0:nt], in0=PSM[:, 0:nt],
                   scalar1=1.0 / 1024.0, scalar2=1e-6, op0=ALU.mult, op1=ALU.add)
            self.act(rstd[:, 0:nt], rstd[:, 0:nt], AF.Sqrt, [Brstd], [Brstd])
            self.v("vector", "reciprocal", [Brstd], [Brstd], out=rstd[:, 0:nt], in_=rstd[:, 0:nt])

        def rmsnorm(g_ap_fn, Bg, nt):
            rms_stats(nt)
            for kc in range(8):
                self.v("vector", "scalar_tensor_tensor", [Bx, Brstd, Bg], [BhT], out=hT[:, kc, 0:nt], in0=x[:, kc, 0:nt],
                       scalar=g_ap_fn(kc), in1=rstd[:, 0:nt], op0=ALU.mult, op1=ALU.mult)

        def final_out(nt, dst_fn):
            rms_stats(nt)
            for kc in range(8):
                st, Bst = ostg[kc % 2], Bostg[kc % 2]
                self.v("vector", "scalar_tensor_tensor", [Bx, Brstd, Bgfin], [Bst], out=st[:, 0:nt], in0=x[:, kc, 0:nt],
                       scalar=gfin[:, kc:kc + 1], in1=rstd[:, 0:nt], op0=ALU.mult, op1=ALU.mult)
                P.dma(OQ, dst_fn(kc), st[:, 0:nt], reads=[Bst])

        def proj(l, key, M, nt, K=8):
            wt, Bw = wslot(l, key)
            ps, Bp = next_psa()
            for kc in range(K):
                self.mm(ps[0:M, 0:nt], wt[:, kc, 0:M], hT[:, kc, 0:nt], kc == 0, kc == K - 1, [Bw, BhT], [Bp])
            return ps, Bp

        def rope_apply(src_ps, Bsrc, dst_f32, dst_bf, Bdst_list, nt, cos_ap, sin_ap, Bcos, rows=slice(0, 128), ceng="gpsimd"):
            r = rows
            self.act(ropet[r, 0, 0:nt], src_ps, AF.Copy, [Bsrc], [Bropet])
            self.mm(PSM[r, 0:nt], rot[r, r], ropet[r, 0, 0:nt], True, True, [Brot, Bropet], [BPSM])
            self.v("vector", "tensor_tensor", [BPSM, Bcos], [Bropet], out=ropet[r, 1, 0:nt], in0=PSM[r, 0:nt], in1=sin_ap, op=ALU.mult)
            self.v(ceng, "tensor_tensor", [Bropet, Bcos], [Bropet], out=ropet[r, 0, 0:nt], in0=ropet[r, 0, 0:nt], in1=cos_ap, op=ALU.mult)
            if isinstance(dst_bf, tuple):
                for hh, d_ in enumerate(dst_bf):
                    rr = slice(hh * 64, (hh + 1) * 64)
                    self.v("vector", "tensor_tensor", [Bropet], Bdst_list, out=d_, in0=ropet[rr, 0, 0:nt], in1=ropet[rr, 1, 0:nt], op=ALU.add)
                return
            dst = dst_f32 if dst_f32 is not None else dst_bf
            self.v("vector", "tensor_tensor", [Bropet], Bdst_list, out=dst, in0=ropet[r, 0, 0:nt], in1=ropet[r, 1, 0:nt], op=ALU.add)

        def layer_tile(l, nt, nseq, S, prompt, ti):
            PP, PC = 15, 2
            rmsnorm(lambda kc: gmix[:, l, kc:kc + 1], Bgmix, nt)
            EW = PP + S
            extv = extp[:, :, 0:nseq * EW].rearrange("p g (b e) -> p g b e", e=EW)
            if prompt:
                self.v("gpsimd", "tensor_copy", [Bpoolpref], [Bextp], out=extv[:, :, 0, 0:PP], in_=poolpref[:, l, :, :])
            else:
                P.dma("sync", spool_st[:].rearrange("p g b e -> p (g b e)"), d_spool[l], writes=[Bspool_st])
                self.v("gpsimd", "tensor_copy", [Bspool_st], [Bextp], out=extv[:, :, :, 0:PP], in_=spool_st[:])
            for g in range(4):
                ps, Bp = proj(l, ("pool", g), 64, nt)
                self.act(extv[:, g, :, PP:PP + S], ps[0:64, 0:nt].rearrange("p (b s) -> p b s", s=S), AF.Copy, [Bp], [Bextp])
            if prompt:
                self.v("gpsimd", "tensor_copy", [Bextp], [Bpoolpref], out=poolpref[:, l, :, :], in_=extv[:, :, 0, S:S + PP])
                if ti == cfg.NTILE - 1:
                    P.dma(OQ, o_poolp[l], poolpref[:, l, :, :].rearrange("p g e -> p (g e)"), reads=[Bpoolpref])
            else:
                self.v("gpsimd", "tensor_copy", [Bextp], [Bspool_st], out=spool_st[:], in_=extv[:, :, :, S:S + PP])
                P.dma(OQ, o_pools[l], spool_st[:].rearrange("p g b e -> p (g b e)"), reads=[Bspool_st])
            for g in range(4):
                cur = extv[:, g, :, :]
                off = 0
                width = EW
                for step in range(g + 1):
                    sh = 1 << step
                    t_i = step % 2
                    nw = width - sh
                    dst = ptmp[t_i][:, 0:nseq * nw].rearrange("p (b e) -> p b e", e=nw)
                    self.v("vector", "tensor_tensor", [Bextp, Bptmp[1 - t_i]] if step else [Bextp], [Bptmp[t_i]],
                           out=dst, in0=cur[:, :, sh:width], in1=cur[:, :, 0:nw], op=ALU.add)
                    cur = dst
                    off += sh
                    width = nw
                w = 2 << g
                j0 = PP - off
                self.v("vector", "scalar_tensor_tensor", [Bptmp[g % 2], Bextp], [Bpooled],
                       out=pooled[:, g, 0:nt].rearrange("p (b s) -> p b s", s=S), in0=cur[:, :, j0:j0 + S], scalar=1.0 / w,
                       in1=extv[:, g, :, PP:PP + S], op0=ALU.mult, op1=ALU.subtract)
                if prompt and ti == 0:
                    self.v("vector", "tensor_tensor", [Bptmp[g % 2], Brcnt], [Bptmp[g % 2]], out=cur[:, 0, j0:j0 + 16],
                           in0=cur[:, 0, j0:j0 + 16], in1=rcnt[:, g, :], op=ALU.mult)
                    self.v("vector", "tensor_tensor", [Bptmp[g % 2], Bextp], [Bpooled], out=pooled[:, g, 0:16],
                           in0=cur[:, 0, j0:j0 + 16], in1=extv[:, g, 0, PP:PP + 16], op=ALU.subtract)
            wpw, Bwpw = wslot(l, ("pool_w", 0))
            for g in range(4):
                ps, Bp = next_psa()
                self.mm(ps[0:64, 0:nt], wpw[0:64, g, 0:64], pooled[:, g, 0:nt], True, True, [Bwpw, Bpooled], [Bp])
                self.v("vector", "tensor_scalar", [Bp, Bpscale], [Bypool], out=ypool[:, g, 0:nt], in0=ps[0:64, 0:nt],
                       scalar1=pscale[:, l, g:g + 1], scalar2=None, op0=ALU.mult)
            CW = PC + S
            for j in range(2):
                ps, Bp = proj(l, ("cb", j), 128, nt)
                self.act(cbuf[:, j, 0:nt], ps[:, 0:nt], AF.Copy, [Bp], [Bcb])
            extcv = extc[:, :, 0:nseq * CW].rearrange("p j (b e) -> p j b e", e=CW)
            if prompt:
                self.v("gpsimd", "tensor_copy", [Bconvpref], [Bextc], out=extcv[:, :, 0, 0:PC], in_=convpref[:, l, :, :])
            else:
                P.dma("sync", sconv_st[:].rearrange("p j b e -> p (j b e)"), d_sconv[l], writes=[Bsconv_st])
                self.v("gpsimd", "tensor_copy", [Bsconv_st], [Bextc], out=extcv[:, :, :, 0:PC], in_=sconv_st[:])
            for j in range(2):
                ps, Bp = proj(l, ("cc", j), 128, nt)
                self.act(cctmp[:, 0:nt], ps[:, 0:nt], AF.Copy, [Bp], [Bcct])
                ps2, Bp2 = proj(l, ("cx", j), 128, nt)
                self.v("vector", "tensor_tensor", [Bp2, Bcct], [Bextc], out=extcv[:, j, :, PC:PC + S],
                       in0=ps2[:, 0:nt].rearrange("p (b s) -> p b s", s=S), in1=cctmp[:, 0:nt].rearrange("p (b s) -> p b s", s=S), op=ALU.mult)
            if prompt:
                self.v("gpsimd", "tensor_copy", [Bextc], [Bconvpref], out=convpref[:, l, :, :], in_=extcv[:, :, 0, S:S + PC])
                if ti == cfg.NTILE - 1:
                    P.dma(OQ, o_convp[l], convpref[:, l, :, :].rearrange("p j e -> p (j e)"), reads=[Bconvpref])
            else:
                self.v("gpsimd", "tensor_copy", [Bextc], [Bsconv_st], out=sconv_st[:], in_=extcv[:, :, :, S:S + PC])
                P.dma(OQ, o_convs[l], sconv_st[:].rearrange("p j b e -> p (j b e)"), reads=[Bsconv_st])
            for j in range(2):
                cv = ctmp[:, 0:nt].rearrange("p (b s) -> p b s", s=S)
                self.v("vector", "tensor_scalar", [Bextc, Bconvw], [Bctmp], out=cv, in0=extcv[:, j, :, 0:S],
                       scalar1=convw[:, l, j, 0:1], scalar2=None, op0=ALU.mult)
                for k in (1, 2):
                    self.v("vector", "scalar_tensor_tensor", [Bextc, Bconvw, Bctmp], [Bctmp], out=cv, in0=extcv[:, j, :, k:k + S],
                           scalar=convw[:, l, j, k:k + 1], in1=cv, op0=ALU.mult, op1=ALU.add)
                self.v("vector", "tensor_tensor", [Bctmp, Bcb], [Byconv], out=yconv[:, j, 0:nt], in0=ctmp[:, 0:nt], in1=cbuf[:, j, 0:nt], op=ALU.mult)
            if prompt:
                t0 = ti * NT
                P.dma("sync", cosT[:, 0:nt], dC["cos_p"][:, t0:t0 + nt], writes=[Bcs])
                P.dma("sync", sinT[:, 0:nt], dC["sin_p"][:, t0:t0 + nt], writes=[Bcs])
                cos_ap, sin_ap, Bcos = cosT[:, 0:nt], sinT[:, 0:nt], Bcs
            else:
                cos_ap, sin_ap, Bcos = cos_s[:, 0:nt], sin_s[:, 0:nt], Bcs1
            for s in range(6):
                ps, Bp = proj(l, ("kv", s), 128, nt)
                if s in (2, 4):
                    rope_apply(ps[:, 0:nt], Bp, kvf[:, s, 0:nt], None, [Bkvf[s]], nt, cos_ap, sin_ap, Bcos)
                else:
                    self.act(kvf[:, s, 0:nt], ps[:, 0:nt], AF.Copy, [Bp], [Bkvf[s]])
            if prompt:
                t0 = ti * NT
                for s in range(4):
                    P.dma(OQ, o_kvp[l, s, :, t0:t0 + nt], kvf[:, s, 0:nt], reads=[Bkvf[s]])
                if t0 >= SEQ - 512:
                    w0 = t0 - (SEQ - 512)
                    for s in range(2):
                        P.dma(OQ, o_winp[l, s, :, w0:w0 + nt], kvf[:, 4 + s, 0:nt], reads=[Bkvf[4 + s]])
            else:
                for s in range(4):
                    P.dma(OQ, o_kvs[l, s], kvf[:, s, 0:nt], reads=[Bkvf[s]])
                for s in range(2):
                    P.dma(OQ, o_wins_new[l, s], kvf[:, 4 + s, 0:nt], reads=[Bkvf[4 + s]])
            for r in range(4):
                ps, Bp = proj(l, ("q", r), 128, nt)
                rope_apply(ps[:, 0:nt], Bp, None, (QTz[0][0:64, r, 0:nt], QTz[1][64:128, r, 0:nt]), [BQT], nt, cos_ap, sin_ap, Bcos)
            wg, Bwg = wslot(l, ("ng", 0))
            if prompt:
                for sub in range(NSUB):
                    for kc in range(8):
                        self.mm(PSM[:, 0:24], hT[:, kc, sub * 128:(sub + 1) * 128], wg[:, kc, 0:24], kc == 0, kc == 7, [BhT, Bwg], [BPSM])
                    self.act(gsig[:, sub, :], PSM[:, 0:24], AF.Sigmoid, [BPSM], [Bgsig])
            else:
                for b in range(NB):
                    for kc in range(8):
                        self.mm(PSM[0:4, b * 24:(b + 1) * 24], hT[:, kc, b * 4:(b + 1) * 4], wg[:, kc, 0:24], kc == 0, kc == 7, [BhT, Bwg], [BPSM])
                self.act(gsig[0:4, 0:NB, :], PSM[0:4, 0:NB * 24].rearrange("p (b c) -> p b c", c=24), AF.Sigmoid, [BPSM], [Bgsig])
            if cfg.nsa:
                if prompt:
                    nsa_prompt(l, ti)
                else:
                    nsa_sample(l)
            else:
                self.v("gpsimd", "memset", [], [Bynsa], ynsa[:], 0.0)
            for m in range(8):
                for bi, gk in enumerate(("gp", "gn", "gc")):
                    ps, Bp = proj(l, (gk, m), 128, nt)
                    sg, Bsg = sgate[bi % 2], Bsgate[bi % 2]
                    self.act(sg[:, 0:nt], ps[:, 0:nt], AF.Sigmoid, [Bp], [Bsg])
                    ps2, Bp2 = next_psa()
                    if bi == 0:
                        wb, Bwb = wslot(l, ("brp", m))
                        for g in range(4):
                            self.mm(ps2[:, 0:nt], wb[0:64, g, :], ypool[:, g, 0:nt], g == 0, g == 3, [Bwb, Bypool], [Bp2])
                        self.v("vector", "tensor_tensor", [Bp2, Bsg], [Bmacc], out=macc[:, 0:nt], in0=ps2[:, 0:nt], in1=sg[:, 0:nt], op=ALU.mult)
                    elif bi == 1:
                        wb, Bwb = wslot(l, ("brnc", m))
                        for k in range(4):
                            self.mm(ps2[:, 0:nt], wb[:, k, :], ynsa[:, k, 0:nt], k == 0, k == 3, [Bwb, Bynsa], [Bp2])
                        self.v("vector", "tensor_tensor", [Bp2, Bsg], [Bmtmp], out=mtmp[:, 0:nt], in0=ps2[:, 0:nt], in1=sg[:, 0:nt], op=ALU.mult)
                        self.v("gpsimd", "tensor_tensor", [Bmtmp, Bmacc], [Bmacc], out=macc[:, 0:nt], in0=macc[:, 0:nt], in1=mtmp[:, 0:nt], op=ALU.add)
                    else:
                        for k in range(2):
                            self.mm(ps2[:, 0:nt], wb[:, 4 + k, :], yconv[:, k, 0:nt], k == 0, k == 1, [Bwb, Byconv], [Bp2])
                        self.v("vector", "tensor_tensor", [Bp2, Bsg], [Bmtmp], out=mtmp[:, 0:nt], in0=ps2[:, 0:nt], in1=sg[:, 0:nt], op=ALU.mult)
                        self.v("gpsimd", "tensor_tensor", [Bmtmp, Bmacc], [Bsq], out=sq[:, m, 0:nt], in0=macc[:, 0:nt], in1=mtmp[:, 0:nt], op=ALU.add)
            for m in range(8):
                wt, Bw = wslot(l, ("wo", m))
                ps, Bp = next_psa()
                for kc in range(8):
                    self.mm(ps[:, 0:nt], wt[:, kc, :], sq[:, kc, 0:nt], kc == 0, kc == 7, [Bw, Bsq], [Bp])
                self.v("vector", "tensor_tensor", [Bp, Bx], [Bx], out=x[:, m, 0:nt], in0=x[:, m, 0:nt], in1=ps[:, 0:nt], op=ALU.add)
            rmsnorm(lambda kc: gffn[:, l, kc:kc + 1], Bgffn, nt)
            AW = 2 + S
            if not prompt:
                P.dma("sync", sffn[:].rearrange("p f b e -> p (f b e)"), d_sffn[l], writes=[Bsffn])
            for f in range(22):
                ea, Bea = exta[f % 2], Bexta[f % 2]
                eav = ea[:, 0:nseq * AW].rearrange("p (b e) -> p b e", e=AW)
                if prompt:
                    self.v("gpsimd", "tensor_copy", [Bffnpref], [Bea], out=eav[:, 0, 0:2], in_=ffnpref[:, l, f, :])
                else:
                    self.v("gpsimd", "tensor_copy", [Bsffn], [Bea], out=eav[:, :, 0:2], in_=sffn[:, f, :, :])
                ps, Bp = proj(l, ("fa", f), 128, nt)
                self.act(eav[:, :, 2:2 + S], ps[:, 0:nt].rearrange("p (b s) -> p b s", s=S), AF.Copy, [Bp], [Bea])
                if prompt:
                    self.v("gpsimd", "tensor_copy", [Bea], [Bffnpref], out=ffnpref[:, l, f, :], in_=eav[:, 0, S:S + 2])
                else:
                    self.v("gpsimd", "tensor_copy", [Bea], [Bffnnew], out=ffnnew[:, f, :, :], in_=eav[:, :, S:S + 2])
                at, Bat = atmp[f % 2], Batmp[f % 2]
                av = at[:, 0:nt].rearrange("p (b s) -> p b s", s=S)
                self.v("vector", "tensor_scalar", [Bea, Bffnw], [Bat], out=av, in0=eav[:, :, 0:S], scalar1=ffnw[:, l, f, 0:1], scalar2=None, op0=ALU.mult)
                for k in (1, 2):
                    self.v("vector", "scalar_tensor_tensor", [Bea, Bffnw, Bat], [Bat], out=av, in0=eav[:, :, k:k + S],
                           scalar=ffnw[:, l, f, k:k + 1], in1=av, op0=ALU.mult, op1=ALU.add)
                self.act(at[:, 0:nt], at[:, 0:nt], AF.Silu, [Bat], [Bat])
                ps2, Bp2 = proj(l, ("fb", f), 128, nt)
                self.v("vector", "tensor_tensor", [Bp2, Bat], [BgT], out=gT[:, f, 0:nt], in0=ps2[:, 0:nt], in1=at[:, 0:nt], op=ALU.mult)
            if prompt:
                if ti == cfg.NTILE - 1:
                    P.dma(OQ, o_ffnp[l], ffnpref[:, l, :, :].rearrange("p f e -> p (f e)"), reads=[Bffnpref])
            else:
                P.dma(OQ, o_ffns[l], ffnnew[:].rearrange("p f b e -> p (f b e)"), reads=[Bffnnew])
            for m in range(8):
                ps, Bp = next_psa()
                for part in range(3):
                    wt, Bw = wslot(l, ("fd", m, part))
                    kcs = list(range(part * 8, min(22, part * 8 + 8)))
                    for i, kc in enumerate(kcs):
                        self.mm(ps[:, 0:nt], wt[:, i, :], gT[:, kc, 0:nt], kc == 0, kc == 21, [Bw, BgT], [Bp])
                self.v("vector", "tensor_tensor", [Bp, Bx], [Bx], out=x[:, m, 0:nt], in0=x[:, m, 0:nt], in1=ps[:, 0:nt], op=ALU.add)

        def compress(l, src_fn, col0, nblk, Bsrc, kc_dst_fn, vc_dst_fn, Bkcd, Bvcd, cosc, sinc, Bcc, ceng="gpsimd"):
            for s in range(2):
                w1 = [wslot(l, ("w1", s, q4)) for q4 in range(4)]
                w2, Bw2 = wslot(l, ("w2", 0))
                for li in range(32):
                    wt, Bw = w1[li // 8]
                    self.mm(PSM[:, 0:1], wt[0:64, li % 8, :], peT[:, l, s, li:li + 1], li == 0, li == 31, [Bw, BpeT], [BPSM])
                self.act(cvec[:, s:s + 1], PSM[:, 0:1], AF.Copy, [BPSM], [Bcvec])
                ps, Bp = next_psa()
                for g in range(2):
                    for li in range(32):
                        wt, Bw = w1[li // 8]
                        self.mm(ps[:, g * 128:g * 128 + nblk], wt[:, li % 8, :], src_fn(s, g, li), li == 0, li == 31,
                                [Bw, Bsrc], [Bp])
                pv = ps[:, 0:256].rearrange("p (g n) -> p g n", n=128)[:, :, 0:nblk]
                xg = cgel[:, :, 0:nblk]
                x2 = cgel2[:, :, 0:nblk]
                self.act(xg, pv, AF.Identity, [Bp, Bcvec], [Bcgel], bias=cvec[:, s:s + 1])
                self.act(x2, xg, AF.Square, [Bcgel], [Bcgel2])
                self.v("vector", "tensor_scalar", [Bcgel2], [Bcgel2], out=x2, in0=x2, scalar1=0.044715, scalar2=1.0, op0=ALU.mult, op1=ALU.add)
                self.v("vector", "tensor_tensor", [Bcgel2, Bcgel], [Bcgel2], out=x2, in0=x2, in1=xg, op=ALU.mult)
                self.act(x2, x2, AF.Sigmoid, [Bcgel2], [Bcgel2], scale=1.5957691216057308)
                self.v("vector", "tensor_tensor", [Bcgel2, Bcgel], [Bcgb], out=cgb[:, :, 0:nblk], in0=x2, in1=xg, op=ALU.mult)
                for g in range(2):
                    r = slice(g * 64, (g + 1) * 64)
                    self.mm(PST[:, g * 128:g * 128 + nblk], w2[:, s, :], cgb[:, g, 0:nblk], True, True, [Bw2, Bcgb], [BPST])
                    if s == 0:
                        rope_apply(PST[r, g * 128:g * 128 + nblk], BPST, None, kc_dst_fn(g), [Bkcd], nblk,
                                   cosc[r, col0:col0 + nblk], sinc[r, col0:col0 + nblk], Bcc, rows=r, ceng=ceng)
                    else:
                        self.act(vc_dst_fn(g), PST[r, g * 128:g * 128 + nblk], AF.Copy, [BPST], [Bvcd])

        def attend(q_rhs, nq, key_tiles, BQ):
            pso, Bpso = next_pso()
            nkt = len(key_tiles)
            pts = []
            for i, kt in enumerate(key_tiles):
                pss, Bpss = next_pss()
                M = kt["M"]
                nm = len(kt["masks"])
                self.mm(pss[0:M, 0:nq], kt["k"], q_rhs, True, nm == 0, [kt["Bk"], BQ], [Bpss])
                for mi, (ml, mr, mreads) in enumerate(kt["masks"]):
                    self.mm(pss[0:M, 0:nq], ml, mr, False, mi == nm - 1, mreads, [Bpss])
                pt, Bpt = next_pt()
                self.act(pt[0:M, 0:nq], pss[0:M, 0:nq], AF.Exp, [Bpss], [Bpt], scale=0.125)
                self.mm(pso[0:65, 0:nq], kt["v"], pt[0:M, 0:nq], i == 0, i == nkt - 1, [kt["Bv"], Bpt], [Bpso])
                pts.append((pt, Bpt, M))
            return pso, Bpso, pts

        def combine(pso, Bpso, nqtok, nr, branch, g, sub, first):
            ob, Bob = osb[self.osb_i % 2], Bosb[self.osb_i % 2]
            self.osb_i += 1
            nq = nr * nqtok
            self.act(ob[:, 0:nq], pso[0:65, 0:nq], AF.Copy, [Bpso], [Bob])
            for r in range(nr):
                self.tr(PST[0:nqtok, r * 65:(r + 1) * 65], ob[:, r * nqtok:(r + 1) * nqtok], identf[0:65, 0:65], [Bob, Bidentf], [BPST])
            tv = PST[0:nqtok, 0:nr * 65].rearrange("p (r e) -> p r e", e=65)
            self.v("vector", "tensor_scalar", [BPST], [Bsmal], out=smal[0:nqtok, 0:nr], in0=tv[:, :, 64], scalar1=1e-30, scalar2=None, op0=ALU.max)
            self.v("vector", "reciprocal", [Bsmal], [Bsmal], out=smal[0:nqtok, 0:nr], in_=smal[0:nqtok, 0:nr])
            c0 = branch * 8 + 4 * g
            self.v("vector", "tensor_tensor", [Bsmal, Bgsig], [Bsmal], out=smal[0:nqtok, 8:8 + nr], in0=smal[0:nqtok, 0:nr],
                   in1=gsig[0:nqtok, sub, c0:c0 + nr], op=ALU.mult)
            for r in range(nr):
                h = 4 * g + r
                if first:
                    self.v("vector", "tensor_scalar", [BPST, Bsmal], [Botok], out=otok[0:nqtok, h * 64:(h + 1) * 64], in0=tv[:, r, 0:64],
                           scalar1=smal[0:nqtok, 8 + r:9 + r], scalar2=None, op0=ALU.mult)
                else:
                    self.v("vector", "scalar_tensor_tensor", [BPST, Bsmal, Botok], [Botok], out=otok[0:nqtok, h * 64:(h + 1) * 64], in0=tv[:, r, 0:64],
                           scalar=smal[0:nqtok, 8 + r:9 + r], in1=otok[0:nqtok, h * 64:(h + 1) * 64], op0=ALU.mult, op1=ALU.add)

        def importance(pts, cover_fn, Bcov, nqtok, nr):
            first = True
            n = len(pts)
            for ci, (pt, Bpt, M) in enumerate(pts):
                for r in range(nr):
                    self.mm(PST[0:nqtok, r * 65:(r + 1) * 65], pt[0:M, r * nqtok:(r + 1) * nqtok], cover_fn(ci, M), first, (ci == n - 1) and (r == nr - 1),
                            [Bpt, Bcov], [BPST], skip=True)
                    first = False
            tv = PST[0:nqtok, 0:nr * 65].rearrange("p (r e) -> p r e", e=65)
            self.v("vector", "tensor_scalar", [BPST], [Bsmal], out=smal[0:nqtok, 16:16 + nr], in0=tv[:, :, 64], scalar1=1e-30, scalar2=None, op0=ALU.max)
            self.v("vector", "reciprocal", [Bsmal], [Bsmal], out=smal[0:nqtok, 16:16 + nr], in_=smal[0:nqtok, 16:16 + nr])
            for r in range(nr):
                if r == 0:
                    self.v("vector", "tensor_scalar", [BPST, Bsmal], [Bimp], out=imp[0:nqtok, :], in0=tv[:, r, 0:64], scalar1=smal[0:nqtok, 16:17], scalar2=None, op0=ALU.mult)
                else:
                    self.v("vector", "scalar_tensor_tensor", [BPST, Bsmal, Bimp], [Bimp], out=imp[0:nqtok, :], in0=tv[:, r, 0:64],
                           scalar=smal[0:nqtok, 16 + r:17 + r], in1=imp[0:nqtok, :], op0=ALU.mult, op1=ALU.add)

        def select_blocks(A_ap, B_ap, BAB, nqtok, nsel):
            self.v("vector", "tensor_tensor", [Bimp, BAB], [Bscore], out=score[0:nqtok, :], in0=imp[0:nqtok, :], in1=A_ap, op=ALU.mult)
            self.v("vector", "tensor_tensor", [Bscore, BAB], [Bscore], out=score[0:nqtok, :], in0=score[0:nqtok, :], in1=B_ap, op=ALU.add)
            if nsel > 16:
                self.v("vector", "max", [Bscore], [Bmx], out=mx[0:nqtok, 0:8], in_=score[0:nqtok, :])
                self.v("vector", "match_replace", [Bscore, Bmx], [Bscw], out=scw[0:nqtok, :], in_to_replace=mx[0:nqtok, 0:8], in_values=score[0:nqtok, :], imm_value=-1e30)
                self.v("vector", "max", [Bscw], [Bmx], out=mx[0:nqtok, 8:16], in_=scw[0:nqtok, :])
                self.v("vector", "tensor_scalar", [Bmx], [Bmx], out=mx[0:nqtok, 15:16], in0=mx[0:nqtok, 15:16], scalar1=0.0, scalar2=None, op0=ALU.max)
            else:
                self.v("vector", "memset", [], [Bmx], mx[0:nqtok, 15:16], 0.0)
            self.v("vector", "tensor_scalar", [Bscore, Bmx], [Bselb], out=selb[0:nqtok, 0:64], in0=score[0:nqtok, :], scalar1=mx[0:nqtok, 15:16], scalar2=None,
                   op0=ALU.is_lt)
            self.v("vector", "tensor_scalar", [Bselb], [Bselb], out=selb[0:nqtok, 0:64], in0=selb[0:nqtok, 0:64], scalar1=NEG, scalar2=None, op0=ALU.mult)
            self.tr(PSM[0:64, 0:nqtok], selb[0:nqtok, 0:64], identf[0:nqtok, 0:nqtok], [Bselb, Bidentf], [BPSM])
            self.act(selbT[:, 0:nqtok], PSM[0:64, 0:nqtok], AF.Copy, [BPSM], [BselbT])

        def nsa_prompt(l, ti):
            t0 = ti * NT
            if ti > 0:
                P.dma("sync", Kwk[:, 0:t0], hist_k[l, :, 0:t0], reads=[Bhk[l]], writes=[BKwk])
                P.dma("sync", Vwk[:, 0:ti * NSUB, :, :].rearrange("p k g e -> p (k g e)"), hist_v[l, :, 0:ti * NSUB * 130], reads=[Bhv[l]], writes=[BVwk])
            self.v("gpsimd", "tensor_copy", [Bkvf[2]], [BKwk], out=Kwk[:, t0:t0 + NT], in_=kvf[:, 2, :])
            wsl = ti % NWS
            self.v("gpsimd", "tensor_copy", [Bkvf[4]], [BKw[l]], out=KwT[l][:, wsl, :], in_=kvf[:, 4, :])
            for sub in range(NSUB):
                for (s, dst, Bd, idx) in ((3, Vwk, BVwk, ti * NSUB + sub), (5, Vw[l], BVw[l], wsl * NSUB + sub)):
                    self.tr(PST[:, 0:128], kvf[:, s, sub * 128:(sub + 1) * 128], identf[:], [Bkvf[s], Bidentf], [BPST])
                    self.act(dst[:, idx, :, 0:64], PST[:, 0:128].rearrange("p (g d) -> p g d", d=64), AF.Copy, [BPST], [Bd])
            if ti < cfg.NTILE - 1:
                P.dma(OQ, hist_k[l, :, t0:t0 + NT], Kwk[:, t0:t0 + NT], reads=[BKwk], writes=[Bhk[l]])
                P.dma(OQ, hist_v[l, :, ti * NSUB * 130:(ti + 1) * NSUB * 130],
                      Vwk[:, ti * NSUB:(ti + 1) * NSUB, :, :].rearrange("p k g e -> p (k g e)"), reads=[BVwk], writes=[Bhv[l]])
            nb_t = NT // 16
            for g in range(2):
                r = slice(g * 64, (g + 1) * 64)
                self.v("gpsimd", "tensor_copy", [Bctail[l]], [Bcext], out=cext[r, g, :, 0:16], in_=ctail[l][r, :, :])
                for s in range(2):
                    self.v("gpsimd" if g else "vector", "tensor_copy", [Bkvf[s]], [Bcext], out=cext[r, g, s, 16:16 + NT], in_=kvf[r, s, :])
                self.v("gpsimd", "tensor_copy", [Bcext], [Bctail[l]], out=ctail[l][r, :, :], in_=cext[r, g, :, NT:NT + 16])
            c0 = nb_t * ti
            ch = c0 // 128

            def src_fn(s, g, li):
                a = cext[:, g, s, li:li + 1]
                return bass.AP(a.tensor, a.offset, [list(a.ap[0]), [16, nb_t]])

            compress(l, src_fn, c0, nb_t, Bcext,
                     lambda g: kcT[l][g * 64:(g + 1) * 64, c0:c0 + nb_t],
                     lambda g: vcT[l][g * 64:(g + 1) * 64, c0:c0 + nb_t],
                     Bkc[l], BvcT[l], cosc_p, sinc_p, Bcc1)
            self.tr(PSTb[:, 0:128], vcT[l][:, ch * 128:(ch + 1) * 128], identb[:], [BvcT[l], Bidentb], [BPST])
            self.act(vcp[l][:, ch, :, 0:64], PSTb[:, 0:128].rearrange("p (g d) -> p g d", d=64), AF.Copy, [BPST], [Bvc[l]])
            nch = ch + 1
            for sub in range(NSUB):
                qi = ti * NSUB + sub
                P.dma("sync", scAB[:, 0, :], dC["scA"][qi], writes=[BscAB])
                P.dma("sync", scAB[:, 1, :], dC["scB"][qi], writes=[BscAB])
                P.dma("gpsimd", cmpb[:], dC["cmpb"][qi], writes=[Bcmpb])
                for g in range(2):
                    r = slice(g * 64, (g + 1) * 64)
                    qa = QTz[g][:, :, sub * 128:(sub + 1) * 128]
                    kts = []
                    for c in range(nch):
                        kts.append(dict(k=kcT[l][:, c * 128:(c + 1) * 128], Bk=Bkc[l], M=128, v=vcp[l][:, c, g, :], Bv=Bvc[l],
                                        masks=[(identb[:], bc_mid(cmpb[:, c, :], 4), [Bidentb, Bcmpb])]))
                    pso, Bpso, pts = attend(qa, 512, kts, BQT)
                    importance(pts, lambda ci, M: cover_p[0:M, ci, :], Bcover_p, 128, 4)
                    combine(pso, Bpso, 128, 4, 0, g, sub, True)
                    select_blocks(scAB[:, 0, :], scAB[:, 1, :], BscAB, 128, cfg.NSEL_P)
                    kts = []
                    for kt in range(qi + 1):
                        masks = [(eexp[:, kt * 128:(kt + 1) * 128], bc_mid(selbT[:, 0:128], 4), [Beexp, BselbT])]
                        if kt == qi:
                            masks.append((identb[:], bc_mid(tri[:], 4), [Bidentb, Btri]))
                        kts.append(dict(k=Kwk[:, kt * 128:(kt + 1) * 128], Bk=BKwk, M=128, v=Vwk[:, kt, g, :], Bv=BVwk, masks=masks))
                    pso, Bpso, pts = attend(qa, 512, kts, BQT)
                    combine(pso, Bpso, 128, 4, 1, g, sub, False)
                    kts = []
                    for kt in range(max(0, qi - 4), qi + 1):
                        masks = []
                        if kt == qi - 4:
                            masks.append((identb[:], bc_mid(winlo[:], 4), [Bidentb, Bwinlo]))
                        if kt == qi:
                            masks.append((identb[:], bc_mid(tri[:], 4), [Bidentb, Btri]))
                        ws_, wsub = (kt // NSUB) % NWS, kt % NSUB
                        kts.append(dict(k=KwT[l][:, ws_, wsub * 128:(wsub + 1) * 128], Bk=BKw[l], M=128, v=Vw[l][:, ws_ * NSUB + wsub, g, :], Bv=BVw[l], masks=masks))
                    pso, Bpso, pts = attend(qa, 512, kts, BQT)
                    combine(pso, Bpso, 128, 4, 2, g, sub, False)
                self.v("gpsimd", "tensor_copy", [Botok], [Botokb], out=otokb[:], in_=otok[:])
                for jj in range(4):
                    self.tr(PSMb[:, jj * 128:(jj + 1) * 128], otokb[:, jj * 128:(jj + 1) * 128], identb[:], [Botokb, Bidentb], [BPSM])
                self.act(ynsa[:, :, sub * 128:(sub + 1) * 128], PSMb[:, 0:512].rearrange("p (j q) -> p j q", q=128), AF.Copy, [BPSM], [Bynsa])

        NKA = 2
        self.ar_o = 0
        KA = [P.sb("KA0", [128, 5, TSP], BF16), carve(5 * TSP).rearrange("p (s t) -> p s t", t=TSP)]
        BKA = [P.buf(f"KA{i}") for i in range(NKA)]
        VB = [P.sb("VB0", [128, NPG + 1, 2, 65], BF16), carve((NPG + 1) * 130).rearrange("p (k g e) -> p k g e", g=2, e=65)]
        BVB = [P.buf(f"VB{i}") for i in range(NKA)]
        KWs = [P.sb("KWs0", [128, 512 + 4], BF16), carve(516)]
        BKWs = [P.buf(f"KWs{i}") for i in range(NKA)]
        VWs = [P.sb("VWs0", [128, 5, 2, 65], BF16), carve(5 * 130).rearrange("p (k g e) -> p k g e", g=2, e=65)]
        BVWs = [P.buf(f"VWs{i}") for i in range(NKA)]
        NSTG = 2
        stgA = [P.sb(f"stgA{i}", [128, 384], F32) for i in range(NSTG)]
        BstgA = [P.buf(f"stgA{i}") for i in range(NSTG)]
        stgB = [P.sb(f"stgB{i}", [128, 128], F32) for i in range(NSTG)]
        BstgB = [P.buf(f"stgB{i}") for i in range(NSTG)]
        stgW = carve(1024).bitcast(F32); BstgW = P.buf("stgW")
        stgV = carve(1024).bitcast(F32).rearrange("p (k c) -> p k c", c=128); BstgV = P.buf("stgV")
        kcTs = P.sb("kcTs", [128, 128], BF16); BkcTs = P.buf("kcTs")
        vcTs = P.sb("vcTs", [128, 128], BF16); BvcTs = P.buf("vcTs")
        vcs = P.sb("vcs", [128, 2, 65], BF16); Bvcs = P.buf("vcs")
        vnew = P.sb("vnew", [4, 2, 2, 65], BF16); Bvnew = P.buf("vnew")
        idxf = P.sb("idxf", [128, NB * NPG], F32); Bidxf = P.buf("idxf")
        idxi = P.sb("idxi", [128, NB * NPG], I32); Bidxi = P.buf("idxi")
        pti = P.sb("pti", [128, NB * NPG], I32); Bpti = P.buf("pti")
        self.v("gpsimd", "memset", [], [Bvcs], vcs[:], 1.0)
        self.v("gpsimd", "memset", [], [BvcTs], vcTs[:], 0.0)
        self.v("gpsimd", "memset", [], [BkcTs], kcTs[:], 0.0)
        self.v("gpsimd", "memset", [], [Bvnew], vnew[:], 1.0)
        def init_sample_set(i):
            self.v("gpsimd", "memset", [], [BKA[i]], KA[i][:], 0.0)
            self.v("gpsimd", "memset", [], [BVB[i]], VB[i][:], 1.0)
            self.v("gpsimd", "memset", [], [BVWs[i]], VWs[i][:], 1.0)

        init_sample_set(0)
        fdummy = P.sb("fdummy", [128, 1], F32)
        self.stg_i = 0
        self.seq_i = 0

        def nsa_sample(l):
            if l > 0:
                self.v("vector", "tensor_scalar", [Bidxf], [Bidxf], out=idxf[:], in0=idxf[:], scalar1=float(cfg.NPOOL * 128), scalar2=None, op0=ALU.add)
            self.v("vector", "tensor_copy", [Bidxf], [Bidxi], out=idxi[:, :], in_=idxf[:])

            def load_seq(b, par):
                ka, Bka, vb, Bvb = KA[par], BKA[par], VB[par], BVB[par]
                kw, Bkw, vw, Bvw = KWs[par], BKWs[par], VWs[par], BVWs[par]
                for pg in range(NPG):
                    si = self.stg_i % NSTG
                    self.stg_i += 1
                    col = b * NPG + pg
                    P.dma_fn("gpsimd", (lambda e, si=si, col=col: e.indirect_dma_start(
                        out=stgA[si][:, :], out_offset=None, in_=d_poolA[:, :],
                        in_offset=bass.IndirectOffsetOnAxis(ap=idxi[:, col:col + 1], axis=0))), [Bidxi], [BstgA[si]])
                    P.dma_fn("gpsimd", (lambda e, si=si, col=col: e.indirect_dma_start(
                        out=stgB[si][:, :], out_offset=None, in_=d_poolB[:, :],
                        in_offset=bass.IndirectOffsetOnAxis(ap=idxi[:, col:col + 1], axis=0))), [Bidxi], [BstgB[si]])
                    sv = stgA[si][:, :].rearrange("p (s t) -> p s t", t=128)
                    self.v("gpsimd", "tensor_copy", [BstgA[si]], [Bka], out=ka[0:64, 0:2, pg * 128:(pg + 1) * 128], in_=sv[0:64, 0:2, :])
                    self.v("gpsimd", "tensor_copy", [BstgA[si]], [Bka], out=ka[64:128, 2:4, pg * 128:(pg + 1) * 128], in_=sv[64:128, 0:2, :])
                    self.v("gpsimd", "tensor_copy", [BstgA[si]], [Bka], out=ka[:, 4, pg * 128:(pg + 1) * 128], in_=sv[:, 2, :])
                    self.v("gpsimd", "tensor_copy", [BstgB[si]], [Bvb], out=vb[:, pg, :, 0:64], in_=stgB[si][:, :].rearrange("p (g d) -> p g d", d=64))
                for s in range(2):
                    self.v("gpsimd", "tensor_copy", [Bkvf[s]], [Bka], out=ka[0:64, s, cfg.PAST:cfg.PAST + 4], in_=kvf[0:64, s, b * 4:(b + 1) * 4])
                    self.v("gpsimd", "tensor_copy", [Bkvf[s]], [Bka], out=ka[64:128, 2 + s, cfg.PAST:cfg.PAST + 4], in_=kvf[64:128, s, b * 4:(b + 1) * 4])
                self.v("gpsimd", "tensor_copy", [Bkvf[2]], [Bka], out=ka[:, 4, cfg.PAST:cfg.PAST + 4], in_=kvf[:, 2, b * 4:(b + 1) * 4])
                P.dma("sync", stgW[:], d_kwinT[l, b], writes=[BstgW])
                P.dma("sync", stgV[:], d_cwin[l, b].rearrange("(k p) c -> p k c", p=128)[:, :, 128:256], writes=[BstgV])
                self.v("gpsimd", "tensor_copy", [BstgW], [Bkw], out=kw[:, 0:512], in_=stgW[:])
                self.v("gpsimd", "tensor_copy", [Bkvf[4]], [Bkw], out=kw[:, 512:516], in_=kvf[:, 4, b * 4:(b + 1) * 4])
                self.v("gpsimd", "tensor_copy", [BstgV], [Bvw], out=vw[:, 0:4, :, 0:64], in_=stgV[:].rearrange("p k (g d) -> p k g d", d=64))
                P.dma("sync", o_wins_old[l, b], d_cwin[l, b, 4:512, :])

            def compute_seq(b, par):
                ka, Bka, vb, Bvb = KA[par], BKA[par], VB[par], BVB[par]
                kw, Bkw, vw, Bvw = KWs[par], BKWs[par], VWs[par], BVWs[par]
                for (s, wi) in ((3, 0), (5, 1)):
                    self.tr(PST[0:4, 0:128], kvf[:, s, b * 4:(b + 1) * 4], identf[:], [Bkvf[s], Bidentf], [BPST])
                    self.act(vnew[:, wi, :, 0:64], PST[0:4, 0:128].rearrange("p (g d) -> p g d", d=64), AF.Copy, [BPST], [Bvnew])
                nblk = cfg.NCMP_S

                def src_fn(s, g, li, ka=ka):
                    a = ka[:, 2 * g + s, li:li + 1]
                    return bass.AP(a.tensor, a.offset, [list(a.ap[0]), [16, nblk]])

                compress(l, src_fn, 0, nblk, Bka,
                         lambda g: kcTs[g * 64:(g + 1) * 64, 0:nblk],
                         lambda g: vcTs[g * 64:(g + 1) * 64, 0:nblk],
                         BkcTs, BvcTs, cosc_s, sinc_s, Bcc3, ceng="vector")
                self.tr(PSTb[:, 0:128], vcTs[:, :], identb[:], [BvcTs, Bidentb], [BPST])
                self.act(vcs[:, :, 0:64], PSTb[:, 0:128].rearrange("p (g d) -> p g d", d=64), AF.Copy, [BPST], [Bvcs])
                for g in range(2):
                    r = slice(g * 64, (g + 1) * 64)
                    qa = QTz[g][:, :, b * 4:(b + 1) * 4]
                    kts = [dict(k=kcTs[:, 0:nblk], Bk=BkcTs, M=nblk, v=vcs[0:nblk, g, :], Bv=Bvcs, masks=[])]
                    pso, Bpso, pts = attend(qa, 16, kts, BQT)
                    importance(pts, lambda ci, M: cover_s[0:M, :], Bcover_s, 4, 4)
                    combine(pso, Bpso, 4, 4, 0, g, b, True)
                    select_blocks(scA_s[:, :], scB_s[:, :], BscA_s, 4, cfg.NSEL_S)
                    kts = []
                    for kt in range(NPG):
                        kts.append(dict(k=ka[:, 4, kt * 128:(kt + 1) * 128], Bk=Bka, M=128, v=vb[:, kt, g, :], Bv=Bvb,
                                        masks=[(eexp[:, kt * 128:(kt + 1) * 128], bc_mid(selbT[:, 0:4], 4), [Beexp, BselbT])]))
                    kts.append(dict(k=ka[:, 4, cfg.PAST:cfg.PAST + 4], Bk=Bka, M=4, v=vnew[:, 0, g, :], Bv=Bvnew,
                                    masks=[(eexp[:, cfg.PAST:cfg.PAST + 4], bc_mid(selbT[:, 0:4], 4), [Beexp, BselbT]),
                                           (identb[0:4, 0:4], bc_mid(newtri[:, :], 4), [Bidentb, Bnewtri])]))
                    pso, Bpso, pts = attend(qa, 16, kts, BQT)
                    combine(pso, Bpso, 4, 4, 1, g, b, False)
                    kts = []
                    for kt in range(4):
                        masks = [(identb[:], bc_mid(winlo_s[:, :], 4), [Bidentb, Bwinlo_s])] if kt == 0 else []
                        kts.append(dict(k=kw[:, kt * 128:(kt + 1) * 128], Bk=Bkw, M=128, v=vw[:, kt, g, :], Bv=Bvw, masks=masks))
                    kts.append(dict(k=kw[:, 512:516], Bk=Bkw, M=4, v=vnew[:, 1, g, :], Bv=Bvnew,
                                    masks=[(identb[0:4, 0:4], bc_mid(newtri[:, :], 4), [Bidentb, Bnewtri])]))
                    pso, Bpso, pts = attend(qa, 16, kts, BQT)
                    combine(pso, Bpso, 4, 4, 2, g, b, False)
                self.v("vector", "tensor_copy", [Botok], [Botokb], out=otokb[0:4, :], in_=otok[0:4, :])
                for jj in range(4):
                    self.tr(PSMb[:, jj * 4:(jj + 1) * 4], otokb[0:4, jj * 128:(jj + 1) * 128], identb[0:4, 0:4], [Botokb, Bidentb], [BPSM])
                self.act(ynsa[:, :, b * 4:(b + 1) * 4], PSMb[:, 0:16].rearrange("p (j q) -> p j q", q=4), AF.Copy, [BPSM], [Bynsa])

            pars = []
            for b in range(NB):
                pars.append(self.seq_i % NKA)
                self.seq_i += 1
            load_seq(0, pars[0])
            for b in range(NB):
                if b + 1 < NB:
                    load_seq(b + 1, pars[b + 1])
                compute_seq(b, pars[b])

        P.dma("sync", pti[:], d_pt.partition_broadcast(128), writes=[Bpti])
        self.v("vector", "tensor_copy", [Bpti], [Bidxf], out=idxf[:], in_=pti[:])
        self.v("vector", "tensor_scalar", [Bidxf], [Bidxf], out=idxf[:], in0=idxf[:], scalar1=128.0, scalar2=None, op0=ALU.mult)
        self.v("vector", "tensor_scalar", [Bidxf, Biota], [Bidxf], out=idxf[:], in0=idxf[:], scalar1=iota_p[:, 0:1], scalar2=None, op0=ALU.add)

        xp_v = d_xp.rearrange("(k p) t -> p k t", p=128)
        yp_v = o_yp.rearrange("(k p) t -> p k t", p=128)
        for ti in range(cfg.ntile_run if cfg.do_prompt else 0):
            t0 = ti * NT
            P.dma("sync", x[:], xp_v[:, :, t0:t0 + NT], writes=[Bx])
            for l in range(L):
                layer_tile(l, NT, 1, NT, True, ti)
            final_out(NT, lambda kc, t0=t0: yp_v[:, kc, t0:t0 + NT])
        self.v("gpsimd", "memset", [], [BKwk, BVwk, Bcext] + BKw + BVw + Bkc + BvcT + Bvc + [BKA[1], BVB[1], BKWs[1], BVWs[1], BstgW, BstgV],
               fdummy[:], 0.0)
        init_sample_set(1)
        nts = NB * 4
        P.dma("sync", x[:, :, 0:nts], d_xs.rearrange("(k p) t -> p k t", p=128), writes=[Bx])
        for l in range(L if cfg.do_sample else 0):
            layer_tile(l, nts, NB, 4, False, 0)
        ys_v = o_ys.rearrange("(k p) t -> p k t", p=128)
        final_out(nts, lambda kc: ys_v[:, kc, :])
        return P.finalize()


_CACHE = {}


def get_builder(cfg, key):
    if key not in _CACHE:
        b = Builder(cfg)
        b.stats = b.build()
        _CACHE[key] = b
    return _CACHE[key]


def run(cfg, key, inputs, n_cores=8):
    bld = get_builder(cfg, key)
    L, NB = cfg.L, cfg.NB
    f32 = np.float32
    A = lambda a: np.ascontiguousarray(np.asarray(a))
    consts = make_consts(cfg)
    shared = {}
    shared["w_in"] = A(inputs["w_in"])
    shared["pool_w"] = A(inputs["pool_w"]).reshape(L, 256, 64)
    for s in range(2):
        shared[f"cmp_w1_{s}"] = A(np.asarray(inputs["cmp_w1"])[:, s])
        shared[f"cmp_w2_{s}"] = A(np.asarray(inputs["cmp_w2"])[:, s])
    for k in ("w_br_pool", "w_br_nsa", "w_br_conv", "w_out", "ffn_up", "ffn_down"):
        shared[k] = A(inputs[k])
    shared["gmix"] = A(np.asarray(inputs["norm_mix"]).reshape(L, 8, 128).transpose(2, 0, 1))
    shared["gffn"] = A(np.asarray(inputs["norm_ffn"]).reshape(L, 8, 128).transpose(2, 0, 1))
    shared["gfin"] = A(np.asarray(inputs["norm_final"]).reshape(8, 128).transpose(1, 0))
    shared["pscale"] = A(np.asarray(inputs["pool_scale"]).reshape(L, 4, 64).transpose(2, 0, 1))
    shared["convw"] = A(np.asarray(inputs["conv_w"]).reshape(L, 3, 2, 128).transpose(3, 0, 2, 1))
    shared["ffnw"] = A(np.asarray(inputs["ffn_conv"]).reshape(L, 3, 22, 128).transpose(3, 0, 2, 1))
    shared["peT"] = A(np.asarray(inputs["cmp_pe"]).transpose(3, 0, 1, 2))
    for k, a in consts.items():
        shared["c_" + k] = A(a.astype(f32))
    ckv = np.asarray(inputs["cache_kv"])
    npool = ckv.shape[1]
    shared["poolA"] = A(ckv[:, :, :, 0:3].transpose(0, 1, 4, 5, 3, 2)).reshape(L * npool * 128, 384)
    shared["poolB"] = A(ckv[:, :, :, 3]).reshape(L * npool * 128, 128)
    xp = np.asarray(inputs["x_prompt"])
    xs = np.asarray(inputs["x_sample"])
    cw = np.asarray(inputs["cache_win"])
    sp = np.asarray(inputs["state_pool"])
    sc = np.asarray(inputs["state_conv"])
    sf = np.asarray(inputs["state_ffn"])
    pt = np.asarray(inputs["page_table"]).astype(np.int32)
    in_maps = []
    for c in range(n_cores):
        m = dict(shared)
        m["xpT"] = A(xp[c % xp.shape[0]].T)
        sl = slice(c * NB, (c + 1) * NB)
        m["xsT"] = A(xs[sl].reshape(NB * 4, 1024).T)
        m["ptab"] = A(pt[sl].reshape(1, -1))
        m["cwin"] = A(cw[:, sl].reshape(L, NB, 512, 256))
        m["kwinT"] = A(cw[:, sl, :, 0].reshape(L, NB, 512, 128).transpose(0, 1, 3, 2))
        m["spoolT"] = A(sp[:, sl].reshape(L, NB, 15, 4, 64).transpose(0, 4, 3, 1, 2)).reshape(L, 64, -1)
        m["sconvT"] = A(sc[:, sl].reshape(L, NB, 2, 2, 128).transpose(0, 4, 3, 1, 2)).reshape(L, 128, -1)
        m["sffnT"] = A(sf[:, sl].reshape(L, NB, 2, 22, 128).transpose(0, 4, 3, 1, 2)).reshape(L, 128, -1)
        in_maps.append(m)
    res = run_bass_kernel_spmd(bld.nc, in_maps, core_ids=list(range(n_cores)))
    R = res.results
    nbp = xp.shape[0]
    SEQ = cfg.SEQ
    y_prompt = np.stack([R[c]["o_yp"].T for c in range(nbp)])
    y_sample = np.concatenate([R[c]["o_ys"].T.reshape(NB, 4, 1024) for c in range(n_cores)])
    kv_prompt = np.stack([R[c]["o_kvp"].reshape(L, 4, 2, 64, SEQ).transpose(0, 4, 1, 2, 3) for c in range(nbp)], axis=1)
    kv_sample = np.concatenate([R[c]["o_kvs"].reshape(L, 4, 2, 64, NB, 4).transpose(0, 4, 5, 1, 2, 3) for c in range(n_cores)], axis=1)
    win_prompt = np.stack([R[c]["o_winp"].reshape(L, 2, 2, 64, 512).transpose(0, 4, 1, 2, 3) for c in range(nbp)], axis=1)
    wn = [R[c]["o_wins_new"].reshape(L, 2, 2, 64, NB, 4).transpose(0, 4, 5, 1, 2, 3) for c in range(n_cores)]
    wo = [R[c]["o_wins_old"].reshape(L, NB, 508, 2, 2, 64) for c in range(n_cores)]
    win_sample = np.concatenate([np.concatenate([wo[c], wn[c]], axis=2) for c in range(n_cores)], axis=1)
    pool_prompt = np.stack([R[c]["o_poolp"].reshape(L, 64, 4, 15).transpose(0, 3, 2, 1).reshape(L, 15, 256) for c in range(nbp)], axis=1)
    pool_sample = np.concatenate([R[c]["o_pools"].reshape(L, 64, 4, NB, 15).transpose(0, 3, 4, 2, 1).reshape(L, NB, 15, 256) for c in range(n_cores)], axis=1)
    conv_prompt = np.stack([R[c]["o_convp"].reshape(L, 128, 2, 2).transpose(0, 3, 2, 1).reshape(L, 2, 256) for c in range(nbp)], axis=1)
    conv_sample = np.concatenate([R[c]["o_convs"].reshape(L, 128, 2, NB, 2).transpose(0, 3, 4, 2, 1).reshape(L, NB, 2, 256) for c in range(n_cores)], axis=1)
    ffn_prompt = np.stack([R[c]["o_ffnp"].reshape(L, 128, 22, 2).transpose(0, 3, 2, 1).reshape(L, 2, 2816) for c in range(nbp)], axis=1)
    ffn_sample = np.concatenate([R[c]["o_ffns"].reshape(L, 128, 22, NB, 2).transpose(0, 3, 4, 2, 1).reshape(L, NB, 2, 2816) for c in range(n_cores)], axis=1)
    outs = (y_prompt, y_sample, kv_prompt, kv_sample, win_prompt, win_sample, pool_prompt, pool_sample,
            conv_prompt, conv_sample, ffn_prompt, ffn_sample)
    return tuple(np.ascontiguousarray(o.astype(np.float32)) for o in outs)


def kernel(**inputs):
    return run(FULL, "full", inputs)
```
